# Optimizing a Trainium2 kernel written in Bass

```python
import math, functools
import jax, jax.numpy as jnp
from jax import lax
import numpy as np

D_MODEL = 1024
BATCH = 32
SEQ = 256
DEPTH = 4
DEC_BATCH = 2
DEC_SEQ = 1024
PAST_LEN = 256

GRID_W = 64
N_ATTN_LAYERS = (DEPTH + 1) // 2
N_DELTA_LAYERS = DEPTH // 2
HEAD_DIM = 128
N_HEADS = D_MODEL // HEAD_DIM
KV_HEADS = 2
GROUP = N_HEADS // KV_HEADS
Q_WIDTH = N_HEADS * HEAD_DIM
KV_WIDTH = KV_HEADS * HEAD_DIM
Q_BLOCK = 128
ROPE_THETA = 10000.0
ROPE_FREQS = HEAD_DIM // 4
DN_DK = 128
DN_DV = 128
DN_HEADS = D_MODEL // DN_DV
DN_KEY = DN_HEADS * DN_DK
DN_VAL = DN_HEADS * DN_DV
DN_QKV = 2 * DN_KEY + DN_VAL
DN_IN = DN_QKV + DN_VAL + 4 * DN_HEADS
CONV_K = 5
CHUNK = 64
D_FF = 4 * D_MODEL
N_MOD = 6
EPS = 1e-6

kernel_name = 'hybrid_diffusion_gqa_deltanet_step'


def rms_norm(x, gain):
    xf = x.astype(jnp.float32)
    y = xf * lax.rsqrt(jnp.mean(xf * xf, axis=-1, keepdims=True) + EPS)
    return (y * gain.astype(jnp.float32)).astype(x.dtype)


def l2_norm(x):
    xf = x.astype(jnp.float32)
    return xf * lax.rsqrt(jnp.sum(xf * xf, axis=-1, keepdims=True) + EPS)


def modulation(cond, w_mod, b_mod):
    m = jax.nn.silu(cond) @ w_mod + b_mod
    return jnp.split(m[..., None, :], N_MOD, axis=-1)


def trunk_layer(x, mods, gain1, gain2, w1, w2, mixer):
    shift1, scale1, gate1, shift2, scale2, gate2 = mods
    mix, aux = mixer(rms_norm(x, gain1) * (1 + scale1) + shift1)
    x = x + gate1 * mix
    h = rms_norm(x, gain2) * (1 + scale2) + shift2
    x = x + gate2 * (jnp.square(jax.nn.relu(h @ w1)) @ w2)
    return x, aux


def axial_angles(n_tokens):
    rows = n_tokens // GRID_W
    row = jnp.repeat(jnp.arange(rows), GRID_W).astype(jnp.float32)
    col = jnp.tile(jnp.arange(GRID_W), rows).astype(jnp.float32)
    inv = ROPE_THETA ** (-jnp.arange(ROPE_FREQS, dtype=jnp.float32) / ROPE_FREQS)
    return jnp.stack([row[:, None] * inv, col[:, None] * inv], axis=1)


def axial_rope(x, ang):
    xs = x.reshape(*x.shape[:-1], 2, 2, ROPE_FREQS)
    x1, x2 = xs[..., 0, :], xs[..., 1, :]
    cos = jnp.cos(ang)[:, None].astype(x.dtype)
    sin = jnp.sin(ang)[:, None].astype(x.dtype)
    out = jnp.stack([x1 * cos - x2 * sin, x2 * cos + x1 * sin], axis=-2)
    return out.reshape(x.shape)


def attn_qkv(h, w_in, q_gain, k_gain):
    B, T, _ = h.shape
    proj = h @ w_in
    q = proj[..., :Q_WIDTH].reshape(B, T, N_HEADS, HEAD_DIM)
    k = proj[..., Q_WIDTH:Q_WIDTH + KV_WIDTH].reshape(B, T, KV_HEADS, HEAD_DIM)
    v = proj[..., Q_WIDTH + KV_WIDTH:].reshape(B, T, KV_HEADS, HEAD_DIM)
    return rms_norm(q, q_gain), rms_norm(k, k_gain), v


def block_attention(q, k, v):
    B, Tq, _, _ = q.shape
    nb = Tq // Q_BLOCK
    qb = q.reshape(B, nb, Q_BLOCK, KV_HEADS, GROUP, HEAD_DIM).transpose(1, 0, 2, 3, 4, 5)
    scale = HEAD_DIM ** -0.5

    def one_block(qi):
        s = jnp.einsum('bqkgd,bskd->bkgqs', qi, k).astype(jnp.float32) * scale
        p = jax.nn.softmax(s, axis=-1).astype(v.dtype)
        return jnp.einsum('bkgqs,bskd->bqkgd', p, v)

    o = lax.map(one_block, qb)
    return o.transpose(1, 0, 2, 3, 4, 5).reshape(B, Tq, Q_WIDTH)


def context_attention(h, w_in, q_gain, k_gain, w_out):
    q, k, v = attn_qkv(h, w_in, q_gain, k_gain)
    return block_attention(q, k, v) @ w_out, (k, v)


def latent_attention(h, ck, cv, ang, w_in, q_gain, k_gain, w_out):
    q, k, v = attn_qkv(h, w_in, q_gain, k_gain)
    q, k = axial_rope(q, ang), axial_rope(k, ang)
    keys = jnp.concatenate([ck.astype(k.dtype), k], axis=1)
    vals = jnp.concatenate([cv.astype(v.dtype), v], axis=1)
    return block_attention(q, keys, vals) @ w_out, None


def centred_depthwise_conv(x, w):
    return lax.conv_general_dilated(
        x, w[:, None, :].astype(x.dtype), window_strides=(1,),
        padding=[(CONV_K // 2, CONV_K // 2)], dimension_numbers=('NWC', 'WIO', 'NWC'),
        feature_group_count=x.shape[-1])


def chunk_gated_delta(q, k, v, g, beta, s0):
    B, T, H, DK = q.shape
    DV = v.shape[-1]
    N = T // CHUNK
    f32 = jnp.float32
    to_chunks = lambda a: a.astype(f32).reshape(B, N, CHUNK, H, a.shape[-1]).transpose(1, 0, 3, 2, 4)
    q, k, v = to_chunks(q) * (DK ** -0.5), to_chunks(k), to_chunks(v)
    g = g.astype(f32).reshape(B, N, CHUNK, H).transpose(1, 0, 3, 2)
    beta = beta.astype(f32).reshape(B, N, CHUNK, H).transpose(1, 0, 3, 2)
    gc = jnp.cumsum(g, axis=-1)
    tril = jnp.tril(jnp.ones((CHUNK, CHUNK), bool))
    strict = jnp.tril(jnp.ones((CHUNK, CHUNK), bool), -1)
    diff = gc[..., :, None] - gc[..., None, :]
    decay = jnp.where(tril, jnp.exp(jnp.where(tril, diff, 0.0)), 0.0)
    kb = k * beta[..., None]
    vb = v * beta[..., None]
    lower = jnp.where(strict, jnp.einsum('nbhcd,nbhed->nbhce', kb, k) * decay, 0.0)
    eye = jnp.eye(CHUNK, dtype=f32)
    tinv = lax.linalg.triangular_solve(eye + lower, jnp.broadcast_to(eye, lower.shape),
                                       left_side=True, lower=True, unit_diagonal=True)
    u = tinv @ vb
    w = tinv @ (kb * jnp.exp(gc)[..., None])
    qk = jnp.where(tril, jnp.einsum('nbhcd,nbhed->nbhce', q, k) * decay, 0.0)

    def step(S, xs):
        q_i, k_i, u_i, w_i, qk_i, gc_i = xs
        v_new = u_i - w_i @ S
        o = (q_i * jnp.exp(gc_i)[..., None]) @ S + qk_i @ v_new
        g_last = gc_i[..., -1]
        S = S * jnp.exp(g_last)[..., None, None] + jnp.einsum(
            'bhcd,bhce->bhde', k_i * jnp.exp(g_last[..., None] - gc_i)[..., None], v_new)
        return S, o

    S, o = lax.scan(step, s0.astype(f32), (q, k, u, w, qk, gc))
    return o.transpose(1, 0, 3, 2, 4).reshape(B, T, H, DV), S


def delta_mixer(h, s0, w_in, conv_w, a_log, dt_bias, norm_gain, w_out):
    B, T, _ = h.shape
    proj = h @ w_in
    qkv, z, a, b = jnp.split(proj, [DN_QKV, DN_QKV + DN_VAL, DN_QKV + DN_VAL + 2 * DN_HEADS], axis=-1)
    qkv = jax.nn.silu(centred_depthwise_conv(qkv, conv_w))
    q, k, v = jnp.split(qkv, [DN_KEY, 2 * DN_KEY], axis=-1)
    q = l2_norm(q.reshape(B, T, DN_HEADS, DN_DK))
    k = l2_norm(k.reshape(B, T, DN_HEADS, DN_DK))
    v = v.reshape(B, T, DN_HEADS, DN_DV)
    a = a.reshape(B, T, 2, DN_HEADS).astype(jnp.float32)
    b = b.reshape(B, T, 2, DN_HEADS).astype(jnp.float32)
    g = -jnp.exp(a_log.astype(jnp.float32)) * jax.nn.softplus(a + dt_bias.astype(jnp.float32))
    beta = jax.nn.sigmoid(b)
    o_f, s_f = chunk_gated_delta(q, k, v, g[:, :, 0], beta[:, :, 0], s0[:, 0])
    rev = lambda t: jnp.flip(t, axis=1)
    o_b, s_b = chunk_gated_delta(rev(q), rev(k), rev(v), rev(g[:, :, 1]), rev(beta[:, :, 1]), s0[:, 1])
    o = o_f + rev(o_b)
    o = rms_norm(o, norm_gain) * jax.nn.silu(z.reshape(B, T, DN_HEADS, DN_DV).astype(jnp.float32))
    out = o.reshape(B, T, DN_VAL).astype(h.dtype) @ w_out
    return out, jnp.stack([s_f, s_b], axis=1)


def setup_inputs(seed: int = 0) -> dict:
    key = jax.random.key(seed)
    ks = jax.random.split(key, 24)
    f32 = jnp.float32
    D = D_MODEL
    nrm = lambda k, shape, s: jax.random.normal(k, shape, f32) * s
    dn_a_log = jnp.log(jax.random.uniform(ks[17], (N_DELTA_LAYERS, 2, DN_HEADS), f32, 1.0, 16.0))
    dt = jnp.exp(jax.random.uniform(ks[18], (N_DELTA_LAYERS, 2, DN_HEADS), f32,
                                    math.log(1e-3), math.log(1e-1)))
    dn_dt_bias = dt + jnp.log(-jnp.expm1(-dt))
    return {
        'x_prompt': nrm(ks[0], (BATCH, SEQ, D), 1.0),
        'x_sample': nrm(ks[1], (DEC_BATCH, DEC_SEQ, D), 1.0),
        'cache_k': nrm(ks[2], (DEC_BATCH, N_ATTN_LAYERS, PAST_LEN, KV_HEADS, HEAD_DIM), 1.0),
        'cache_v': nrm(ks[3], (DEC_BATCH, N_ATTN_LAYERS, PAST_LEN, KV_HEADS, HEAD_DIM), 1.0),
        'state_delta': nrm(ks[4], (DEC_BATCH, N_DELTA_LAYERS, 2, DN_HEADS, DN_DK, DN_DV), 0.1),
        'c': nrm(ks[5], (DEC_BATCH, D), 1.0),
        'c_ctx': nrm(ks[6], (D,), 1.0),
        'w_mod': nrm(ks[7], (DEPTH, D, N_MOD * D), 0.5 * D ** -0.5),
        'b_mod': nrm(ks[8], (DEPTH, N_MOD * D), 0.02),
        'norm1': 1.0 + nrm(ks[9], (DEPTH, D), 0.02),
        'norm2': 1.0 + nrm(ks[10], (DEPTH, D), 0.02),
        'attn_w_in': nrm(ks[11], (N_ATTN_LAYERS, D, Q_WIDTH + 2 * KV_WIDTH), D ** -0.5),
        'attn_q_gain': 1.0 + nrm(ks[12], (N_ATTN_LAYERS, HEAD_DIM), 0.02),
        'attn_k_gain': 1.0 + nrm(ks[13], (N_ATTN_LAYERS, HEAD_DIM), 0.02),
        'attn_w_out': nrm(ks[14], (N_ATTN_LAYERS, Q_WIDTH, D), Q_WIDTH ** -0.5),
        'dn_w_in': nrm(ks[15], (N_DELTA_LAYERS, D, DN_IN), D ** -0.5),
        'dn_conv': nrm(ks[16], (N_DELTA_LAYERS, CONV_K, DN_QKV), CONV_K ** -0.5),
        'dn_a_log': dn_a_log,
        'dn_dt_bias': dn_dt_bias,
        'dn_norm': 1.0 + nrm(ks[19], (N_DELTA_LAYERS, DN_DV), 0.02),
        'dn_w_out': nrm(ks[20], (N_DELTA_LAYERS, DN_VAL, D), DN_VAL ** -0.5),
        'mlp_w1': nrm(ks[21], (DEPTH, D, D_FF), D ** -0.5),
        'mlp_w2': nrm(ks[22], (DEPTH, D_FF, D), D_FF ** -0.5),
        'final_norm': 1.0 + nrm(ks[23], (D,), 0.02),
    }


def reference(x_prompt, x_sample, cache_k, cache_v, state_delta, c, c_ctx,
              w_mod, b_mod, norm1, norm2,
              attn_w_in, attn_q_gain, attn_k_gain, attn_w_out,
              dn_w_in, dn_conv, dn_a_log, dn_dt_bias, dn_norm, dn_w_out,
              mlp_w1, mlp_w2, final_norm):
    x = x_prompt
    bsz = x.shape[0]
    keys_out, vals_out, states_out = [], [], []
    for l in range(DEPTH):
        j = l // 2
        mods = modulation(c_ctx, w_mod[l], b_mod[l])
        if l % 2 == 0:
            mixer = functools.partial(context_attention, w_in=attn_w_in[j], q_gain=attn_q_gain[j],
                                      k_gain=attn_k_gain[j], w_out=attn_w_out[j])
        else:
            s0 = jnp.zeros((bsz, 2, DN_HEADS, DN_DK, DN_DV), jnp.float32)
            mixer = functools.partial(delta_mixer, s0=s0, w_in=dn_w_in[j], conv_w=dn_conv[j],
                                      a_log=dn_a_log[j], dt_bias=dn_dt_bias[j],
                                      norm_gain=dn_norm[j], w_out=dn_w_out[j])
        x, aux = trunk_layer(x, mods, norm1[l], norm2[l], mlp_w1[l], mlp_w2[l], mixer)
        if l % 2 == 0:
            keys_out.append(aux[0])
            vals_out.append(aux[1])
        else:
            states_out.append(aux)
    y_prompt = rms_norm(x, final_norm)
    new_cache_k = jnp.stack(keys_out, axis=1)
    new_cache_v = jnp.stack(vals_out, axis=1)
    new_state_delta = jnp.stack(states_out, axis=1).astype(x_prompt.dtype)

    ang = axial_angles(x_sample.shape[1])
    x = x_sample
    for l in range(DEPTH):
        j = l // 2
        mods = modulation(c, w_mod[l], b_mod[l])
        if l % 2 == 0:
            mixer = functools.partial(latent_attention, ck=cache_k[:, j], cv=cache_v[:, j], ang=ang,
                                      w_in=attn_w_in[j], q_gain=attn_q_gain[j],
                                      k_gain=attn_k_gain[j], w_out=attn_w_out[j])
        else:
            mixer = functools.partial(delta_mixer, s0=state_delta[:, j], w_in=dn_w_in[j],
                                      conv_w=dn_conv[j], a_log=dn_a_log[j], dt_bias=dn_dt_bias[j],
                                      norm_gain=dn_norm[j], w_out=dn_w_out[j])
        x, _ = trunk_layer(x, mods, norm1[l], norm2[l], mlp_w1[l], mlp_w2[l], mixer)
    y_sample = rms_norm(x, final_norm)
    return (y_prompt, y_sample, new_cache_k, new_cache_v, new_state_delta)
```

```python
import numpy as np
from contextlib import ExitStack
import concourse.bass as bass
import concourse.mybir as mybir
from concourse.bass_utils import run_bass_kernel_spmd

F32 = mybir.dt.float32
BF16 = mybir.dt.bfloat16
AF = mybir.ActivationFunctionType
ALU = mybir.AluOpType

D = 1024
NT = 1280
NG = 1024
NE = 256
DEPTH = 4
DFF = 4096
HD = 128
EPS = 1e-6
TBS = [(0, 512), (512, 512), (1024, 256)]
NSLOT = 4
NCONST = 1792
WCH = 16
DBG_STAGE = 99
DBG_SUB = 99
DBG_DUMP = False
DBG_NOPRO = False
SB_BASE = 16384 + 256
SB_END = 229376
KEYG = 256


class Reg:
    __slots__ = ("ap", "keys")

    def __init__(self, ap, keys):
        self.ap = ap
        self.keys = keys


class T:
    def __init__(self, P, name, F, dtype, space="sb", addr=None, handle=None):
        self.P, self.name, self.F, self.dtype, self.space = P, name, F, dtype, space
        self.esz = 4 if dtype == F32 else 2
        self.addr = addr
        self.h = handle

    def v(self, col, dims, p0=0, n=128):
        ap = bass.AP(self.h, p0 * self.F + col, [[self.F, n]] + [list(d) for d in dims])
        span = 1 + sum((c - 1) * s for s, c in dims)
        if self.space == "sb":
            lo = self.addr + col * self.esz
            hi = self.addr + (col + span) * self.esz
            keys = range(lo // KEYG, (hi - 1) // KEYG + 1)
        elif self.space == "ps":
            keys = (self.name,)
        else:
            keys = ()
        return Reg(ap, keys)

    def __call__(self, col=0, n=None, p0=0, np_=128):
        if n is None:
            n = self.F - col
        return self.v(col, [[1, n]], p0, np_)


class Prog:
    ENG = ("pe", "act", "dve", "pool", "sp")

    def __init__(self, nc):
        self.nc = nc
        self.ops = []
        self.sb_ptr = SB_BASE
        self.dram_in = {}
        self.dram_out = {}

    def sb(self, name, F, dtype, addr=None):
        esz = 4 if dtype == F32 else 2
        if addr is None:
            addr = self.sb_ptr
            self.sb_ptr += (F * esz + 63) // 64 * 64
            assert self.sb_ptr <= SB_END, ("SBUF overflow", name, self.sb_ptr)
        h = self.nc.alloc_sbuf_tensor_at(name, [128, F], dtype, offset=addr)
        return T(self, name, F, dtype, "sb", addr, h)

    def din(self, name, shape, dtype=F32):
        h = self.nc.dram_tensor(name, list(shape), dtype, kind="ExternalInput")
        self.dram_in[name] = (tuple(shape), dtype)
        return h

    def dout(self, name, shape, dtype=F32):
        h = self.nc.dram_tensor(name, list(shape), dtype, kind="ExternalOutput")
        self.dram_out[name] = (tuple(shape), dtype)
        return h

    def add(self, eng, fn, reads, writes, dma_key=None):
        rk = []
        for r in reads:
            if r is not None and isinstance(r, Reg):
                rk.extend(r.keys)
        wk = []
        for w in writes:
            wk.extend(w.keys)
        if eng != "pe":
            for r in reads:
                if r is not None and isinstance(r, Reg) and len(r.keys) and isinstance(r.keys[0], str):
                    wk.extend(r.keys)
        self.ops.append((eng, fn, rk, wk, dma_key))

    def finalize(self):
        nc = self.nc
        with ExitStack() as es:
            sems = {e: es.enter_context(nc.semaphore("s_" + e)) for e in ("pe", "act", "dve", "pool")}
            dsem = {}
            cnt = {e: 0 for e in sems}
            dcnt = {}
            known = {e: {} for e in self.ENG}
            last_w = {}
            readers = {}
            per_eng = {e: [] for e in self.ENG}
            for (eng, fn, rk, wk, dkey) in self.ops:
                need = {}

                def req(tok, kind):
                    name, val, teng = tok
                    if teng == eng and dkey is None:
                        if eng == "pe":
                            return
                    if known[eng].get(name, 0) >= val:
                        return
                    if need.get(name, 0) < val:
                        need[name] = val

                for k in rk:
                    t = last_w.get(k)
                    if t is not None:
                        req(t, "raw")
                for k in wk:
                    t = last_w.get(k)
                    if t is not None:
                        req(t, "waw")
                    for t in readers.get(k, ()):
                        req(t, "war")
                for name, val in need.items():
                    known[eng][name] = val
                if dkey is not None:
                    if dkey not in dsem:
                        dsem[dkey] = es.enter_context(nc.semaphore("d_" + dkey))
                        dcnt[dkey] = 0
                    dcnt[dkey] += 16
                    tok = ("d_" + dkey, dcnt[dkey], "dma")
                    inc = (dsem[dkey], 16)
                else:
                    cnt[eng] += 1
                    tok = (eng, cnt[eng], eng)
                    inc = (sems[eng], 1)
                for k in wk:
                    last_w[k] = tok
                    readers[k] = []
                for k in rk:
                    lst = readers.setdefault(k, [])
                    if not lst or lst[-1] != tok:
                        lst.append(tok)
                        if len(lst) > 24:
                            best = {}
                            for (a, b, c) in lst:
                                if best.get(a, (0, None))[0] < b:
                                    best[a] = (b, c)
                            lst[:] = [(a, b, c) for a, (b, c) in best.items()]
                per_eng[eng].append((fn, list(need.items()), inc))
            allsem = dict(sems)
            for k, s in dsem.items():
                allsem["d_" + k] = s
            final_waits = [(("d_" + k), v) for k, v in dcnt.items()]
            self.n_ops = {e: len(v) for e, v in per_eng.items()}
            with nc.Block() as block:
                def runner(ename):
                    def body(e):
                        for fn, waits, inc in per_eng[ename]:
                            for sname, val in waits:
                                e.wait_ge(allsem[sname], val)
                            ins = fn(e)
                            ins.then_inc(inc[0], inc[1])
                        if ename == "sp":
                            for sname, val in final_waits:
                                e.wait_ge(allsem[sname], val)
                            for en2 in ("pe", "act", "dve", "pool"):
                                if cnt[en2]:
                                    e.wait_ge(sems[en2], cnt[en2])
                    return body
                block.tensor(runner("pe"))
                block.scalar(runner("act"))
                block.vector(runner("dve"))
                block.gpsimd(runner("pool"))
                block.sync(runner("sp"))

    def mm(self, out, lhsT, rhs, start=True, stop=True):
        self.add("pe", lambda e: e.matmul(out.ap, lhsT.ap, rhs.ap, start=start, stop=stop),
                 [lhsT, rhs] + ([] if start else [out]), [out])

    def tr(self, out, in_, ident):
        self.add("pe", lambda e: e.transpose(out.ap, in_.ap, ident.ap), [in_, ident], [out])

    def act(self, out, in_, func, bias=0.0, scale=1.0, accum=None):
        b = bias.ap if isinstance(bias, Reg) else bias
        s = scale.ap if isinstance(scale, Reg) else scale
        if accum is None:
            fn = lambda e: e.activation(out.ap, in_.ap, func, bias=b, scale=s)
            w = [out]
        else:
            fn = lambda e: e.activation(out.ap, in_.ap, func, bias=b, scale=s, accum_out=accum.ap)
            w = [out, accum]
        self.add("act", fn, [in_, bias, scale], w)

    def tt(self, eng, out, in0, in1, op):
        self.add(eng, lambda e: e.tensor_tensor(out.ap, in0.ap, in1.ap, op), [in0, in1], [out])

    def ts(self, eng, out, in0, s1, op0, s2=None, op1=None):
        a = s1.ap if isinstance(s1, Reg) else s1
        b = s2.ap if isinstance(s2, Reg) else s2
        if op1 is None:
            fn = lambda e: e.tensor_scalar(out.ap, in0.ap, a, None, op0)
        else:
            fn = lambda e: e.tensor_scalar(out.ap, in0.ap, a, b, op0, op1)
        self.add(eng, fn, [in0, s1, s2], [out])

    def stt(self, eng, out, in0, sc, in1, op0, op1):
        a = sc.ap if isinstance(sc, Reg) else sc
        self.add(eng, lambda e: e.scalar_tensor_tensor(out.ap, in0.ap, a, in1.ap, op0, op1),
                 [in0, sc, in1], [out])

    def cp(self, eng, out, in_):
        if eng == "act":
            self.add("act", lambda e: e.copy(out.ap, in_.ap), [in_], [out])
        else:
            self.add(eng, lambda e: e.tensor_copy(out.ap, in_.ap), [in_], [out])

    def memset(self, eng, out, val):
        self.add(eng, lambda e: e.memset(out.ap, val), [], [out])

    def dma(self, q, out, in_, key):
        self.add(q, lambda e: e.dma_start(out=out.ap, in_=in_.ap), [in_], [out], dma_key=key)


def dreg(h, off, dims):
    return Reg(bass.AP(h, off, [list(d) for d in dims]), ())


class WStream:
    def __init__(self, P, ntiles_hint=None):
        self.P = P
        self.slots = [P.sb("wslot%d" % i, 4096, BF16) for i in range(NSLOT)]
        self.descs = []
        self.issued = 0
        self.h = None

    def bind(self, h, ntiles):
        self.h = h
        self.ntiles = ntiles

    def _issue(self):
        t = self.issued
        if self.h is not None and t < self.ntiles:
            s = self.slots[t % NSLOT]
            self.P.dma("pool", s(0, 4096), dreg(self.h[t // WCH], (t % WCH) * 128 * 4096, [[4096, 128], [1, 4096]]),
                       key="w%d" % (t % NSLOT))
        self.issued += 1

    def start(self):
        for _ in range(NSLOT):
            self._issue()

    def acquire(self, desc):
        t = len(self.descs)
        self.descs.append(desc)
        return self.slots[t % NSLOT]

    def release(self):
        self._issue()


def build(nlayers=DEPTH, mixers=True, wdescs_only=False, ntiles=0, dbg=None):
    nc = bass.Bass("TRN2", target_bir_lowering=False)
    P = Prog(nc)
    x_in = P.din("x_in", [NT, D])
    cond_in = P.din("cond_in", [128, 16])
    bmod_in = P.din("bmod_in", [DEPTH, 128, 48])
    gains_in = P.din("gains_in", [128, 72])
    consts_in = P.din("consts_in", [128, NCONST])
    flags_in = P.din("flags_in", [128, 16])
    if nlayers >= 2 and mixers:
        dnab_in = P.din("dnab_in", [2, 128, 256])
        convw_in = P.din("convw_in", [2, 128, 120])
        dngp_in = P.din("dngp_in", [2, 128, 33])
        s0_in = P.din("s0_in", [2, 16, 128, 128])
        s_out = P.dout("s_out", [2, 5, 2, 8, 128, 128])
    dbg_out = P.dout("dbg_out", [8, 128, NT]) if DBG_DUMP else None
    wts = [P.din("wts%d" % i, [min(WCH, max(ntiles, 1) - i * WCH), 128, 4096])
           for i in range((max(ntiles, 1) + WCH - 1) // WCH)]
    y_out = P.dout("y_out", [NT, D])
    rope_in = P.din("rope_in", [2, 128, NG])
    rott_in = P.din("rott_in", [128, 128])
    ck_in = P.din("ck_in", [2, 256, 256])
    cv_in = P.din("cv_in", [2, 256, 256])
    maskb_in = P.din("maskb_in", [128, 20])
    again_in = P.din("again_in", [2, 128, 256])
    k_out = P.dout("k_out", [2, NT, 256])
    v_out = P.dout("v_out", [2, NT, 256])

    W = WStream(P)
    W.bind(wts, ntiles)

    X = P.sb("X", 8 * NT, F32)
    H = P.sb("H", 8 * NT, BF16)
    CONST = P.sb("CONST", NCONST, F32)
    IDENT = P.sb("IDENT", 128, F32)
    FLG = P.sb("FLG", 16, F32)
    ONESB = P.sb("ONESB", 128, BF16)
    IDENTB = P.sb("IDENTB", 128, BF16)
    EPSB = P.sb("EPSB", 4, F32)
    ROTT = P.sb("ROTT", 128, BF16)
    CONDT = P.sb("CONDT", 16, F32)
    SC = P.sb("SC", 16, BF16)
    GAINS = P.sb("GAINS", 72, F32)
    BMOD = P.sb("BMOD", 48, F32)
    MODS = [P.sb("MOD%d" % i, 96, F32) for i in range(2)]
    AB = [P.sb("AB%d" % i, 32, F32) for i in range(2)]
    scr0 = P.sb_ptr
    SCR_SZ = SB_END - scr0
    assert SCR_SZ >= 102400, SCR_SZ

    def scr(name, F, dtype, off):
        esz = 4 if dtype == F32 else 2
        assert off + F * esz <= SCR_SZ, (name, off, F * esz, SCR_SZ)
        return P.sb(name, F, dtype, addr=scr0 + off)

    HID = scr("HID", 32 * NT, BF16, 0)
    RSTD = scr("RSTD", NT, F32, 81920)
    SQ = [scr("SQ%d" % i, NT, BF16, 81920 + 5120 + i * 2560) for i in range(2)]
    TMPN = [scr("TMPN%d" % i, NT, F32, 81920 + 10240 + i * 5120) for i in range(2)]
    XTOK = [scr("XTOK%d" % i, 1024, F32, i * 4096) for i in range(2)]
    O = scr("O", 8 * NT, BF16, 81920)
    QT = scr("QT", 8 * NT, BF16, 0)
    KT = scr("KT", 2 * 1536, BF16, 20480)
    VT = scr("VT", 12 * 256, BF16, 26624)
    COS = scr("COS", NG, F32, 32768)
    SIN = scr("SIN", NG, F32, 36864)
    QRAW = [scr("QRAW%d" % i, NT, F32, 40960 + i * 5120) for i in range(2)]
    QN = scr("QN", NT, F32, 51200)
    QNB = scr("QNB", NT, BF16, 56320)
    T1 = [scr("T1_%d" % i, 512, F32, 58880 + i * 2048) for i in range(2)]
    T2 = [scr("T2_%d" % i, 512, F32, 62976 + i * 2048) for i in range(2)]
    PT = [scr("PT%d" % i, 512, BF16, 67072 + i * 1024) for i in range(4)]
    KTOK = [scr("KTOK%d" % i, 256, F32, 71168 + i * 1024) for i in range(2)]
    VSTG = [scr("VSTG%d" % i, 256, F32, 73216 + i * 1024) for i in range(2)]
    CKV = scr("CKV", 512, F32, 75264)
    MASKB = scr("MASKB", 32, F32, 77312)
    AGN = scr("AGN", 129, F32, 77440)
    KSS = scr("KSS", 16, F32, 78016)
    REC = scr("REC", 512, F32, 78080)
    AGN2 = scr("AGN2", 256, F32, 102400)
    KGB = AGN2
    KNB = scr("KNB", 256, BF16, 62976)
    KNF = scr("KNF", 256, F32, 62976 + 512)
    if nlayers >= 2 and mixers:
        _o = [0]

        def dscr(name, F, dtype):
            esz = 4 if dtype == F32 else 2
            t = scr(name, F, dtype, _o[0])
            _o[0] += (F * esz + 63) // 64 * 64
            assert _o[0] <= 81920, (name, _o[0])
            P.dscr_used = _o[0]
            return t
        D_NB = [dscr("D_NB%d" % i, 512, F32) for i in range(10)]
        D_FT = P.sb("D_FT", NT, F32, addr=D_NB[5].addr)
        D_G = P.sb("D_G", 160, F32, addr=D_NB[9].addr)
        D_TG = P.sb("D_TG", 160, F32, addr=D_NB[9].addr + 640)
        D_GLO = P.sb("D_GLO", 160, F32, addr=D_NB[9].addr + 1280)
        D_QNB = dscr("D_QNB", NT, BF16)
        D_KNBS = [dscr("D_KNB%d" % i, NT, BF16) for i in range(2)]
        D_VB = dscr("D_VB", NT, BF16)
        D_ZS = [dscr("D_ZS%d" % i, NT, BF16) for i in range(2)]
        D_PAD = dscr("D_PAD", 5 * 260, BF16)
        D_DG = dscr("D_DG", 5 * 128, BF16)
        D_SQ = P.sb("D_SQ", NT, BF16, addr=D_VB.addr)
        D_SQF = P.sb("D_SQF", 512, BF16, addr=D_NB[4].addr)
        D_RSQ = dscr("D_RSQ", 512, F32)
        D_VTOK = dscr("D_VTOK", NT, BF16)
        D_KDEC = dscr("D_KDEC", 2 * NT, BF16)
        D_XB = dscr("D_XB", 5 * 512, BF16)
        D_QKM = dscr("D_QKM", 5 * 512, BF16)
        D_QGT = dscr("D_QGT", 2 * NT, BF16)
        D_OD = dscr("D_OD", 2 * NT, BF16)
        D_SF = [dscr("D_SF%d" % i, 128, F32) for i in range(2)]
        D_SB = [dscr("D_SB%d" % i, 128, BF16) for i in range(2)]
        D_RB = [dscr("D_RB%d" % i, 128, BF16) for i in range(2)]
        D_VNB = [dscr("D_VNB%d" % i, 128, BF16) for i in range(2)]
        D_STG = [dscr("D_STG%d" % i, 128, F32) for i in range(4)]
        D_GC = dscr("D_GC", 160, F32)
        D_NEGGC = dscr("D_NEGGC", 160, F32)
        D_NEGEGC = dscr("D_NEGEGC", 160, F32)
        D_KD = dscr("D_KD", 160, F32)
        D_BETA = dscr("D_BETA", 160, F32)
        D_NEGBETA = dscr("D_NEGBETA", 160, F32)
        D_EGL = dscr("D_EGL", 320, F32)
        D_GP = dscr("D_GP", 33, F32)
        D_EA = dscr("D_EA", 16, F32)
        D_WAB = dscr("D_WAB", 256, BF16)
        D_CW = dscr("D_CW", 120, F32)


    with ExitStack() as es:
        PS = []
        for i in range(8):
            h = es.enter_context(nc.psum_tensor("ps%d" % i, [128, 512], F32))
            PS.append(T(P, "ps%d" % i, 512, F32, "ps", None, h))

        P.dma("sp", IDENT(0, 128), dreg(consts_in, 0, [[NCONST, 128], [1, 128]]), "cI")
        if not DBG_NOPRO:
            for ci in range(0, NCONST, 512):
                cw = min(512, NCONST - ci)
                P.dma("sp", CONST(ci, cw), dreg(consts_in, ci, [[NCONST, 128], [1, cw]]), "cI%d" % ci)
        if not DBG_NOPRO:
            P.dma("sp", FLG(0, 16), dreg(flags_in, 0, [[16, 128], [1, 16]]), "cF")
            P.memset("dve", EPSB(3, 1), 1.0)
            P.cp("act", IDENTB(0, 128), CONST(0, 128))
        P.dma("sp", CONDT(0, 16), dreg(cond_in, 0, [[16, 128], [1, 16]]), "cC")
        P.dma("sp", GAINS(0, 72), dreg(gains_in, 0, [[72, 128], [1, 72]]), "cG")
        P.memset("dve", ONESB(0, 128), 1.0)
        P.dma("pool", ROTT(0, 128), dreg(rott_in, 0, [[128, 128], [1, 128]]), "c1")
        P.memset("dve", EPSB(0, 1), float(D * EPS))
        P.memset("dve", EPSB(1, 1), float(HD * EPS))
        P.memset("dve", EPSB(2, 1), float(EPS))
        P.act(SC(0, 16), CONDT(0, 16), AF.Silu)
        W.start()

        for b in range(10):
            xt = XTOK[b % 2]
            P.dma("sp", xt(0, 1024), dreg(x_in, b * 128 * D, [[D, 128], [1, D]]), "xin%d" % (b % 2))
            for half in range(2):
                ps = PS[(2 * b + half) % 4]
                for j in range(4):
                    c = half * 4 + j
                    P.tr(ps(j * 128, 128), xt(c * 128, 128), IDENT(0, 128))
                P.cp("dve" if half == 0 else "act",
                     X.v(half * 4 * NT + b * 128, [[NT, 4], [1, 128]]),
                     ps.v(0, [[128, 4], [1, 128]]))

        def modulation(l, par):
            MOD = MODS[par]
            P.dma("sp", BMOD(0, 48), dreg(bmod_in, l * 128 * 48, [[48, 128], [1, 48]]), "bmod")
            ps = PS[7]
            for t in range(12):
                slot = W.acquire(("mod", l, t))
                for j in range(4):
                    f = t * 4 + j
                    for kc in range(8):
                        P.mm(ps(f * 2, 2), slot(kc * 512 + j * 128, 128), SC(kc * 2, 2),
                             start=(kc == 0), stop=(kc == 7))
                W.release()
                yield
            P.tt("dve", MOD.v(0, [[2, 48], [1, 2]]), ps.v(0, [[2, 48], [1, 2]]),
                 BMOD.v(0, [[1, 48], [0, 2]]), ALU.add)
            A = AB[par]
            for which, (m_scale, goff) in enumerate(((1, l * 8), (4, 32 + l * 8))):
                P.ts("dve", A.v(which * 16, [[2, 8], [1, 2]]), MOD.v(m_scale * 16, [[2, 8], [1, 2]]),
                     1.0, ALU.add, 32.0, ALU.mult)
                P.tt("dve", A.v(which * 16, [[2, 8], [1, 2]]), A.v(which * 16, [[2, 8], [1, 2]]),
                     GAINS.v(goff, [[1, 8], [0, 2]]), ALU.mult)
            yield

        def run_all(gen):
            for _ in gen:
                pass

        def rms_rstd(src_chunk):
            for c in range(8):
                sq = SQ[c % 2]
                P.act(sq(0, NT), src_chunk(c), AF.Square)
                for ti, (t0, tn) in enumerate(TBS):
                    P.mm(PS[4 + ti](0, tn), ONESB(0, 128), sq(t0, tn), start=(c == 0), stop=(c == 7))
            for ti, (t0, tn) in enumerate(TBS):
                P.act(RSTD(t0, tn), PS[4 + ti](0, tn), AF.Sqrt, bias=EPSB(0, 1))
                P.add("dve", (lambda o, i: (lambda e: e.reciprocal(o.ap, i.ap)))(RSTD(t0, tn), RSTD(t0, tn)),
                      [RSTD(t0, tn)], [RSTD(t0, tn)])

        def norm_mod(par, which, m_shift):
            MOD = MODS[par]
            A = AB[par]
            rms_rstd(lambda c: X(c * NT, NT))
            for c in range(8):
                tmp = TMPN[c % 2]
                P.tt("dve" if c % 2 == 0 else "pool", tmp(0, NT), X(c * NT, NT), RSTD(0, NT), ALU.mult)
                for (t0, tn, cd) in ((0, NG, 0), (NG, NE, 1)):
                    P.act(H(c * NT + t0, tn), tmp(t0, tn), AF.Identity,
                          bias=MOD((m_shift * 8 + c) * 2 + cd, 1),
                          scale=A(which * 16 + c * 2 + cd, 1))

        def gated_residual(par, m_gate, c, pss):
            MOD = MODS[par]
            for ti, (t0, tn) in enumerate(TBS):
                cd = 0 if t0 < NG else 1
                P.stt("dve", X(c * NT + t0, tn), pss[ti](0, tn), MOD((m_gate * 8 + c) * 2 + cd, 1),
                      X(c * NT + t0, tn), ALU.mult, ALU.add)

        def mlp(l, par, side_gen):
            norm_mod(par, 1, 3)
            RT = TMPN
            for t in range(8):
                slot = W.acquire(("w1", l, t))
                for j in range(4):
                    oc = t * 4 + j
                    base = (oc % 2) * 3
                    for kc in range(8):
                        for ti, (t0, tn) in enumerate(TBS):
                            P.mm(PS[base + ti](0, tn), slot(kc * 512 + j * 128, 128), H(kc * NT + t0, tn),
                                 start=(kc == 0), stop=(kc == 7))
                    rt = RT[oc % 2]
                    for ti, (t0, tn) in enumerate(TBS):
                        P.act(rt(t0, tn), PS[base + ti](0, tn), AF.Relu)
                    P.tt("pool", HID(oc * NT, NT), rt(0, NT), rt(0, NT), ALU.mult)
                W.release()
                if side_gen is not None:
                    next(side_gen, None)
            for oc in range(8):
                slot = W.acquire(("w2", l, oc))
                base = (oc % 2) * 3
                for kc in range(32):
                    for ti, (t0, tn) in enumerate(TBS):
                        P.mm(PS[base + ti](0, tn), slot(kc * 128, 128), HID(kc * NT + t0, tn),
                             start=(kc == 0), stop=(kc == 31))
                gated_residual(par, 5, oc, [PS[base + ti] for ti in range(3)])
                W.release()
                if side_gen is not None:
                    next(side_gen, None)
            if side_gen is not None:
                run_all(side_gen)


        def attention(l, par):
            j = l // 2
            norm_mod(par, 0, 0)
            P.dma("sp", COS(0, NG), dreg(rope_in, 0, [[NG, 128], [1, NG]]), "aCOS")
            P.dma("sp", SIN(0, NG), dreg(rope_in, 128 * NG, [[NG, 128], [1, NG]]), "aSIN")
            P.dma("sp", MASKB(0, 20), dreg(maskb_in, 0, [[20, 128], [1, 20]]), "aMB")
            P.dma("sp", AGN2(0, 256), dreg(again_in, j * 128 * 256, [[256, 128], [1, 256]]), "aAG")
            P.dma("sp", CKV.v(0, [[256, 2], [1, 256]]),
                  dreg(ck_in, j * 256 * 256, [[256, 128], [128 * 256, 2], [1, 256]]), "aCK")
            P.dma("pool", VT.v(0, [[256, 2], [1, 256]]),
                  dreg(cv_in, j * 256 * 256, [[256, 128], [128 * 256, 2], [1, 256]]), "acv")
            if DBG_STAGE == 0:
                return
            for kc in range(2):
                ps = PS[6 + kc]
                for kv in range(2):
                    P.tr(ps(kv * 128, 128), CKV(kc * 256 + kv * 128, 128), IDENT(0, 128))
                P.cp("act", KT.v(kc * 128, [[1536, 2], [1, 128]]), ps.v(0, [[128, 2], [1, 128]]))
            if DBG_STAGE == 1:
                return
            for t in range(2):
                slot = W.acquire(("aq", j, t))
                for jj in range(4):
                    h = t * 4 + jj
                    base = (h % 2) * 3
                    for kc in range(8):
                        for ti, (t0, tn) in enumerate(TBS):
                            P.mm(PS[base + ti](0, tn), slot(kc * 512 + jj * 128, 128), H(kc * NT + t0, tn),
                                 start=(kc == 0), stop=(kc == 7))
                    qraw = QRAW[h % 2]
                    sq = SQ[h % 2]
                    for ti, (t0, tn) in enumerate(TBS):
                        P.cp("dve", qraw(t0, tn), PS[base + ti](0, tn))
                        P.act(sq(t0, tn), qraw(t0, tn), AF.Square)
                    for ti, (t0, tn) in enumerate(TBS):
                        if DBG_SUB == 0:
                            break
                        pb = PS[6 + ti % 2]
                        P.mm(pb(0, tn), ONESB(0, 128), sq(t0, tn))
                        P.act(REC(0, tn), pb(0, tn), AF.Sqrt, bias=EPSB(2, 1), scale=1.0 / HD)
                        P.add("dve", (lambda o: (lambda e: e.reciprocal(o.ap, o.ap)))(REC(0, tn)),
                              [REC(0, tn)], [REC(0, tn)])
                        if DBG_SUB == 1:
                            continue
                        P.stt("dve", QN(t0, tn), qraw(t0, tn), AGN2(128, 1), REC(0, tn), ALU.mult, ALU.mult)
                        if DBG_SUB == 2:
                            continue
                        if t0 >= NG:
                            P.cp("act", QT(h * NT + t0, tn), QN(t0, tn))
                        else:
                            P.cp("act", QNB(t0, tn), QN(t0, tn))
                            P.mm(pb(0, tn), ROTT(0, 128), QNB(t0, tn))
                            if DBG_SUB == 3:
                                continue
                            P.tt("dve", T1[ti](0, tn), pb(0, tn), SIN(t0, tn), ALU.mult)
                            P.tt("pool", T2[ti](0, tn), QN(t0, tn), COS(t0, tn), ALU.mult)
                            P.tt("pool", QT(h * NT + t0, tn), T1[ti](0, tn), T2[ti](0, tn), ALU.add)
                W.release()
            if DBG_STAGE == 2:
                return
            P.memset("dve", KSS(0, 16), 1.0)
            slot = W.acquire(("akv", j))
            for b in range(10):
                ps = PS[b % 4]
                for kc in range(8):
                    P.mm(ps(0, 512), H(kc * NT + b * 128, 128), slot(kc * 512, 512),
                         start=(kc == 0), stop=(kc == 7))
                ktok = KTOK[b % 2]
                vstg = VSTG[b % 2]
                if DBG_SUB == 10:
                    continue
                P.act(T1[0](0, 256), ps(0, 256), AF.Square)
                P.add("dve", (lambda o, i: (lambda e: e.tensor_reduce(o.ap, i.ap, mybir.AxisListType.X, ALU.add)))(
                    KSS(0, 2), T1[0].v(0, [[128, 2], [1, 128]])), [T1[0](0, 256)], [KSS(0, 2)])
                P.act(KSS(8, 8), KSS(0, 8), AF.Sqrt, bias=EPSB(2, 1), scale=1.0 / HD)
                P.add("dve", (lambda o: (lambda e: e.reciprocal(o.ap, o.ap)))(KSS(8, 8)), [KSS(8, 8)], [KSS(8, 8)])
                for kv in range(2):
                    P.stt("dve", ktok(kv * 128, 128), ps(kv * 128, 128), KSS(8 + kv, 1), KGB(0, 128),
                          ALU.mult, ALU.mult)
                if DBG_SUB == 11:
                    continue
                P.dma("sp", dreg(k_out, (j * NT + b * 128) * 256, [[256, 128], [1, 256]]), ktok(0, 256),
                      "ko%d" % (b % 2))
                if DBG_SUB == 12:
                    continue
                P.cp("act", VT((2 + b) * 256, 256), ps(256, 256))
                P.cp("dve", vstg(0, 256), ps(256, 256))
                P.dma("sp", dreg(v_out, (j * NT + b * 128) * 256, [[256, 128], [1, 256]]), vstg(0, 256),
                      "vo%d" % (b % 2))
                if DBG_SUB == 13:
                    continue
                pt = PS[4 + b % 2]
                for kv in range(2):
                    P.tr(pt(kv * 128, 128), ktok(kv * 128, 128), IDENT(0, 128))
                kdst = KT.v(256 + b * 128, [[1536, 2], [1, 128]])
                if DBG_SUB == 14:
                    continue
                if b >= 8:
                    if DBG_SUB not in (152, 153):
                        P.cp("act", kdst, pt.v(0, [[128, 2], [1, 128]]))
                else:
                    if DBG_SUB not in (151, 153):
                        P.cp("act", KNB(0, 256), pt(0, 256))
                    if DBG_SUB not in (151, 152):
                        P.cp("dve", KNF(0, 256), pt(0, 256))
                    if DBG_SUB in (15, 151, 152, 153):
                        continue
                    pr = PS[6 + b % 2]
                    P.mm(pr(0, 256), ROTT(0, 128), KNB(0, 256))
                    if DBG_SUB == 16:
                        continue
                    P.tt("dve", T1[1].v(0, [[128, 2], [1, 128]]), pr.v(0, [[128, 2], [1, 128]]),
                         SIN.v(b * 128, [[0, 2], [1, 128]]), ALU.mult)
                    P.tt("dve", T2[1].v(0, [[128, 2], [1, 128]]), KNF.v(0, [[128, 2], [1, 128]]),
                         COS.v(b * 128, [[0, 2], [1, 128]]), ALU.mult)
                    P.tt("dve", kdst, T1[1].v(0, [[128, 2], [1, 128]]), T2[1].v(0, [[128, 2], [1, 128]]), ALU.add)
            W.release()
            if DBG_STAGE == 3:
                return
            it = 0
            for h in range(8):
                kv = h // 4
                for (q0, qn, chunks) in ((0, 512, list(range(10))), (512, 512, list(range(10))),
                                         (NG, NE, [10, 11])):
                    po = PS[2 + it % 2]
                    pd = PS[4 + it % 2]
                    it += 1
                    for ci, kc in enumerate(chunks):
                        pS = PS[ci % 2]
                        ptile = PT[ci % 4]
                        P.mm(pS(0, qn), KT(kv * 1536 + kc * 128, 128), QT(h * NT + q0, qn))
                        if q0 >= NG:
                            P.act(ptile(0, qn), pS(0, qn), AF.Exp, scale=float(HD ** -0.5))
                        else:
                            kb = 0 if kc < 2 else 1 + (kc - 2) // 2
                            for hf in range(2):
                                qblk = q0 // 256 + hf
                                P.act(ptile(hf * 256, 256), pS(hf * 256, 256), AF.Exp,
                                      bias=MASKB(kb * 4 + qblk, 1), scale=float(HD ** -0.5))
                        first, last = (ci == 0), (ci == len(chunks) - 1)
                        P.mm(po(0, qn), VT(kc * 256 + kv * 128, 128), ptile(0, qn), start=first, stop=last)
                        P.mm(pd(0, qn), ONESB(0, 128), ptile(0, qn), start=first, stop=last)
                    P.add("dve", (lambda o, i: (lambda e: e.reciprocal(o.ap, i.ap)))(REC(0, qn), pd(0, qn)),
                          [pd(0, qn)], [REC(0, qn)])
                    P.tt("dve", O(h * NT + q0, qn), po(0, qn), REC(0, qn), ALU.mult)
            if DBG_STAGE == 4:
                return
            for t in range(2):
                slot = W.acquire(("ao", j, t))
                for jj in range(4):
                    oc = t * 4 + jj
                    base = (oc % 2) * 3
                    for kc in range(8):
                        for ti, (t0, tn) in enumerate(TBS):
                            P.mm(PS[base + ti](0, tn), slot(kc * 512 + jj * 128, 128), O(kc * NT + t0, tn),
                                 start=(kc == 0), stop=(kc == 7))
                    gated_residual(par, 2, oc, [PS[base + ti] for ti in range(3)])
                W.release()


        def deltanet(l, par):
            j = l // 2
            norm_mod(par, 0, 0)
            C_MADD, C_MS, C_UF, C_UB, C_SAME, C_IND0, C_ONESF = 128, 384, 640, 768, 896, 1024, 1280
            C_BD16, C_OFF1, C_OFF2 = 1408, 1536, 1664
            P.dma("pool", D_WAB(0, 256), dreg(dnab_in, j * 128 * 256, [[256, 128], [1, 256]]), "dWAB")
            P.dma("sp", D_CW(0, 120), dreg(convw_in, j * 128 * 120, [[120, 128], [1, 120]]), "dCW")
            P.dma("sp", D_GP(0, 33), dreg(dngp_in, j * 128 * 33, [[33, 128], [1, 33]]), "dGP")
            P.memset("pool", D_PAD(0, 5 * 260), 0.0)
            pab, pgc, pgl, peg = PS[0], PS[1], PS[2], PS[3]
            for b in range(10):
                for kc in range(8):
                    P.mm(pab(b * 32, 32), H(kc * NT + b * 128, 128), D_WAB(kc * 32, 32),
                         start=(kc == 0), stop=(kc == 7))
            v3 = lambda t, c0=0, w=16, st=16: t.v(c0, [[st, 10], [1, w]])
            P.tt("dve", v3(D_TG), pab.v(0, [[32, 10], [1, 16]]), D_GP.v(16, [[0, 10], [1, 16]]), ALU.add)
            SPY, SPZ, SPP = D_GLO, D_KD, D_NEGGC
            P.ts("dve", SPY(0, 160), D_TG(0, 160), -1.0, ALU.mult)
            P.tt("dve", SPY(0, 160), SPY(0, 160), D_TG(0, 160), ALU.max)
            P.act(SPY(0, 160), SPY(0, 160), AF.Exp, scale=-1.0)
            P.ts("dve", SPZ(0, 160), SPY(0, 160), 2.0, ALU.add)
            P.add("dve", (lambda o: (lambda e: e.reciprocal(o.ap, o.ap)))(SPZ(0, 160)), [SPZ(0, 160)], [SPZ(0, 160)])
            P.tt("dve", SPZ(0, 160), SPY(0, 160), SPZ(0, 160), ALU.mult)
            P.tt("dve", SPY(0, 160), SPZ(0, 160), SPZ(0, 160), ALU.mult)
            P.ts("dve", SPP(0, 160), SPY(0, 160), 1.0 / 9.0, ALU.mult, 1.0 / 7.0, ALU.add)
            for cst in (1.0 / 5.0, 1.0 / 3.0, 1.0):
                P.tt("dve", SPP(0, 160), SPP(0, 160), SPY(0, 160), ALU.mult)
                P.ts("dve", SPP(0, 160), SPP(0, 160), cst, ALU.add)
            P.tt("dve", SPP(0, 160), SPP(0, 160), SPZ(0, 160), ALU.mult)
            P.ts("dve", D_TG(0, 160), D_TG(0, 160), 0.0, ALU.max)
            P.stt("dve", D_TG(0, 160), SPP(0, 160), 2.0, D_TG(0, 160), ALU.mult, ALU.add)
            P.act(D_EA(0, 16), D_GP(0, 16), AF.Exp)
            P.stt("dve", v3(D_G), v3(D_TG), -1.0, D_EA.v(0, [[0, 10], [1, 16]]), ALU.mult, ALU.mult)
            P.act(v3(D_BETA), pab.v(16, [[32, 10], [1, 16]]), AF.Sigmoid)
            P.ts("pool", D_NEGBETA(0, 160), D_BETA(0, 160), -1.0, ALU.mult)
            for b in range(10):
                P.mm(pgc(b * 16, 8), CONST(C_UF, 128), D_G(b * 16, 8))
                P.mm(pgc(b * 16 + 8, 8), CONST(C_UB, 128), D_G(b * 16 + 8, 8))
                P.mm(pgl(b * 16, 16), CONST(C_SAME, 128), D_G(b * 16, 16))
                for c in range(2):
                    P.mm(peg(b * 32 + c * 16, 16), CONST(C_IND0 + c * 128, 128), D_G(b * 16, 16))
            P.cp("dve", D_GC(0, 160), pgc(0, 160))
            P.ts("dve", D_NEGGC(0, 160), pgc(0, 160), -1.0, ALU.mult)
            P.act(D_NEGEGC(0, 160), pgc(0, 160), AF.Exp)
            P.ts("pool", D_NEGEGC(0, 160), D_NEGEGC(0, 160), -1.0, ALU.mult)
            P.tt("dve", D_KD(0, 160), pgl(0, 160), D_GC(0, 160), ALU.subtract)
            P.act(D_KD(0, 160), D_KD(0, 160), AF.Exp)
            P.act(D_EGL(0, 320), peg(0, 320), AF.Exp)

            def stageA(h):
                slot = W.acquire(("dnh", j, h))
                for part in range(4):
                    for kc in range(8):
                        for ti, (t0, tn) in enumerate(TBS):
                            P.mm(PS[ti](0, tn), slot(kc * 512 + part * 128, 128), H(kc * NT + t0, tn),
                                 start=(kc == 0), stop=(kc == 7))
                    yield
                    if part == 3:
                        for ti, (t0, tn) in enumerate(TBS):
                            P.act(D_ZS[h % 2](t0, tn), PS[ti](0, tn), AF.Silu)
                        yield
                        continue
                    ch = part * 8 + h
                    for tap in range(5):
                        P.ts("pool", D_DG(tap * 128, 128), IDENT(0, 128), D_CW(ch * 5 + tap, 1), ALU.mult)
                    for ti, (t0, tn) in enumerate(TBS):
                        ns = tn // 256
                        P.cp("act" if ti != 1 else "dve", D_PAD.v((t0 // 256) * 260 + 2, [[260, ns], [1, 256]]),
                             PS[ti].v(0, [[256, ns], [1, 256]]))
                    P.ts("dve", D_PAD.v(260, [[260, 3], [1, 2]]), D_PAD.v(256, [[260, 3], [1, 2]]), FLG(0, 1), ALU.mult)
                    P.ts("dve", D_PAD.v(258, [[260, 3], [1, 2]]), D_PAD.v(262, [[260, 3], [1, 2]]), FLG(0, 1), ALU.mult)
                    for ti, (t0, tn) in enumerate(TBS):
                        ns = tn // 256
                        for tap in range(5):
                            P.mm(PS[3 + ti](0, tn), D_DG(tap * 128, 128),
                                 D_PAD.v((t0 // 256) * 260 + tap, [[260, ns], [1, 256]]),
                                 start=(tap == 0), stop=(tap == 4))
                    yield
                    dst = D_VB if part == 2 else D_FT
                    for ti, (t0, tn) in enumerate(TBS):
                        P.act(dst(t0, tn), PS[3 + ti](0, tn), AF.Silu)
                    if part == 2:
                        yield
                        continue
                    P.act(D_SQ(0, NT), D_FT(0, NT), AF.Square)
                    out = D_QNB if part == 0 else D_KNBS[h % 2]
                    for ti, (t0, tn) in enumerate(TBS):
                        P.mm(PS[3 + ti](0, tn), ONESB(0, 128), D_SQ(t0, tn))
                        P.act(D_RSQ(0, tn), PS[3 + ti](0, tn), AF.Sqrt, bias=EPSB(2, 1))
                        P.add("dve", (lambda o: (lambda e: e.reciprocal(o.ap, o.ap)))(D_RSQ(0, tn)),
                              [D_RSQ(0, tn)], [D_RSQ(0, tn)])
                        P.stt("dve", out(t0, tn), D_FT(t0, tn), float(HD ** -0.5) if part == 0 else 1.0,
                              D_RSQ(0, tn), ALU.mult, ALU.mult)
                    yield
                W.release()

            def stageB(h):
                D_KNB = D_KNBS[h % 2]
                for b in range(10):
                    ps = PS[b % 2]
                    P.mm(ps(0, 128), D_KNB(b * 128, 128), IDENTB(0, 128))
                    P.mm(ps(128, 128), D_VB(b * 128, 128), IDENTB(0, 128))
                    for dr in range(2):
                        P.ts("dve", D_KDEC(dr * NT + b * 128, 128), ps(0, 128), D_KD(b * 16 + dr * 8 + h, 1), ALU.mult)
                    P.cp("act", D_VTOK(b * 128, 128), ps(128, 128))
                yield
                m4 = lambda t, c0=0: t.v(c0, [[256, 2], [128, 2], [1, 128]])
                for bp in range(5):
                    pk, pc, pt_, pl = PS[2], PS[3], PS[4], (PS[5], PS[0], PS[1])
                    for bl in range(2):
                        blk = 2 * bp + bl
                        P.mm(pk(bl * 128, 128), D_KNB(blk * 128, 128), D_KNB(blk * 128, 128))
                        P.mm(pk(256 + bl * 128, 128), D_KNB(blk * 128, 128), D_QNB(blk * 128, 128))
                    DIAG, EB, DT = D_NB[0], D_NB[1], D_NB[2]
                    Pa, Za = D_NB[3], D_NB[4]
                    gsel = lambda t: t.v(bp * 32 + h, [[16, 2], [8, 2], [0, 128]])
                    P.tt("pool", m4(DIAG), IDENT.v(0, [[0, 2], [0, 2], [1, 128]]), gsel(D_GC), ALU.mult)
                    for m in range(4):
                        P.mm(pc(m * 128, 128), CONST(C_ONESF, 128), DIAG(m * 128, 128))
                    P.act(EB(0, 512), pc(0, 512), AF.Exp)
                    P.tt("pool", D_QGT.v(2 * bp * 128, [[128, 2], [NT, 2], [1, 128]]),
                         D_QNB.v(2 * bp * 128, [[128, 2], [0, 2], [1, 128]]), m4(EB), ALU.mult)
                    P.tt("dve", m4(DT), m4(pc), CONST.v(C_MADD, [[0, 2], [128, 2], [1, 128]]), ALU.add)
                    for m in range(4):
                        bl, dr = m // 2, m % 2
                        P.act(DT(m * 128, 128), DT(m * 128, 128), AF.Exp,
                              bias=D_NEGGC((2 * bp + bl) * 16 + dr * 8 + h, 1))
                    P.tt("dve", m4(D_QKM, bp * 512), pk.v(256, [[128, 2], [0, 2], [1, 128]]), m4(DT), ALU.mult)
                    P.tt("pool", m4(DT), m4(DT), CONST.v(C_MS, [[0, 2], [128, 2], [1, 128]]), ALU.mult)
                    for m in range(4):
                        bl, dr = m // 2, m % 2
                        P.stt("dve", Pa(m * 128, 128), pk(bl * 128, 128),
                              D_NEGBETA((2 * bp + bl) * 16 + dr * 8 + h, 1), DT(m * 128, 128), ALU.mult, ALU.mult)
                    for m in range(4):
                        P.tr(pt_(m * 128, 128), Pa(m * 128, 128), IDENT(0, 128))
                    P.cp("act", Za(0, 512), pt_(0, 512))
                    yield
                    NBs = D_NB
                    msk = lambda c0: CONST.v(c0, [[0, 4], [1, 128]])
                    f4 = lambda t: t.v(0, [[128, 4], [1, 128]])
                    Pc, Zc = NBs[0], NBs[1]
                    P.tt("pool", f4(Pc), f4(Pa), msk(C_BD16), ALU.mult)
                    P.tt("pool", f4(Zc), f4(Za), msk(C_BD16), ALU.mult)
                    Xc, XTc, Xn, XTn = NBs[5], NBs[6], NBs[8], NBs[9]
                    P.tt("pool", f4(Xc), f4(Pc), msk(0), ALU.add)
                    P.tt("dve", f4(XTc), f4(Zc), msk(0), ALU.add)
                    pz = [(NBs[2], NBs[7]), (NBs[0], NBs[1])]
                    for k in range(1, 4):
                        Pn, Zn = pz[(k - 1) % 2]
                        for m in range(4):
                            P.mm(pl[0](m * 128, 128), Zc(m * 128, 128), Pc(m * 128, 128))
                        for m in range(4):
                            P.mm(pl[1](m * 128, 128), Pc(m * 128, 128), Zc(m * 128, 128))
                        P.cp("act", Pn(0, 512), pl[0](0, 512))
                        P.cp("dve", Zn(0, 512), pl[1](0, 512))
                        for m in range(4):
                            P.mm(pl[2](m * 128, 128), Zn(m * 128, 128), Xc(m * 128, 128))
                        for m in range(4):
                            P.mm(pt_(m * 128, 128), Pn(m * 128, 128), XTc(m * 128, 128))
                        P.tt("dve", Xn(0, 512), pl[2](0, 512), Xc(0, 512), ALU.add)
                        P.tt("dve", XTn(0, 512), pt_(0, 512), XTc(0, 512), ALU.add)
                        Pc, Zc = Pn, Zn
                        Xc, XTc, Xn, XTn = Xn, XTn, Xc, XTc
                        yield
                    for lv, cmask in ((1, C_OFF1), (2, C_OFF2)):
                        Nl, NlT, T1, T1p = NBs[0], NBs[1], NBs[2], NBs[7]
                        P.tt("pool", f4(Nl), f4(Pa), msk(cmask), ALU.mult)
                        P.tt("pool", f4(NlT), f4(Za), msk(cmask), ALU.mult)
                        for m in range(4):
                            P.mm(pl[0](m * 128, 128), NlT(m * 128, 128), Xc(m * 128, 128))
                        P.cp("act", T1(0, 512), pl[0](0, 512))
                        if lv < 2:
                            for m in range(4):
                                P.mm(pl[1](m * 128, 128), Nl(m * 128, 128), XTc(m * 128, 128))
                            P.cp("dve", T1p(0, 512), pl[1](0, 512))
                        for m in range(4):
                            P.mm(pl[2](m * 128, 128), XTc(m * 128, 128), T1(m * 128, 128))
                        if lv < 2:
                            for m in range(4):
                                P.mm(pt_(m * 128, 128), Xc(m * 128, 128), T1p(m * 128, 128))
                            P.tt("dve", Xn(0, 512), pl[2](0, 512), Xc(0, 512), ALU.add)
                            P.tt("dve", XTn(0, 512), pt_(0, 512), XTc(0, 512), ALU.add)
                            Xc, XTc, Xn, XTn = Xn, XTn, Xc, XTc
                        else:
                            P.tt("dve", D_XB(bp * 512, 512), pl[2](0, 512), Xc(0, 512), ALU.add)
                        yield

            def chain(h, dr):
                D_KNB = D_KNBS[h % 2]
                SF, SB_, RB, VNB = D_SF[dr], D_SB[dr], D_RB[dr], D_VNB[dr]
                ps = PS[6 + dr]
                pr, pv, po, pss = ps(0, 128), ps(128, 128), ps(256, 64), ps(384, 128)
                gcol = dr * 8 + h
                if dr == 0:
                    P.dma("sp", SF(0, 128), dreg(s0_in, ((j * 16) + gcol) * 16384, [[128, 128], [1, 128]]), "dS0_%d" % dr)
                else:
                    P.memset("dve", SF(0, 128), 0.0)
                P.cp("act", SB_(0, 128), SF(0, 128))
                order = list(range(20)) if dr == 0 else list(range(19, -1, -1))
                nst = 0
                for ch in order:
                    b, hf = ch // 2, ch % 2
                    p0 = hf * 64
                    bp, m = b // 2, (b % 2) * 2 + dr
                    seg = ch // 4
                    half = lambda t, c0=0, w=128: t.v(c0, [[1, w]], p0, 64)
                    P.mm(pr, D_KNB(b * 128, 128), SB_(0, 128))
                    P.stt("dve", half(RB), half(ps, 0), half(D_NEGEGC, b * 16 + gcol, 1), half(D_VTOK, b * 128),
                          ALU.mult, ALU.add)
                    P.mm(pv, half(D_XB, bp * 512 + m * 128), half(RB))
                    P.ts("dve", half(VNB), half(ps, 128), half(D_BETA, b * 16 + gcol, 1), ALU.mult)
                    P.mm(po, SB_(0, 128), D_QGT(dr * NT + b * 128 + p0, 64), start=True, stop=False)
                    P.mm(po, half(VNB), half(D_QKM, bp * 512 + m * 128 + p0, 64), start=False, stop=True)
                    P.cp("act", D_OD(dr * NT + b * 128 + p0, 64), po)
                    P.mm(pss, half(D_KDEC, dr * NT + b * 128), half(VNB))
                    P.stt("dve", SF(0, 128), SF(0, 128), D_EGL(b * 32 + hf * 16 + gcol, 1), pss, ALU.mult, ALU.add)
                    last = (ch % 4 == 3) if dr == 0 else (ch % 4 == 0)
                    if last:
                        stg = D_STG[dr * 2 + nst % 2]
                        nst += 1
                        P.cp("act", stg(0, 128), SF(0, 128))
                        P.dma("sp", dreg(s_out, ((((j * 5 + seg) * 2 + dr) * 8 + h) * 16384), [[128, 128], [1, 128]]),
                              stg(0, 128), "dSO%d" % (dr * 2 + (nst - 1) % 2))
                        if dr == 0:
                            if seg < 3:
                                P.ts("dve", SF(0, 128), SF(0, 128), FLG(0, 1), ALU.mult)
                            elif seg == 3:
                                P.memset("dve", SF(0, 128), 0.0)
                        else:
                            if seg == 4:
                                P.dma("sp", SF(0, 128), dreg(s0_in, ((j * 16) + gcol) * 16384, [[128, 128], [1, 128]]),
                                      "dS0_%d" % dr)
                            elif seg >= 1:
                                P.ts("dve", SF(0, 128), SF(0, 128), FLG(0, 1), ALU.mult)
                    P.cp("act", SB_(0, 128), SF(0, 128))
                    yield

            def finalize(h):
                OS, TM = D_NB[0], D_NB[3]
                for ti, (t0, tn) in enumerate(TBS):
                    P.tt("pool", OS(0, tn), D_OD(t0, tn), D_OD(NT + t0, tn), ALU.add)
                    P.act(D_SQF(0, tn), OS(0, tn), AF.Square)
                    P.mm(PS[ti](0, tn), ONESB(0, 128), D_SQF(0, tn))
                    P.act(D_RSQ(0, tn), PS[ti](0, tn), AF.Sqrt, bias=EPSB(2, 1), scale=1.0 / HD)
                    P.add("dve", (lambda o: (lambda e: e.reciprocal(o.ap, o.ap)))(D_RSQ(0, tn)),
                          [D_RSQ(0, tn)], [D_RSQ(0, tn)])
                    P.stt("dve", TM(0, tn), OS(0, tn), D_GP(32, 1), D_RSQ(0, tn), ALU.mult, ALU.mult)
                    P.tt("pool", O(h * NT + t0, tn), TM(0, tn), D_ZS[h % 2](t0, tn), ALU.mult)

            def interleave(gens):
                gens = [g for g in gens if g is not None]
                while gens:
                    for g in list(gens):
                        try:
                            next(g)
                        except StopIteration:
                            gens.remove(g)

            if DBG_STAGE == 10:
                return
            if DBG_STAGE == 11:
                for h in range(8):
                    run_all(stageA(h))
                return
            if DBG_STAGE == 12:
                for h in range(8):
                    run_all(stageA(h))
                    run_all(stageB(h))
                return
            run_all(stageA(0))
            run_all(stageB(0))
            for h in range(8):
                interleave([chain(h, 0), chain(h, 1), stageA(h + 1) if h < 7 else None])
                if DBG_DUMP and l == 1 and h == 0:
                    for ii, tt_ in enumerate((D_QNB, D_KNB, D_VB)):
                        P.dma("pool", dreg(dbg_out, ii * 128 * NT, [[NT, 128], [1, NT]]), tt_(0, NT), "dbg%d" % ii)
                    for ii, tt_ in enumerate((D_GC, D_BETA, D_KD, D_NEGEGC)):
                        P.dma("sp", dreg(dbg_out, (3 + ii) * 128 * NT, [[NT, 128], [1, 160]]), tt_(0, 160), "dbg%d" % (3 + ii))
                    P.dma("sp", dreg(dbg_out, 7 * 128 * NT, [[NT, 128], [1, 320]]), D_EGL(0, 320), "dbg7")
                finalize(h)
                if h < 7:
                    run_all(stageB(h + 1))
            for t in range(2):
                slot = W.acquire(("do", j, t))
                for jj in range(4):
                    oc = t * 4 + jj
                    base = (oc % 2) * 3
                    for kc in range(8):
                        for ti, (t0, tn) in enumerate(TBS):
                            P.mm(PS[base + ti](0, tn), slot(kc * 512 + jj * 128, 128), O(kc * NT + t0, tn),
                                 start=(kc == 0), stop=(kc == 7))
                    gated_residual(par, 2, oc, [PS[base + ti] for ti in range(3)])
                W.release()

        run_all(modulation(0, 0))
        for l in range(nlayers):
            par = l % 2
            if mixers:
                if l % 2 == 0:
                    attention(l, par)
                else:
                    deltanet(l, par)
            side = modulation(l + 1, 1 - par) if l + 1 < nlayers else None
            mlp(l, par, side)

        rms_rstd(lambda c: X(c * NT, NT))
        for c in range(8):
            tmp = TMPN[c % 2]
            P.tt("dve", tmp(0, NT), X(c * NT, NT), RSTD(0, NT), ALU.mult)
            P.act(X(c * NT, NT), tmp(0, NT), AF.Copy, scale=GAINS(64 + c, 1))
        P.ts("pool", X(0, 8 * NT), X(0, 8 * NT), 32.0, ALU.mult)
        for b in range(10):
            yt = XTOK[b % 2]
            for half in range(2):
                ps = PS[(2 * b + half) % 4]
                for j in range(4):
                    c = half * 4 + j
                    P.tr(ps(j * 128, 128), X(c * NT + b * 128, 128), IDENT(0, 128))
                P.cp("dve" if half == 0 else "act", yt(half * 512, 512), ps(0, 512))
            P.dma("sp", dreg(y_out, b * 128 * D, [[D, 128], [1, D]]), yt(0, 1024), "yout%d" % (b % 2))

        if not wdescs_only:
            P.finalize()
    return nc, P, W.descs


def _tile_kc8(Wm, n0):
    blk = Wm[:, n0:n0 + 512].reshape(8, 128, 512)
    return np.ascontiguousarray(blk.transpose(1, 0, 2)).reshape(128, 4096)


def _tile_w2(Wm, oc):
    blk = Wm[:, oc * 128:(oc + 1) * 128].reshape(32, 128, 128)
    return np.ascontiguousarray(blk.transpose(1, 0, 2)).reshape(128, 4096)


def make_tiles(descs, inp):
    tiles = np.empty((len(descs), 128, 4096), np.float32)
    for i, d in enumerate(descs):
        kind, l = d[0], d[1]
        if kind == "mod":
            tiles[i] = _tile_kc8(inp["w_mod"][l], d[2] * 512)
        elif kind == "aq":
            tiles[i] = _tile_kc8(inp["attn_w_in"][l], d[2] * 512)
        elif kind == "akv":
            tiles[i] = _tile_kc8(inp["attn_w_in"][l], 1024)
        elif kind == "ao":
            tiles[i] = _tile_kc8(inp["attn_w_out"][l], d[2] * 512)
        elif kind == "dnh":
            h = d[2]
            Wm = inp["dn_w_in"][l]
            cols = np.concatenate([Wm[:, p * 1024 + h * 128: p * 1024 + (h + 1) * 128] for p in range(4)], axis=1)
            tiles[i] = _tile_kc8(cols, 0)
        elif kind == "do":
            tiles[i] = _tile_kc8(inp["dn_w_out"][l], d[2] * 512)
        elif kind == "w1":
            tiles[i] = _tile_kc8(inp["mlp_w1"][l], d[2] * 512)
        elif kind == "w2":
            tiles[i] = _tile_w2(inp["mlp_w2"][l], d[2])
        else:
            raise ValueError(d)
    return tiles


_CACHE = {}


def _get_program(nlayers=DEPTH, mixers=True):
    key = (nlayers, mixers)
    if key not in _CACHE:
        _, _, descs = build(nlayers, mixers, wdescs_only=True)
        nc, P, descs2 = build(nlayers, mixers, ntiles=len(descs))
        assert descs == descs2
        _CACHE[key] = (nc, P, descs)
    return _CACHE[key]


def _rope_tables():
    t = np.arange(NG)
    row = (t // 64).astype(np.float32)
    col = (t % 64).astype(np.float32)
    inv = (10000.0 ** (-np.arange(32, dtype=np.float32) / 32)).astype(np.float32)
    d = np.arange(128)
    pos = np.where((d // 64)[:, None] == 0, row[None, :], col[None, :]).astype(np.float32)
    ang = pos * inv[d % 32][:, None]
    return np.stack([np.cos(ang), np.sin(ang)]).astype(np.float32)


def _rott():
    R = np.zeros((128, 128), np.float32)
    for d in range(128):
        g, r = d // 64, d % 64
        if r < 32:
            R[d, g * 64 + r + 32] = -1.0
        else:
            R[d, g * 64 + r - 32] = 1.0
    return np.ascontiguousarray(R.T)


def core_inputs(inp, core, descs_tiles):
    xp, xs = inp["x_prompt"], inp["x_sample"]
    sample = core >= 6
    if not sample:
        xg = xp[4 * core:4 * core + 4].reshape(NG, D)
        condg = inp["c_ctx"]
        rope = np.stack([np.ones((128, NG), np.float32), np.zeros((128, NG), np.float32)])
        ck = np.zeros((2, 256, 256), np.float32)
        cv = np.zeros((2, 256, 256), np.float32)
        s0 = np.zeros((2, 16, 128, 128), np.float32)
        flag = 0.0
        mb = np.full((5, 4), -30000.0, np.float32)
        for q in range(4):
            mb[1 + q, q] = 0.0
    else:
        b = core - 6
        xg = xs[b]
        condg = inp["c"][b]
        rope = _rope_tables()
        ck = inp["cache_k"][b].reshape(2, 256, 256)
        cv = inp["cache_v"][b].reshape(2, 256, 256)
        mb = np.zeros((5, 4), np.float32)
        s0 = inp["state_delta"][b].reshape(2, 16, 128, 128)
        flag = 1.0
    xe = xp[24 + core]
    x_in = np.concatenate([xg, xe], axis=0)
    cond = np.stack([condg, inp["c_ctx"]], axis=0)
    cond_in = np.ascontiguousarray(cond.reshape(2, 8, 128).transpose(2, 1, 0)).reshape(128, 16)
    m = {"x_in": np.ascontiguousarray(x_in, dtype=np.float32), "cond_in": cond_in.astype(np.float32),
         "rope_in": rope, "ck_in": np.ascontiguousarray(ck, dtype=np.float32),
         "cv_in": np.ascontiguousarray(cv, dtype=np.float32),
         "maskb_in": np.ascontiguousarray(np.broadcast_to(mb.reshape(1, 20), (128, 20))),
         "s0_in": np.ascontiguousarray(s0, dtype=np.float32),
         "flags_in": np.full((128, 16), flag, np.float32)}
    for i in range((len(descs_tiles) + WCH - 1) // WCH):
        m["wts%d" % i] = descs_tiles[i * WCH:(i + 1) * WCH]
    return m


def _consts():
    c = np.zeros((128, NCONST), np.float32)
    j = np.arange(128)[:, None]
    i = np.arange(128)[None, :]
    same = (j // 64) == (i // 64)
    c[:, 0:128] = np.eye(128)
    c[:, 128:256] = np.where(same & (i >= j), 0.0, -60000.0)
    c[:, 256:384] = np.where(same & (i <= j), 0.0, -60000.0)
    c[:, 384:512] = (same & (i > j))
    c[:, 512:640] = (same & (i < j))
    c[:, 640:768] = (same & (j <= i))
    c[:, 768:896] = (same & (j >= i))
    c[:, 896:1024] = same
    c[:, 1024:1152] = (j < 64) & (i >= 0)
    c[:, 1152:1280] = (j >= 64) & (i >= 0)
    c[:, 1280:1408] = 1.0
    c[:, 1408:1536] = (j // 16) == (i // 16)
    c[:, 1536:1664] = ((j // 32) == (i // 32)) & ((j // 16) != (i // 16))
    c[:, 1664:1792] = same & ((j // 32) != (i // 32))
    return c


def shared_inputs(inp):
    bmod = np.ascontiguousarray(inp["b_mod"].reshape(DEPTH, 48, 128).transpose(0, 2, 1))
    g = np.concatenate([
        inp["norm1"].reshape(DEPTH * 8, 128).T, inp["norm2"].reshape(DEPTH * 8, 128).T,
        inp["final_norm"].reshape(8, 128).T], axis=1)
    again = np.zeros((2, 128, 256), np.float32)
    for j in range(2):
        again[j, :, 128:] = np.broadcast_to(inp["attn_q_gain"][j][:, None], (128, 128))
        again[j, :, :128] = np.broadcast_to(inp["attn_k_gain"][j][None, :], (128, 128))
    dnab = np.ascontiguousarray(
        inp["dn_w_in"][:, :, 4096:4128].reshape(2, 8, 128, 32).transpose(0, 2, 1, 3)).reshape(2, 128, 256)
    convw = np.ascontiguousarray(
        inp["dn_conv"].reshape(2, 5, 24, 128).transpose(0, 3, 2, 1)).reshape(2, 128, 120)
    dngp = np.zeros((2, 128, 33), np.float32)
    for j in range(2):
        dngp[j, :, 0:16] = np.broadcast_to(inp["dn_a_log"][j].reshape(1, 16), (128, 16))
        dngp[j, :, 16:32] = np.broadcast_to(inp["dn_dt_bias"][j].reshape(1, 16), (128, 16))
        dngp[j, :, 32] = inp["dn_norm"][j]
    return {"bmod_in": bmod.astype(np.float32), "gains_in": np.ascontiguousarray(g, dtype=np.float32),
            "consts_in": _consts(), "rott_in": _rott(), "again_in": again,
            "dnab_in": dnab.astype(np.float32), "convw_in": convw.astype(np.float32), "dngp_in": dngp}


def kernel(**inp):
    inp = {k: np.asarray(v) for k, v in inp.items()}
    nc, P, descs = _get_program()
    tiles = make_tiles(descs, inp)
    sh = shared_inputs(inp)
    in_maps = []
    for core in range(8):
        m = core_inputs(inp, core, tiles)
        m.update(sh)
        in_maps.append(m)
    res = run_bass_kernel_spmd(nc, in_maps, core_ids=list(range(8))).results
    B, S = 32, 256
    y_prompt = np.empty((B, S, D), np.float32)
    y_sample = np.empty((2, NG, D), np.float32)
    nk = np.empty((B, 2, S, 2, 128), np.float32)
    nv = np.empty((B, 2, S, 2, 128), np.float32)
    ns = np.empty((B, 2, 2, 8, 128, 128), np.float32)
    for core in range(8):
        r = res[core]
        y, ko, vo, so = r["y_out"], r["k_out"], r["v_out"], r["s_out"]
        seqs = []
        if core < 6:
            for q in range(4):
                seqs.append((4 * core + q, q))
        else:
            y_sample[core - 6] = y[:NG]
        seqs.append((24 + core, 4))
        for (b, seg) in seqs:
            y_prompt[b] = y[seg * 256:(seg + 1) * 256]
            for j in range(2):
                nk[b, j] = ko[j, seg * 256:(seg + 1) * 256].reshape(S, 2, 128)
                nv[b, j] = vo[j, seg * 256:(seg + 1) * 256].reshape(S, 2, 128)
                ns[b, j] = so[j, seg]
    return (y_prompt, y_sample, nk, nv, ns)
```

```python
import numpy as np
from contextlib import ExitStack
import concourse.bass as bass
import concourse.mybir as mybir
from concourse.bass_utils import run_bass_kernel_spmd

F32 = mybir.dt.float32
BF16 = mybir.dt.bfloat16
AF = mybir.ActivationFunctionType
ALU = mybir.AluOpType

D = 1024
NT = 1280
NG = 1024
NE = 256
DEPTH = 4
DFF = 4096
HD = 128
EPS = 1e-6
TBS = [(0, 512), (512, 512), (1024, 256)]
NSLOT = 2
DN_RATIO = 3
NCONST = 1920
WCH = 16
DBG_STAGE = 99
DBG_SUB = 99
DBG_DUMP = False
DBG_NOPRO = False
SB_BASE = 16384 + 256
SB_END = 229376
KEYG = 256


class Reg:
    __slots__ = ("ap", "keys")

    def __init__(self, ap, keys):
        self.ap = ap
        self.keys = keys


class T:
    def __init__(self, P, name, F, dtype, space="sb", addr=None, handle=None):
        self.P, self.name, self.F, self.dtype, self.space = P, name, F, dtype, space
        self.esz = 4 if dtype == F32 else 2
        self.addr = addr
        self.h = handle

    def v(self, col, dims, p0=0, n=128):
        ap = bass.AP(self.h, p0 * self.F + col, [[self.F, n]] + [list(d) for d in dims])
        span = 1 + sum((c - 1) * s for s, c in dims)
        if self.space == "sb":
            lo = self.addr + col * self.esz
            hi = self.addr + (col + span) * self.esz
            keys = range(lo // KEYG, (hi - 1) // KEYG + 1)
        elif self.space == "ps":
            keys = (self.name,)
        else:
            keys = ()
        return Reg(ap, keys)

    def __call__(self, col=0, n=None, p0=0, np_=128):
        if n is None:
            n = self.F - col
        return self.v(col, [[1, n]], p0, np_)


class Prog:
    ENG = ("pe", "act", "dve", "pool", "sp")

    def __init__(self, nc):
        self.nc = nc
        self.ops = []
        self.sb_ptr = SB_BASE
        self.dram_in = {}
        self.dram_out = {}

    def sb(self, name, F, dtype, addr=None):
        esz = 4 if dtype == F32 else 2
        if addr is None:
            addr = self.sb_ptr
            self.sb_ptr += (F * esz + 63) // 64 * 64
            assert self.sb_ptr <= SB_END, ("SBUF overflow", name, self.sb_ptr)
        h = self.nc.alloc_sbuf_tensor_at(name, [128, F], dtype, offset=addr)
        return T(self, name, F, dtype, "sb", addr, h)

    def din(self, name, shape, dtype=F32):
        h = self.nc.dram_tensor(name, list(shape), dtype, kind="ExternalInput")
        self.dram_in[name] = (tuple(shape), dtype)
        return h

    def dout(self, name, shape, dtype=F32):
        h = self.nc.dram_tensor(name, list(shape), dtype, kind="ExternalOutput")
        self.dram_out[name] = (tuple(shape), dtype)
        return h

    def add(self, eng, fn, reads, writes, dma_key=None):
        rk = []
        for r in reads:
            if r is not None and isinstance(r, Reg):
                rk.extend(r.keys)
        wk = []
        for w in writes:
            wk.extend(w.keys)
        if eng != "pe":
            for r in reads:
                if r is not None and isinstance(r, Reg) and len(r.keys) and isinstance(r.keys[0], str):
                    wk.extend(r.keys)
        self.ops.append((eng, fn, rk, wk, dma_key))

    def finalize(self):
        nc = self.nc
        with ExitStack() as es:
            sems = {e: es.enter_context(nc.semaphore("s_" + e)) for e in ("pe", "act", "dve", "pool")}
            dsem = {}
            cnt = {e: 0 for e in sems}
            dcnt = {}
            known = {e: {} for e in self.ENG}
            last_w = {}
            readers = {}
            per_eng = {e: [] for e in self.ENG}
            for (eng, fn, rk, wk, dkey) in self.ops:
                need = {}

                def req(tok, kind):
                    name, val, teng = tok
                    if teng == eng and dkey is None:
                        if eng == "pe":
                            return
                    if known[eng].get(name, 0) >= val:
                        return
                    if need.get(name, 0) < val:
                        need[name] = val

                for k in rk:
                    t = last_w.get(k)
                    if t is not None:
                        req(t, "raw")
                for k in wk:
                    t = last_w.get(k)
                    if t is not None:
                        req(t, "waw")
                    for t in readers.get(k, ()):
                        req(t, "war")
                for name, val in need.items():
                    known[eng][name] = val
                if dkey is not None:
                    if dkey not in dsem:
                        dsem[dkey] = es.enter_context(nc.semaphore("d_" + dkey))
                        dcnt[dkey] = 0
                    dcnt[dkey] += 16
                    tok = ("d_" + dkey, dcnt[dkey], "dma")
                    inc = (dsem[dkey], 16)
                else:
                    cnt[eng] += 1
                    tok = (eng, cnt[eng], eng)
                    inc = (sems[eng], 1)
                for k in wk:
                    last_w[k] = tok
                    readers[k] = []
                for k in rk:
                    lst = readers.setdefault(k, [])
                    if not lst or lst[-1] != tok:
                        lst.append(tok)
                        if len(lst) > 24:
                            best = {}
                            for (a, b, c) in lst:
                                if best.get(a, (0, None))[0] < b:
                                    best[a] = (b, c)
                            lst[:] = [(a, b, c) for a, (b, c) in best.items()]
                per_eng[eng].append((fn, list(need.items()), inc))
            allsem = dict(sems)
            for k, s in dsem.items():
                allsem["d_" + k] = s
            final_waits = [(("d_" + k), v) for k, v in dcnt.items()]
            self.n_ops = {e: len(v) for e, v in per_eng.items()}
            with nc.Block() as block:
                def runner(ename):
                    def body(e):
                        for fn, waits, inc in per_eng[ename]:
                            for sname, val in waits:
                                e.wait_ge(allsem[sname], val)
                            ins = fn(e)
                            ins.then_inc(inc[0], inc[1])
                        if ename == "sp":
                            for sname, val in final_waits:
                                e.wait_ge(allsem[sname], val)
                            for en2 in ("pe", "act", "dve", "pool"):
                                if cnt[en2]:
                                    e.wait_ge(sems[en2], cnt[en2])
                    return body
                block.tensor(runner("pe"))
                block.scalar(runner("act"))
                block.vector(runner("dve"))
                block.gpsimd(runner("pool"))
                block.sync(runner("sp"))

    def mm(self, out, lhsT, rhs, start=True, stop=True):
        self.add("pe", lambda e: e.matmul(out.ap, lhsT.ap, rhs.ap, start=start, stop=stop),
                 [lhsT, rhs] + ([] if start else [out]), [out])

    def tr(self, out, in_, ident):
        self.add("pe", lambda e: e.transpose(out.ap, in_.ap, ident.ap), [in_, ident], [out])

    def act(self, out, in_, func, bias=0.0, scale=1.0, accum=None):
        b = bias.ap if isinstance(bias, Reg) else bias
        s = scale.ap if isinstance(scale, Reg) else scale
        if accum is None:
            fn = lambda e: e.activation(out.ap, in_.ap, func, bias=b, scale=s)
            w = [out]
        else:
            fn = lambda e: e.activation(out.ap, in_.ap, func, bias=b, scale=s, accum_out=accum.ap)
            w = [out, accum]
        self.add("act", fn, [in_, bias, scale], w)

    def tt(self, eng, out, in0, in1, op):
        self.add(eng, lambda e: e.tensor_tensor(out.ap, in0.ap, in1.ap, op), [in0, in1], [out])

    def ts(self, eng, out, in0, s1, op0, s2=None, op1=None):
        a = s1.ap if isinstance(s1, Reg) else s1
        b = s2.ap if isinstance(s2, Reg) else s2
        if op1 is None:
            fn = lambda e: e.tensor_scalar(out.ap, in0.ap, a, None, op0)
        else:
            fn = lambda e: e.tensor_scalar(out.ap, in0.ap, a, b, op0, op1)
        self.add(eng, fn, [in0, s1, s2], [out])

    def stt(self, eng, out, in0, sc, in1, op0, op1):
        a = sc.ap if isinstance(sc, Reg) else sc
        self.add(eng, lambda e: e.scalar_tensor_tensor(out.ap, in0.ap, a, in1.ap, op0, op1),
                 [in0, sc, in1], [out])

    def cp(self, eng, out, in_):
        if eng == "act":
            self.add("act", lambda e: e.copy(out.ap, in_.ap), [in_], [out])
        else:
            self.add(eng, lambda e: e.tensor_copy(out.ap, in_.ap), [in_], [out])

    def memset(self, eng, out, val):
        self.add(eng, lambda e: e.memset(out.ap, val), [], [out])

    def dma(self, q, out, in_, key):
        self.add(q, lambda e: e.dma_start(out=out.ap, in_=in_.ap), [in_], [out], dma_key=key)


def dreg(h, off, dims):
    return Reg(bass.AP(h, off, [list(d) for d in dims]), ())


class WStream:
    def __init__(self, P, ntiles_hint=None):
        self.P = P
        self.slots = [P.sb("wslot%d" % i, 4096, BF16) for i in range(NSLOT)]
        self.descs = []
        self.issued = 0
        self.h = None

    def bind(self, h, ntiles):
        self.h = h
        self.ntiles = ntiles

    def _issue(self):
        t = self.issued
        if self.h is not None and t < self.ntiles:
            s = self.slots[t % NSLOT]
            self.P.dma("pool", s(0, 4096), dreg(self.h[t // WCH], (t % WCH) * 128 * 4096, [[4096, 128], [1, 4096]]),
                       key="w%d" % (t % NSLOT))
        self.issued += 1

    def start(self):
        for _ in range(NSLOT):
            self._issue()

    def acquire(self, desc):
        t = len(self.descs)
        self.descs.append(desc)
        return self.slots[t % NSLOT]

    def release(self):
        self._issue()


def build(nlayers=DEPTH, mixers=True, wdescs_only=False, ntiles=0, dbg=None):
    nc = bass.Bass("TRN2", target_bir_lowering=False)
    P = Prog(nc)
    x_in = P.din("x_in", [NT, D])
    cond_in = P.din("cond_in", [128, 16])
    bmod_in = P.din("bmod_in", [DEPTH, 128, 48])
    gains_in = P.din("gains_in", [128, 72])
    consts_in = P.din("consts_in", [128, NCONST])
    flags_in = P.din("flags_in", [128, 16])
    if nlayers >= 2 and mixers:
        dnab_in = P.din("dnab_in", [2, 128, 256])
        convw_in = P.din("convw_in", [2, 128, 120])
        dngp_in = P.din("dngp_in", [2, 128, 33])
        s0_in = P.din("s0_in", [2, 16, 128, 128])
        s_out = P.dout("s_out", [2, 5, 2, 8, 128, 128])
    dbg_out = P.dout("dbg_out", [8, 128, NT]) if DBG_DUMP else None
    wts = [P.din("wts%d" % i, [min(WCH, max(ntiles, 1) - i * WCH), 128, 4096])
           for i in range((max(ntiles, 1) + WCH - 1) // WCH)]
    y_out = P.dout("y_out", [NT, D])
    rope_in = P.din("rope_in", [2, 128, NG])
    rott_in = P.din("rott_in", [128, 128])
    ck_in = P.din("ck_in", [2, 256, 256])
    cv_in = P.din("cv_in", [2, 256, 256])
    maskb_in = P.din("maskb_in", [128, 20])
    again_in = P.din("again_in", [2, 128, 256])
    k_out = P.dout("k_out", [2, NT, 256])
    v_out = P.dout("v_out", [2, NT, 256])

    W = WStream(P)
    W.bind(wts, ntiles)

    X = P.sb("X", 8 * NT, F32)
    H = P.sb("H", 8 * NT, BF16)
    CONST = P.sb("CONST", NCONST, F32)
    IDENT = P.sb("IDENT", 128, F32)
    FLG = P.sb("FLG", 16, F32)
    ONESB = P.sb("ONESB", 128, BF16)
    IDENTB = P.sb("IDENTB", 128, BF16)
    EPSB = P.sb("EPSB", 4, F32)
    ROTT = P.sb("ROTT", 128, BF16)
    CONDT = P.sb("CONDT", 16, F32)
    SC = P.sb("SC", 16, BF16)
    GAINS = P.sb("GAINS", 72, F32)
    BMOD = P.sb("BMOD", 48, F32)
    MODS = [P.sb("MOD%d" % i, 96, F32) for i in range(2)]
    AB = [P.sb("AB%d" % i, 32, F32) for i in range(2)]
    scr0 = P.sb_ptr
    SCR_SZ = SB_END - scr0
    assert SCR_SZ >= 102400, SCR_SZ

    def scr(name, F, dtype, off):
        esz = 4 if dtype == F32 else 2
        assert off + F * esz <= SCR_SZ, (name, off, F * esz, SCR_SZ)
        return P.sb(name, F, dtype, addr=scr0 + off)

    HID = scr("HID", 32 * NT, BF16, 0)
    RSTD = scr("RSTD", NT, F32, 81920)
    SQ = [scr("SQ%d" % i, NT, BF16, 81920 + 5120 + i * 2560) for i in range(2)]
    TMPN = [scr("TMPN%d" % i, NT, F32, 81920 + 10240 + i * 5120) for i in range(2)]
    XTOK = [scr("XTOK%d" % i, 1024, F32, i * 4096) for i in range(2)]
    O = scr("O", 8 * NT, BF16, 81920)
    QT = scr("QT", 8 * NT, BF16, 0)
    KT = scr("KT", 2 * 1536, BF16, 20480)
    VT = scr("VT", 12 * 256, BF16, 26624)
    COS = scr("COS", NG, F32, 32768)
    SIN = scr("SIN", NG, F32, 36864)
    QRAW = [scr("QRAW%d" % i, NT, F32, 40960 + i * 5120) for i in range(2)]
    QN = scr("QN", NT, F32, 51200)
    QNB = scr("QNB", NT, BF16, 56320)
    T1 = [scr("T1_%d" % i, 512, F32, 58880 + i * 2048) for i in range(2)]
    T2 = [scr("T2_%d" % i, 512, F32, 62976 + i * 2048) for i in range(2)]
    PT = [scr("PT%d" % i, 512, BF16, 67072 + i * 1024) for i in range(4)]
    KTOK = [scr("KTOK%d" % i, 256, F32, 71168 + i * 1024) for i in range(2)]
    VSTG = [scr("VSTG%d" % i, 256, F32, 73216 + i * 1024) for i in range(2)]
    CKV = scr("CKV", 512, F32, 75264)
    MASKB = scr("MASKB", 32, F32, 77312)
    AGN = scr("AGN", 129, F32, 77440)
    KSS = scr("KSS", 16, F32, 78016)
    REC = scr("REC", 512, F32, 78080)
    AGN2 = scr("AGN2", 256, F32, 102400)
    KGB = AGN2
    KNB = scr("KNB", 256, BF16, 62976)
    KNF = scr("KNF", 256, F32, 62976 + 512)
    if nlayers >= 2 and mixers:
        _o = [0]

        def dscr(name, F, dtype):
            esz = 4 if dtype == F32 else 2
            t = scr(name, F, dtype, _o[0])
            _o[0] += (F * esz + 63) // 64 * 64
            assert _o[0] <= 81920, (name, _o[0])
            P.dscr_used = _o[0]
            return t
        D_NB = [dscr("D_NB%d" % i, 512, F32) for i in range(10)]
        D_FT = P.sb("D_FT", NT, F32, addr=D_NB[5].addr)
        D_G = P.sb("D_G", 160, F32, addr=D_NB[9].addr)
        D_TG = P.sb("D_TG", 160, F32, addr=D_NB[9].addr + 640)
        D_GLO = P.sb("D_GLO", 160, F32, addr=D_NB[9].addr + 1280)
        D_QNB = dscr("D_QNB", NT, BF16)
        D_KNBS = [dscr("D_KNB%d" % i, NT, BF16) for i in range(2)]
        D_VB = dscr("D_VB", NT, BF16)
        D_ZS = [dscr("D_ZS%d" % i, NT, BF16) for i in range(2)]
        D_PAD = dscr("D_PAD", 5 * 260, BF16)
        D_DG = dscr("D_DG", 5 * 128, BF16)
        D_SQ = P.sb("D_SQ", NT, BF16, addr=D_VB.addr)
        D_SQF = P.sb("D_SQF", 512, BF16, addr=D_NB[4].addr)
        D_RSQ = P.sb("D_RSQ", 512, F32, addr=D_NB[8].addr)
        _top = 103424
        D_VTOKS = [dscr("D_VTOK0", NT, BF16), dscr("D_VTOK1", NT, BF16)]
        D_KDECS = [dscr("D_KDEC0", 2 * NT, BF16), scr("D_KDEC1", 2 * NT, BF16, _top)]
        D_XBS = [dscr("D_XB0", 5 * 512, BF16), scr("D_XB1", 5 * 512, BF16, _top + 5120)]
        D_QKMS = [dscr("D_QKM0", 5 * 512, BF16), scr("D_QKM1", 5 * 512, BF16, _top + 10240)]
        D_QGTS = [dscr("D_QGT0", 2 * NT, BF16), scr("D_QGT1", 2 * NT, BF16, _top + 15360)]
        D_OD = dscr("D_OD", 2 * NT, BF16)
        D_SF = [dscr("D_SF%d" % i, 128, F32) for i in range(2)]
        D_SB = [dscr("D_SB%d" % i, 128, BF16) for i in range(2)]
        D_RB = [dscr("D_RB%d" % i, 128, BF16) for i in range(2)]
        D_VNB = [dscr("D_VNB%d" % i, 128, BF16) for i in range(2)]
        D_STG = [dscr("D_STG%d" % i, 128, F32) for i in range(2)]
        D_GC = dscr("D_GC", 160, F32)
        D_NEGGC = dscr("D_NEGGC", 160, F32)
        D_NEGEGC = dscr("D_NEGEGC", 160, F32)
        D_KD = dscr("D_KD", 160, F32)
        D_BETA = dscr("D_BETA", 160, F32)
        D_NEGBETA = dscr("D_NEGBETA", 160, F32)
        D_EGL = dscr("D_EGL", 320, F32)
        D_GP = dscr("D_GP", 33, F32)
        D_EA = dscr("D_EA", 16, F32)
        D_WAB = dscr("D_WAB", 256, BF16)
        D_CW = dscr("D_CW", 120, F32)


    with ExitStack() as es:
        PS = []
        for i in range(8):
            h = es.enter_context(nc.psum_tensor("ps%d" % i, [128, 512], F32))
            PS.append(T(P, "ps%d" % i, 512, F32, "ps", None, h))

        P.dma("sp", IDENT(0, 128), dreg(consts_in, 0, [[NCONST, 128], [1, 128]]), "cI")
        if not DBG_NOPRO:
            for ci in range(0, NCONST, 512):
                cw = min(512, NCONST - ci)
                P.dma("sp", CONST(ci, cw), dreg(consts_in, ci, [[NCONST, 128], [1, cw]]), "cI%d" % ci)
        if not DBG_NOPRO:
            P.dma("sp", FLG(0, 16), dreg(flags_in, 0, [[16, 128], [1, 16]]), "cF")
            P.memset("dve", EPSB(3, 1), 1.0)
            P.cp("act", IDENTB(0, 128), CONST(0, 128))
        P.dma("sp", CONDT(0, 16), dreg(cond_in, 0, [[16, 128], [1, 16]]), "cC")
        P.dma("sp", GAINS(0, 72), dreg(gains_in, 0, [[72, 128], [1, 72]]), "cG")
        P.memset("dve", ONESB(0, 128), 1.0)
        P.dma("pool", ROTT(0, 128), dreg(rott_in, 0, [[128, 128], [1, 128]]), "c1")
        P.memset("dve", EPSB(0, 1), float(D * EPS))
        P.memset("dve", EPSB(1, 1), float(HD * EPS))
        P.memset("dve", EPSB(2, 1), float(EPS))
        P.act(SC(0, 16), CONDT(0, 16), AF.Silu)
        W.start()

        for b in range(10):
            xt = XTOK[b % 2]
            P.dma("sp", xt(0, 1024), dreg(x_in, b * 128 * D, [[D, 128], [1, D]]), "xin%d" % (b % 2))
            for half in range(2):
                ps = PS[(2 * b + half) % 4]
                for j in range(4):
                    c = half * 4 + j
                    P.tr(ps(j * 128, 128), xt(c * 128, 128), IDENT(0, 128))
                P.cp("dve" if half == 0 else "act",
                     X.v(half * 4 * NT + b * 128, [[NT, 4], [1, 128]]),
                     ps.v(0, [[128, 4], [1, 128]]))

        def modulation(l, par):
            MOD = MODS[par]
            P.dma("sp", BMOD(0, 48), dreg(bmod_in, l * 128 * 48, [[48, 128], [1, 48]]), "bmod")
            ps = PS[7]
            for t in range(12):
                slot = W.acquire(("mod", l, t))
                for j in range(4):
                    f = t * 4 + j
                    for kc in range(8):
                        P.mm(ps(f * 2, 2), slot(kc * 512 + j * 128, 128), SC(kc * 2, 2),
                             start=(kc == 0), stop=(kc == 7))
                W.release()
                yield
            P.tt("dve", MOD.v(0, [[2, 48], [1, 2]]), ps.v(0, [[2, 48], [1, 2]]),
                 BMOD.v(0, [[1, 48], [0, 2]]), ALU.add)
            A = AB[par]
            for which, (m_scale, goff) in enumerate(((1, l * 8), (4, 32 + l * 8))):
                P.ts("dve", A.v(which * 16, [[2, 8], [1, 2]]), MOD.v(m_scale * 16, [[2, 8], [1, 2]]),
                     1.0, ALU.add, 32.0, ALU.mult)
                P.tt("dve", A.v(which * 16, [[2, 8], [1, 2]]), A.v(which * 16, [[2, 8], [1, 2]]),
                     GAINS.v(goff, [[1, 8], [0, 2]]), ALU.mult)
            yield

        def run_all(gen):
            for _ in gen:
                pass

        def rms_rstd(src_chunk):
            for c in range(8):
                sq = SQ[c % 2]
                P.act(sq(0, NT), src_chunk(c), AF.Square)
                for ti, (t0, tn) in enumerate(TBS):
                    P.mm(PS[4 + ti](0, tn), ONESB(0, 128), sq(t0, tn), start=(c == 0), stop=(c == 7))
            for ti, (t0, tn) in enumerate(TBS):
                P.act(RSTD(t0, tn), PS[4 + ti](0, tn), AF.Sqrt, bias=EPSB(0, 1))
                P.add("dve", (lambda o, i: (lambda e: e.reciprocal(o.ap, i.ap)))(RSTD(t0, tn), RSTD(t0, tn)),
                      [RSTD(t0, tn)], [RSTD(t0, tn)])

        def norm_mod(par, which, m_shift):
            MOD = MODS[par]
            A = AB[par]
            rms_rstd(lambda c: X(c * NT, NT))
            for c in range(8):
                tmp = TMPN[c % 2]
                P.tt("dve" if c % 2 == 0 else "pool", tmp(0, NT), X(c * NT, NT), RSTD(0, NT), ALU.mult)
                for (t0, tn, cd) in ((0, NG, 0), (NG, NE, 1)):
                    P.act(H(c * NT + t0, tn), tmp(t0, tn), AF.Identity,
                          bias=MOD((m_shift * 8 + c) * 2 + cd, 1),
                          scale=A(which * 16 + c * 2 + cd, 1))

        def gated_residual(par, m_gate, c, pss):
            MOD = MODS[par]
            for ti, (t0, tn) in enumerate(TBS):
                cd = 0 if t0 < NG else 1
                P.stt("dve", X(c * NT + t0, tn), pss[ti](0, tn), MOD((m_gate * 8 + c) * 2 + cd, 1),
                      X(c * NT + t0, tn), ALU.mult, ALU.add)

        def mlp(l, par, side_gen):
            norm_mod(par, 1, 3)
            RT = TMPN
            for t in range(8):
                slot = W.acquire(("w1", l, t))
                for j in range(4):
                    oc = t * 4 + j
                    base = (oc % 2) * 3
                    for kc in range(8):
                        for ti, (t0, tn) in enumerate(TBS):
                            P.mm(PS[base + ti](0, tn), slot(kc * 512 + j * 128, 128), H(kc * NT + t0, tn),
                                 start=(kc == 0), stop=(kc == 7))
                    rt = RT[oc % 2]
                    for ti, (t0, tn) in enumerate(TBS):
                        P.act(rt(t0, tn), PS[base + ti](0, tn), AF.Relu)
                    P.tt("pool", HID(oc * NT, NT), rt(0, NT), rt(0, NT), ALU.mult)
                W.release()
                if side_gen is not None:
                    next(side_gen, None)
            for oc in range(8):
                slot = W.acquire(("w2", l, oc))
                base = (oc % 2) * 3
                for kc in range(32):
                    for ti, (t0, tn) in enumerate(TBS):
                        P.mm(PS[base + ti](0, tn), slot(kc * 128, 128), HID(kc * NT + t0, tn),
                             start=(kc == 0), stop=(kc == 31))
                gated_residual(par, 5, oc, [PS[base + ti] for ti in range(3)])
                W.release()
                if side_gen is not None:
                    next(side_gen, None)
            if side_gen is not None:
                run_all(side_gen)


        def attention(l, par):
            j = l // 2
            norm_mod(par, 0, 0)
            P.dma("sp", COS(0, NG), dreg(rope_in, 0, [[NG, 128], [1, NG]]), "aCOS")
            P.dma("sp", SIN(0, NG), dreg(rope_in, 128 * NG, [[NG, 128], [1, NG]]), "aSIN")
            P.dma("sp", MASKB(0, 20), dreg(maskb_in, 0, [[20, 128], [1, 20]]), "aMB")
            P.dma("sp", AGN2(0, 256), dreg(again_in, j * 128 * 256, [[256, 128], [1, 256]]), "aAG")
            P.dma("sp", CKV.v(0, [[256, 2], [1, 256]]),
                  dreg(ck_in, j * 256 * 256, [[256, 128], [128 * 256, 2], [1, 256]]), "aCK")
            P.dma("pool", VT.v(0, [[256, 2], [1, 256]]),
                  dreg(cv_in, j * 256 * 256, [[256, 128], [128 * 256, 2], [1, 256]]), "acv")
            if DBG_STAGE == 0:
                return
            for kc in range(2):
                ps = PS[6 + kc]
                for kv in range(2):
                    P.tr(ps(kv * 128, 128), CKV(kc * 256 + kv * 128, 128), IDENT(0, 128))
                P.cp("act", KT.v(kc * 128, [[1536, 2], [1, 128]]), ps.v(0, [[128, 2], [1, 128]]))
            if DBG_STAGE == 1:
                return
            for t in range(2):
                slot = W.acquire(("aq", j, t))
                for jj in range(4):
                    h = t * 4 + jj
                    base = (h % 2) * 3
                    for kc in range(8):
                        for ti, (t0, tn) in enumerate(TBS):
                            P.mm(PS[base + ti](0, tn), slot(kc * 512 + jj * 128, 128), H(kc * NT + t0, tn),
                                 start=(kc == 0), stop=(kc == 7))
                    qraw = QRAW[h % 2]
                    sq = SQ[h % 2]
                    for ti, (t0, tn) in enumerate(TBS):
                        P.cp("dve", qraw(t0, tn), PS[base + ti](0, tn))
                        P.act(sq(t0, tn), qraw(t0, tn), AF.Square)
                    for ti, (t0, tn) in enumerate(TBS):
                        if DBG_SUB == 0:
                            break
                        pb = PS[6 + ti % 2]
                        P.mm(pb(0, tn), ONESB(0, 128), sq(t0, tn))
                        P.act(REC(0, tn), pb(0, tn), AF.Sqrt, bias=EPSB(2, 1), scale=1.0 / HD)
                        P.add("dve", (lambda o: (lambda e: e.reciprocal(o.ap, o.ap)))(REC(0, tn)),
                              [REC(0, tn)], [REC(0, tn)])
                        if DBG_SUB == 1:
                            continue
                        P.stt("dve", QN(t0, tn), qraw(t0, tn), AGN2(128, 1), REC(0, tn), ALU.mult, ALU.mult)
                        if DBG_SUB == 2:
                            continue
                        if t0 >= NG:
                            P.cp("act", QT(h * NT + t0, tn), QN(t0, tn))
                        else:
                            P.cp("act", QNB(t0, tn), QN(t0, tn))
                            P.mm(pb(0, tn), ROTT(0, 128), QNB(t0, tn))
                            if DBG_SUB == 3:
                                continue
                            P.tt("dve", T1[ti](0, tn), pb(0, tn), SIN(t0, tn), ALU.mult)
                            P.tt("pool", T2[ti](0, tn), QN(t0, tn), COS(t0, tn), ALU.mult)
                            P.tt("pool", QT(h * NT + t0, tn), T1[ti](0, tn), T2[ti](0, tn), ALU.add)
                W.release()
            if DBG_STAGE == 2:
                return
            P.memset("dve", KSS(0, 16), 1.0)
            slot = W.acquire(("akv", j))
            for b in range(10):
                ps = PS[b % 4]
                for kc in range(8):
                    P.mm(ps(0, 512), H(kc * NT + b * 128, 128), slot(kc * 512, 512),
                         start=(kc == 0), stop=(kc == 7))
                ktok = KTOK[b % 2]
                vstg = VSTG[b % 2]
                if DBG_SUB == 10:
                    continue
                P.act(T1[0](0, 256), ps(0, 256), AF.Square)
                P.add("dve", (lambda o, i: (lambda e: e.tensor_reduce(o.ap, i.ap, mybir.AxisListType.X, ALU.add)))(
                    KSS(0, 2), T1[0].v(0, [[128, 2], [1, 128]])), [T1[0](0, 256)], [KSS(0, 2)])
                P.act(KSS(8, 8), KSS(0, 8), AF.Sqrt, bias=EPSB(2, 1), scale=1.0 / HD)
                P.add("dve", (lambda o: (lambda e: e.reciprocal(o.ap, o.ap)))(KSS(8, 8)), [KSS(8, 8)], [KSS(8, 8)])
                for kv in range(2):
                    P.stt("dve", ktok(kv * 128, 128), ps(kv * 128, 128), KSS(8 + kv, 1), KGB(0, 128),
                          ALU.mult, ALU.mult)
                if DBG_SUB == 11:
                    continue
                P.dma("sp", dreg(k_out, (j * NT + b * 128) * 256, [[256, 128], [1, 256]]), ktok(0, 256),
                      "ko%d" % (b % 2))
                if DBG_SUB == 12:
                    continue
                P.cp("act", VT((2 + b) * 256, 256), ps(256, 256))
                P.cp("dve", vstg(0, 256), ps(256, 256))
                P.dma("sp", dreg(v_out, (j * NT + b * 128) * 256, [[256, 128], [1, 256]]), vstg(0, 256),
                      "vo%d" % (b % 2))
                if DBG_SUB == 13:
                    continue
                pt = PS[4 + b % 2]
                for kv in range(2):
                    P.tr(pt(kv * 128, 128), ktok(kv * 128, 128), IDENT(0, 128))
                kdst = KT.v(256 + b * 128, [[1536, 2], [1, 128]])
                if DBG_SUB == 14:
                    continue
                if b >= 8:
                    if DBG_SUB not in (152, 153):
                        P.cp("act", kdst, pt.v(0, [[128, 2], [1, 128]]))
                else:
                    if DBG_SUB not in (151, 153):
                        P.cp("act", KNB(0, 256), pt(0, 256))
                    if DBG_SUB not in (151, 152):
                        P.cp("dve", KNF(0, 256), pt(0, 256))
                    if DBG_SUB in (15, 151, 152, 153):
                        continue
                    pr = PS[6 + b % 2]
                    P.mm(pr(0, 256), ROTT(0, 128), KNB(0, 256))
                    if DBG_SUB == 16:
                        continue
                    P.tt("dve", T1[1].v(0, [[128, 2], [1, 128]]), pr.v(0, [[128, 2], [1, 128]]),
                         SIN.v(b * 128, [[0, 2], [1, 128]]), ALU.mult)
                    P.tt("dve", T2[1].v(0, [[128, 2], [1, 128]]), KNF.v(0, [[128, 2], [1, 128]]),
                         COS.v(b * 128, [[0, 2], [1, 128]]), ALU.mult)
                    P.tt("dve", kdst, T1[1].v(0, [[128, 2], [1, 128]]), T2[1].v(0, [[128, 2], [1, 128]]), ALU.add)
            W.release()
            if DBG_STAGE == 3:
                return
            it = 0
            for h in range(8):
                kv = h // 4
                for (q0, qn, chunks) in ((0, 512, list(range(10))), (512, 512, list(range(10))),
                                         (NG, NE, [10, 11])):
                    po = PS[2 + it % 2]
                    pd = PS[4 + it % 2]
                    it += 1
                    for ci, kc in enumerate(chunks):
                        pS = PS[ci % 2]
                        ptile = PT[ci % 4]
                        P.mm(pS(0, qn), KT(kv * 1536 + kc * 128, 128), QT(h * NT + q0, qn))
                        if q0 >= NG:
                            P.act(ptile(0, qn), pS(0, qn), AF.Exp, scale=float(HD ** -0.5))
                        else:
                            kb = 0 if kc < 2 else 1 + (kc - 2) // 2
                            for hf in range(2):
                                qblk = q0 // 256 + hf
                                P.act(ptile(hf * 256, 256), pS(hf * 256, 256), AF.Exp,
                                      bias=MASKB(kb * 4 + qblk, 1), scale=float(HD ** -0.5))
                        first, last = (ci == 0), (ci == len(chunks) - 1)
                        P.mm(po(0, qn), VT(kc * 256 + kv * 128, 128), ptile(0, qn), start=first, stop=last)
                        P.mm(pd(0, qn), ONESB(0, 128), ptile(0, qn), start=first, stop=last)
                    P.add("dve", (lambda o, i: (lambda e: e.reciprocal(o.ap, i.ap)))(REC(0, qn), pd(0, qn)),
                          [pd(0, qn)], [REC(0, qn)])
                    P.tt("dve", O(h * NT + q0, qn), po(0, qn), REC(0, qn), ALU.mult)
            if DBG_STAGE == 4:
                return
            for t in range(2):
                slot = W.acquire(("ao", j, t))
                for jj in range(4):
                    oc = t * 4 + jj
                    base = (oc % 2) * 3
                    for kc in range(8):
                        for ti, (t0, tn) in enumerate(TBS):
                            P.mm(PS[base + ti](0, tn), slot(kc * 512 + jj * 128, 128), O(kc * NT + t0, tn),
                                 start=(kc == 0), stop=(kc == 7))
                    gated_residual(par, 2, oc, [PS[base + ti] for ti in range(3)])
                W.release()


        def deltanet(l, par):
            j = l // 2
            norm_mod(par, 0, 0)
            C_MADD, C_MS, C_UF, C_UB, C_SAME, C_IND0, C_ONESF = 128, 384, 640, 768, 896, 1024, 1280
            C_BD16, C_OFF1, C_OFF2, C_OFF3 = 1408, 1536, 1664, 1792
            P.dma("pool", D_WAB(0, 256), dreg(dnab_in, j * 128 * 256, [[256, 128], [1, 256]]), "dWAB")
            P.dma("sp", D_CW(0, 120), dreg(convw_in, j * 128 * 120, [[120, 128], [1, 120]]), "dCW")
            P.dma("sp", D_GP(0, 33), dreg(dngp_in, j * 128 * 33, [[33, 128], [1, 33]]), "dGP")
            P.memset("pool", D_PAD(0, 5 * 260), 0.0)
            pab, pgc, pgl, peg = PS[0], PS[1], PS[2], PS[3]
            for b in range(10):
                for kc in range(8):
                    P.mm(pab(b * 32, 32), H(kc * NT + b * 128, 128), D_WAB(kc * 32, 32),
                         start=(kc == 0), stop=(kc == 7))
            v3 = lambda t, c0=0, w=16, st=16: t.v(c0, [[st, 10], [1, w]])
            P.tt("dve", v3(D_TG), pab.v(0, [[32, 10], [1, 16]]), D_GP.v(16, [[0, 10], [1, 16]]), ALU.add)
            SPY, SPZ, SPP = D_GLO, D_KD, D_NEGGC
            P.ts("dve", SPY(0, 160), D_TG(0, 160), -1.0, ALU.mult)
            P.tt("dve", SPY(0, 160), SPY(0, 160), D_TG(0, 160), ALU.max)
            P.act(SPY(0, 160), SPY(0, 160), AF.Exp, scale=-1.0)
            P.ts("dve", SPZ(0, 160), SPY(0, 160), 2.0, ALU.add)
            P.add("dve", (lambda o: (lambda e: e.reciprocal(o.ap, o.ap)))(SPZ(0, 160)), [SPZ(0, 160)], [SPZ(0, 160)])
            P.tt("dve", SPZ(0, 160), SPY(0, 160), SPZ(0, 160), ALU.mult)
            P.tt("dve", SPY(0, 160), SPZ(0, 160), SPZ(0, 160), ALU.mult)
            P.ts("dve", SPP(0, 160), SPY(0, 160), 1.0 / 9.0, ALU.mult, 1.0 / 7.0, ALU.add)
            for cst in (1.0 / 5.0, 1.0 / 3.0, 1.0):
                P.tt("dve", SPP(0, 160), SPP(0, 160), SPY(0, 160), ALU.mult)
                P.ts("dve", SPP(0, 160), SPP(0, 160), cst, ALU.add)
            P.tt("dve", SPP(0, 160), SPP(0, 160), SPZ(0, 160), ALU.mult)
            P.ts("dve", D_TG(0, 160), D_TG(0, 160), 0.0, ALU.max)
            P.stt("dve", D_TG(0, 160), SPP(0, 160), 2.0, D_TG(0, 160), ALU.mult, ALU.add)
            P.act(D_EA(0, 16), D_GP(0, 16), AF.Exp)
            P.stt("dve", v3(D_G), v3(D_TG), -1.0, D_EA.v(0, [[0, 10], [1, 16]]), ALU.mult, ALU.mult)
            P.act(v3(D_BETA), pab.v(16, [[32, 10], [1, 16]]), AF.Sigmoid)
            P.ts("pool", D_NEGBETA(0, 160), D_BETA(0, 160), -1.0, ALU.mult)
            for b in range(10):
                P.mm(pgc(b * 16, 8), CONST(C_UF, 128), D_G(b * 16, 8))
                P.mm(pgc(b * 16 + 8, 8), CONST(C_UB, 128), D_G(b * 16 + 8, 8))
                P.mm(pgl(b * 16, 16), CONST(C_SAME, 128), D_G(b * 16, 16))
                for c in range(2):
                    P.mm(peg(b * 32 + c * 16, 16), CONST(C_IND0 + c * 128, 128), D_G(b * 16, 16))
            P.cp("dve", D_GC(0, 160), pgc(0, 160))
            P.ts("dve", D_NEGGC(0, 160), pgc(0, 160), -1.0, ALU.mult)
            P.act(D_NEGEGC(0, 160), pgc(0, 160), AF.Exp)
            P.ts("pool", D_NEGEGC(0, 160), D_NEGEGC(0, 160), -1.0, ALU.mult)
            P.tt("dve", D_KD(0, 160), pgl(0, 160), D_GC(0, 160), ALU.subtract)
            P.act(D_KD(0, 160), D_KD(0, 160), AF.Exp)
            P.act(D_EGL(0, 320), peg(0, 320), AF.Exp)

            def stageA(h):
                slot = W.acquire(("dnh", j, h))
                for part in range(4):
                    for kc in range(8):
                        for ti, (t0, tn) in enumerate(TBS):
                            P.mm(PS[ti](0, tn), slot(kc * 512 + part * 128, 128), H(kc * NT + t0, tn),
                                 start=(kc == 0), stop=(kc == 7))
                    yield
                    if part == 3:
                        for ti, (t0, tn) in enumerate(TBS):
                            P.act(D_ZS[h % 2](t0, tn), PS[ti](0, tn), AF.Silu)
                        yield
                        continue
                    ch = part * 8 + h
                    for tap in range(5):
                        P.ts("pool", D_DG(tap * 128, 128), IDENT(0, 128), D_CW(ch * 5 + tap, 1), ALU.mult)
                    for ti, (t0, tn) in enumerate(TBS):
                        ns = tn // 256
                        P.cp("act" if ti != 1 else "dve", D_PAD.v((t0 // 256) * 260 + 2, [[260, ns], [1, 256]]),
                             PS[ti].v(0, [[256, ns], [1, 256]]))
                    P.ts("dve", D_PAD.v(260, [[260, 3], [1, 2]]), D_PAD.v(256, [[260, 3], [1, 2]]), FLG(0, 1), ALU.mult)
                    P.ts("dve", D_PAD.v(258, [[260, 3], [1, 2]]), D_PAD.v(262, [[260, 3], [1, 2]]), FLG(0, 1), ALU.mult)
                    for ti, (t0, tn) in enumerate(TBS):
                        ns = tn // 256
                        for tap in range(5):
                            P.mm(PS[3 + ti](0, tn), D_DG(tap * 128, 128),
                                 D_PAD.v((t0 // 256) * 260 + tap, [[260, ns], [1, 256]]),
                                 start=(tap == 0), stop=(tap == 4))
                    yield
                    dst = D_VB if part == 2 else D_FT
                    for ti, (t0, tn) in enumerate(TBS):
                        P.act(dst(t0, tn), PS[3 + ti](0, tn), AF.Silu)
                    if part == 2:
                        yield
                        continue
                    P.act(D_SQ(0, NT), D_FT(0, NT), AF.Square)
                    out = D_QNB if part == 0 else D_KNBS[h % 2]
                    for ti, (t0, tn) in enumerate(TBS):
                        P.mm(PS[3 + ti](0, tn), ONESB(0, 128), D_SQ(t0, tn))
                        P.act(D_RSQ(0, tn), PS[3 + ti](0, tn), AF.Sqrt, bias=EPSB(2, 1))
                        P.add("dve", (lambda o: (lambda e: e.reciprocal(o.ap, o.ap)))(D_RSQ(0, tn)),
                              [D_RSQ(0, tn)], [D_RSQ(0, tn)])
                        P.stt("dve", out(t0, tn), D_FT(t0, tn), float(HD ** -0.5) if part == 0 else 1.0,
                              D_RSQ(0, tn), ALU.mult, ALU.mult)
                    yield
                W.release()

            def stageB(h):
                D_KNB = D_KNBS[h % 2]
                D_VTOK, D_KDEC, D_XB, D_QKM, D_QGT = (D_VTOKS[h % 2], D_KDECS[h % 2], D_XBS[h % 2],
                                                      D_QKMS[h % 2], D_QGTS[h % 2])
                for b in range(10):
                    ps = PS[b % 2]
                    P.mm(ps(0, 128), D_KNB(b * 128, 128), IDENTB(0, 128))
                    P.mm(ps(128, 128), D_VB(b * 128, 128), IDENTB(0, 128))
                    for dr in range(2):
                        P.ts("dve", D_KDEC(dr * NT + b * 128, 128), ps(0, 128), D_KD(b * 16 + dr * 8 + h, 1), ALU.mult)
                    P.cp("act", D_VTOK(b * 128, 128), ps(128, 128))
                yield
                m4 = lambda t, c0=0: t.v(c0, [[256, 2], [128, 2], [1, 128]])
                for bp in range(5):
                    pk, pc, pt_, pl = PS[2], PS[3], PS[4], (PS[5], PS[0], PS[1])
                    for bl in range(2):
                        blk = 2 * bp + bl
                        P.mm(pk(bl * 128, 128), D_KNB(blk * 128, 128), D_KNB(blk * 128, 128))
                        P.mm(pk(256 + bl * 128, 128), D_KNB(blk * 128, 128), D_QNB(blk * 128, 128))
                    DIAG, EB, DT = D_NB[0], D_NB[1], D_NB[2]
                    Pa, Za = D_NB[3], D_NB[4]
                    gsel = lambda t: t.v(bp * 32 + h, [[16, 2], [8, 2], [0, 128]])
                    P.tt("pool", m4(DIAG), IDENT.v(0, [[0, 2], [0, 2], [1, 128]]), gsel(D_GC), ALU.mult)
                    for m in range(4):
                        P.mm(pc(m * 128, 128), CONST(C_ONESF, 128), DIAG(m * 128, 128))
                    P.act(EB(0, 512), pc(0, 512), AF.Exp)
                    P.tt("pool", D_QGT.v(2 * bp * 128, [[128, 2], [NT, 2], [1, 128]]),
                         D_QNB.v(2 * bp * 128, [[128, 2], [0, 2], [1, 128]]), m4(EB), ALU.mult)
                    P.tt("dve", m4(DT), m4(pc), CONST.v(C_MADD, [[0, 2], [128, 2], [1, 128]]), ALU.add)
                    for m in range(4):
                        bl, dr = m // 2, m % 2
                        P.act(DT(m * 128, 128), DT(m * 128, 128), AF.Exp,
                              bias=D_NEGGC((2 * bp + bl) * 16 + dr * 8 + h, 1))
                    P.tt("dve", m4(D_QKM, bp * 512), pk.v(256, [[128, 2], [0, 2], [1, 128]]), m4(DT), ALU.mult)
                    P.tt("pool", m4(DT), m4(DT), CONST.v(C_MS, [[0, 2], [128, 2], [1, 128]]), ALU.mult)
                    for m in range(4):
                        bl, dr = m // 2, m % 2
                        P.stt("dve", Pa(m * 128, 128), pk(bl * 128, 128),
                              D_NEGBETA((2 * bp + bl) * 16 + dr * 8 + h, 1), DT(m * 128, 128), ALU.mult, ALU.mult)
                    for m in range(4):
                        P.tr(pt_(m * 128, 128), Pa(m * 128, 128), IDENT(0, 128))
                    P.cp("act", Za(0, 512), pt_(0, 512))
                    yield
                    NBs = D_NB
                    msk = lambda c0: CONST.v(c0, [[0, 4], [1, 128]])
                    f4 = lambda t: t.v(0, [[128, 4], [1, 128]])
                    Pc, Zc = NBs[0], NBs[1]
                    P.tt("pool", f4(Pc), f4(Pa), msk(C_BD16), ALU.mult)
                    P.tt("pool", f4(Zc), f4(Za), msk(C_BD16), ALU.mult)
                    Xc, XTc, Xn, XTn = NBs[5], NBs[6], NBs[8], NBs[9]
                    P.tt("pool", f4(Xc), f4(Pc), msk(0), ALU.add)
                    P.tt("dve", f4(XTc), f4(Zc), msk(0), ALU.add)
                    pz = [(NBs[2], NBs[7]), (NBs[0], NBs[1])]
                    for k in range(1, 4):
                        Pn, Zn = pz[(k - 1) % 2]
                        for m in range(4):
                            P.mm(pl[0](m * 128, 128), Zc(m * 128, 128), Pc(m * 128, 128))
                        for m in range(4):
                            P.mm(pl[1](m * 128, 128), Pc(m * 128, 128), Zc(m * 128, 128))
                        P.cp("act", Pn(0, 512), pl[0](0, 512))
                        P.cp("dve", Zn(0, 512), pl[1](0, 512))
                        yield
                        for m in range(4):
                            P.mm(pl[2](m * 128, 128), Zn(m * 128, 128), Xc(m * 128, 128))
                        for m in range(4):
                            P.mm(pt_(m * 128, 128), Pn(m * 128, 128), XTc(m * 128, 128))
                        P.tt("dve", Xn(0, 512), pl[2](0, 512), Xc(0, 512), ALU.add)
                        P.tt("dve", XTn(0, 512), pt_(0, 512), XTc(0, 512), ALU.add)
                        Pc, Zc = Pn, Zn
                        Xc, XTc, Xn, XTn = Xn, XTn, Xc, XTc
                        yield
                    for lv, cmask in ((1, C_OFF1), (2, C_OFF2), (3, C_OFF3)):
                        Nl, NlT, T1, T1p = NBs[0], NBs[1], NBs[2], NBs[7]
                        P.tt("pool", f4(Nl), f4(Pa), msk(cmask), ALU.mult)
                        P.tt("pool", f4(NlT), f4(Za), msk(cmask), ALU.mult)
                        for m in range(4):
                            P.mm(pl[0](m * 128, 128), NlT(m * 128, 128), Xc(m * 128, 128))
                        P.cp("act", T1(0, 512), pl[0](0, 512))
                        if lv < 3:
                            for m in range(4):
                                P.mm(pl[1](m * 128, 128), Nl(m * 128, 128), XTc(m * 128, 128))
                            P.cp("dve", T1p(0, 512), pl[1](0, 512))
                        yield
                        for m in range(4):
                            P.mm(pl[2](m * 128, 128), XTc(m * 128, 128), T1(m * 128, 128))
                        if lv < 3:
                            for m in range(4):
                                P.mm(pt_(m * 128, 128), Xc(m * 128, 128), T1p(m * 128, 128))
                            P.tt("dve", Xn(0, 512), pl[2](0, 512), Xc(0, 512), ALU.add)
                            P.tt("dve", XTn(0, 512), pt_(0, 512), XTc(0, 512), ALU.add)
                            Xc, XTc, Xn, XTn = Xn, XTn, Xc, XTc
                        else:
                            P.tt("dve", D_XB(bp * 512, 512), pl[2](0, 512), Xc(0, 512), ALU.add)
                        yield

            def chain(h, dr):
                D_KNB = D_KNBS[h % 2]
                D_VTOK, D_KDEC, D_XB, D_QKM, D_QGT = (D_VTOKS[h % 2], D_KDECS[h % 2], D_XBS[h % 2],
                                                      D_QKMS[h % 2], D_QGTS[h % 2])
                SF, SB_, RB, VNB = D_SF[dr], D_SB[dr], D_RB[dr], D_VNB[dr]
                ps = PS[6 + dr]
                gcol = dr * 8 + h
                if dr == 0:
                    P.dma("sp", SF(0, 128), dreg(s0_in, ((j * 16) + gcol) * 16384, [[128, 128], [1, 128]]), "dS0_%d" % dr)
                else:
                    P.memset("dve", SF(0, 128), 0.0)
                P.cp("act", SB_(0, 128), SF(0, 128))
                pr, pv, po, pss = ps(0, 128), ps(128, 128), ps(256, 128), ps(384, 128)
                order = list(range(10)) if dr == 0 else list(range(9, -1, -1))
                nst = 0
                for b in order:
                    bp, m = b // 2, (b % 2) * 2 + dr
                    seg = b // 2
                    P.mm(pr, D_KNB(b * 128, 128), SB_(0, 128))
                    P.stt("dve", RB(0, 128), pr, D_NEGEGC(b * 16 + gcol, 1), D_VTOK(b * 128, 128),
                          ALU.mult, ALU.add)
                    yield
                    P.mm(pv, D_XB(bp * 512 + m * 128, 128), RB(0, 128))
                    P.ts("dve", VNB(0, 128), pv, D_BETA(b * 16 + gcol, 1), ALU.mult)
                    yield
                    P.mm(po, SB_(0, 128), D_QGT(dr * NT + b * 128, 128), start=True, stop=False)
                    P.mm(po, VNB(0, 128), D_QKM(bp * 512 + m * 128, 128), start=False, stop=True)
                    P.cp("act", D_OD(dr * NT + b * 128, 128), po)
                    yield
                    P.mm(pss, D_KDEC(dr * NT + b * 128, 128), VNB(0, 128))
                    P.stt("dve", SF(0, 128), SF(0, 128), D_EGL(b * 32 + gcol, 1), pss, ALU.mult, ALU.add)
                    last = (b % 2 == 1) if dr == 0 else (b % 2 == 0)
                    if last:
                        stg = D_STG[dr]
                        nst += 1
                        P.cp("act", stg(0, 128), SF(0, 128))
                        P.dma("sp", dreg(s_out, ((((j * 5 + seg) * 2 + dr) * 8 + h) * 16384), [[128, 128], [1, 128]]),
                              stg(0, 128), "dSO%d" % dr)
                        if dr == 0:
                            if seg < 3:
                                P.ts("dve", SF(0, 128), SF(0, 128), FLG(0, 1), ALU.mult)
                            elif seg == 3:
                                P.memset("dve", SF(0, 128), 0.0)
                        else:
                            if seg == 4:
                                P.dma("sp", SF(0, 128), dreg(s0_in, ((j * 16) + gcol) * 16384, [[128, 128], [1, 128]]),
                                      "dS0_%d" % dr)
                            elif seg >= 1:
                                P.ts("dve", SF(0, 128), SF(0, 128), FLG(0, 1), ALU.mult)
                    P.cp("act", SB_(0, 128), SF(0, 128))
                    yield

            def finalize(h):
                OS, TM = D_NB[0], D_NB[3]
                for ti, (t0, tn) in enumerate(TBS):
                    P.tt("pool", OS(0, tn), D_OD(t0, tn), D_OD(NT + t0, tn), ALU.add)
                    P.act(D_SQF(0, tn), OS(0, tn), AF.Square)
                    P.mm(PS[ti](0, tn), ONESB(0, 128), D_SQF(0, tn))
                    P.act(D_RSQ(0, tn), PS[ti](0, tn), AF.Sqrt, bias=EPSB(2, 1), scale=1.0 / HD)
                    P.add("dve", (lambda o: (lambda e: e.reciprocal(o.ap, o.ap)))(D_RSQ(0, tn)),
                          [D_RSQ(0, tn)], [D_RSQ(0, tn)])
                    P.stt("dve", TM(0, tn), OS(0, tn), D_GP(32, 1), D_RSQ(0, tn), ALU.mult, ALU.mult)
                    P.tt("pool", O(h * NT + t0, tn), TM(0, tn), D_ZS[h % 2](t0, tn), ALU.mult)

            def seq(*gens):
                for g in gens:
                    if g is not None:
                        for _ in g:
                            yield

            def interleave(chains, side, ratio):
                chains = list(chains)
                while chains or side is not None:
                    for g in list(chains):
                        try:
                            next(g)
                        except StopIteration:
                            chains.remove(g)
                    if side is not None:
                        for _ in range(ratio if chains else 10 ** 9):
                            try:
                                next(side)
                            except StopIteration:
                                side = None
                                break

            if DBG_STAGE == 10:
                return
            run_all(stageA(0))
            run_all(stageB(0))
            for h in range(8):
                side = seq(stageA(h + 1), stageB(h + 1)) if h < 7 else None
                interleave([chain(h, 0), chain(h, 1)], side, DN_RATIO)
                finalize(h)
            for t in range(2):
                slot = W.acquire(("do", j, t))
                for jj in range(4):
                    oc = t * 4 + jj
                    base = (oc % 2) * 3
                    for kc in range(8):
                        for ti, (t0, tn) in enumerate(TBS):
                            P.mm(PS[base + ti](0, tn), slot(kc * 512 + jj * 128, 128), O(kc * NT + t0, tn),
                                 start=(kc == 0), stop=(kc == 7))
                    gated_residual(par, 2, oc, [PS[base + ti] for ti in range(3)])
                W.release()

        run_all(modulation(0, 0))
        for l in range(nlayers):
            par = l % 2
            if mixers:
                if l % 2 == 0:
                    attention(l, par)
                else:
                    deltanet(l, par)
            side = modulation(l + 1, 1 - par) if l + 1 < nlayers else None
            mlp(l, par, side)

        rms_rstd(lambda c: X(c * NT, NT))
        for c in range(8):
            tmp = TMPN[c % 2]
            P.tt("dve", tmp(0, NT), X(c * NT, NT), RSTD(0, NT), ALU.mult)
            P.act(X(c * NT, NT), tmp(0, NT), AF.Copy, scale=GAINS(64 + c, 1))
        P.ts("pool", X(0, 8 * NT), X(0, 8 * NT), 32.0, ALU.mult)
        for b in range(10):
            yt = XTOK[b % 2]
            for half in range(2):
                ps = PS[(2 * b + half) % 4]
                for j in range(4):
                    c = half * 4 + j
                    P.tr(ps(j * 128, 128), X(c * NT + b * 128, 128), IDENT(0, 128))
                P.cp("dve" if half == 0 else "act", yt(half * 512, 512), ps(0, 512))
            P.dma("sp", dreg(y_out, b * 128 * D, [[D, 128], [1, D]]), yt(0, 1024), "yout%d" % (b % 2))

        if not wdescs_only:
            P.finalize()
    return nc, P, W.descs


def _tile_kc8(Wm, n0):
    blk = Wm[:, n0:n0 + 512].reshape(8, 128, 512)
    return np.ascontiguousarray(blk.transpose(1, 0, 2)).reshape(128, 4096)


def _tile_w2(Wm, oc):
    blk = Wm[:, oc * 128:(oc + 1) * 128].reshape(32, 128, 128)
    return np.ascontiguousarray(blk.transpose(1, 0, 2)).reshape(128, 4096)


def make_tiles(descs, inp):
    tiles = np.empty((len(descs), 128, 4096), np.float32)
    for i, d in enumerate(descs):
        kind, l = d[0], d[1]
        if kind == "mod":
            tiles[i] = _tile_kc8(inp["w_mod"][l], d[2] * 512)
        elif kind == "aq":
            tiles[i] = _tile_kc8(inp["attn_w_in"][l], d[2] * 512)
        elif kind == "akv":
            tiles[i] = _tile_kc8(inp["attn_w_in"][l], 1024)
        elif kind == "ao":
            tiles[i] = _tile_kc8(inp["attn_w_out"][l], d[2] * 512)
        elif kind == "dnh":
            h = d[2]
            Wm = inp["dn_w_in"][l]
            cols = np.concatenate([Wm[:, p * 1024 + h * 128: p * 1024 + (h + 1) * 128] for p in range(4)], axis=1)
            tiles[i] = _tile_kc8(cols, 0)
        elif kind == "do":
            tiles[i] = _tile_kc8(inp["dn_w_out"][l], d[2] * 512)
        elif kind == "w1":
            tiles[i] = _tile_kc8(inp["mlp_w1"][l], d[2] * 512)
        elif kind == "w2":
            tiles[i] = _tile_w2(inp["mlp_w2"][l], d[2])
        else:
            raise ValueError(d)
    return tiles


_CACHE = {}


def _get_program(nlayers=DEPTH, mixers=True):
    key = (nlayers, mixers)
    if key not in _CACHE:
        _, _, descs = build(nlayers, mixers, wdescs_only=True)
        nc, P, descs2 = build(nlayers, mixers, ntiles=len(descs))
        assert descs == descs2
        _CACHE[key] = (nc, P, descs)
    return _CACHE[key]


def _rope_tables():
    t = np.arange(NG)
    row = (t // 64).astype(np.float32)
    col = (t % 64).astype(np.float32)
    inv = (10000.0 ** (-np.arange(32, dtype=np.float32) / 32)).astype(np.float32)
    d = np.arange(128)
    pos = np.where((d // 64)[:, None] == 0, row[None, :], col[None, :]).astype(np.float32)
    ang = pos * inv[d % 32][:, None]
    return np.stack([np.cos(ang), np.sin(ang)]).astype(np.float32)


def _rott():
    R = np.zeros((128, 128), np.float32)
    for d in range(128):
        g, r = d // 64, d % 64
        if r < 32:
            R[d, g * 64 + r + 32] = -1.0
        else:
            R[d, g * 64 + r - 32] = 1.0
    return np.ascontiguousarray(R.T)


def core_inputs(inp, core, descs_tiles):
    xp, xs = inp["x_prompt"], inp["x_sample"]
    sample = core >= 6
    if not sample:
        xg = xp[4 * core:4 * core + 4].reshape(NG, D)
        condg = inp["c_ctx"]
        rope = np.stack([np.ones((128, NG), np.float32), np.zeros((128, NG), np.float32)])
        ck = np.zeros((2, 256, 256), np.float32)
        cv = np.zeros((2, 256, 256), np.float32)
        s0 = np.zeros((2, 16, 128, 128), np.float32)
        flag = 0.0
        mb = np.full((5, 4), -30000.0, np.float32)
        for q in range(4):
            mb[1 + q, q] = 0.0
    else:
        b = core - 6
        xg = xs[b]
        condg = inp["c"][b]
        rope = _rope_tables()
        ck = inp["cache_k"][b].reshape(2, 256, 256)
        cv = inp["cache_v"][b].reshape(2, 256, 256)
        mb = np.zeros((5, 4), np.float32)
        s0 = inp["state_delta"][b].reshape(2, 16, 128, 128)
        flag = 1.0
    xe = xp[24 + core]
    x_in = np.concatenate([xg, xe], axis=0)
    cond = np.stack([condg, inp["c_ctx"]], axis=0)
    cond_in = np.ascontiguousarray(cond.reshape(2, 8, 128).transpose(2, 1, 0)).reshape(128, 16)
    m = {"x_in": np.ascontiguousarray(x_in, dtype=np.float32), "cond_in": cond_in.astype(np.float32),
         "rope_in": rope, "ck_in": np.ascontiguousarray(ck, dtype=np.float32),
         "cv_in": np.ascontiguousarray(cv, dtype=np.float32),
         "maskb_in": np.ascontiguousarray(np.broadcast_to(mb.reshape(1, 20), (128, 20))),
         "s0_in": np.ascontiguousarray(s0, dtype=np.float32),
         "flags_in": np.full((128, 16), flag, np.float32)}
    for i in range((len(descs_tiles) + WCH - 1) // WCH):
        m["wts%d" % i] = descs_tiles[i * WCH:(i + 1) * WCH]
    return m


def _consts():
    c = np.zeros((128, NCONST), np.float32)
    j = np.arange(128)[:, None]
    i = np.arange(128)[None, :]
    c[:, 0:128] = np.eye(128)
    c[:, 128:256] = np.where(i >= j, 0.0, -60000.0)
    c[:, 256:384] = np.where(i <= j, 0.0, -60000.0)
    c[:, 384:512] = (i > j)
    c[:, 512:640] = (i < j)
    c[:, 640:768] = (j <= i)
    c[:, 768:896] = (j >= i)
    c[:, 896:1024] = 1.0
    c[:, 1024:1152] = 1.0
    c[:, 1152:1280] = 1.0
    c[:, 1280:1408] = 1.0
    c[:, 1408:1536] = (j // 16) == (i // 16)
    c[:, 1536:1664] = ((j // 32) == (i // 32)) & ((j // 16) != (i // 16))
    c[:, 1664:1792] = ((j // 64) == (i // 64)) & ((j // 32) != (i // 32))
    c[:, 1792:1920] = ((j // 64) != (i // 64))
    return c


def shared_inputs(inp):
    bmod = np.ascontiguousarray(inp["b_mod"].reshape(DEPTH, 48, 128).transpose(0, 2, 1))
    g = np.concatenate([
        inp["norm1"].reshape(DEPTH * 8, 128).T, inp["norm2"].reshape(DEPTH * 8, 128).T,
        inp["final_norm"].reshape(8, 128).T], axis=1)
    again = np.zeros((2, 128, 256), np.float32)
    for j in range(2):
        again[j, :, 128:] = np.broadcast_to(inp["attn_q_gain"][j][:, None], (128, 128))
        again[j, :, :128] = np.broadcast_to(inp["attn_k_gain"][j][None, :], (128, 128))
    dnab = np.ascontiguousarray(
        inp["dn_w_in"][:, :, 4096:4128].reshape(2, 8, 128, 32).transpose(0, 2, 1, 3)).reshape(2, 128, 256)
    convw = np.ascontiguousarray(
        inp["dn_conv"].reshape(2, 5, 24, 128).transpose(0, 3, 2, 1)).reshape(2, 128, 120)
    dngp = np.zeros((2, 128, 33), np.float32)
    for j in range(2):
        dngp[j, :, 0:16] = np.broadcast_to(inp["dn_a_log"][j].reshape(1, 16), (128, 16))
        dngp[j, :, 16:32] = np.broadcast_to(inp["dn_dt_bias"][j].reshape(1, 16), (128, 16))
        dngp[j, :, 32] = inp["dn_norm"][j]
    return {"bmod_in": bmod.astype(np.float32), "gains_in": np.ascontiguousarray(g, dtype=np.float32),
            "consts_in": _consts(), "rott_in": _rott(), "again_in": again,
            "dnab_in": dnab.astype(np.float32), "convw_in": convw.astype(np.float32), "dngp_in": dngp}


def kernel(**inp):
    inp = {k: np.asarray(v) for k, v in inp.items()}
    nc, P, descs = _get_program()
    tiles = make_tiles(descs, inp)
    sh = shared_inputs(inp)
    in_maps = []
    for core in range(8):
        m = core_inputs(inp, core, tiles)
        m.update(sh)
        in_maps.append(m)
    res = run_bass_kernel_spmd(nc, in_maps, core_ids=list(range(8))).results
    B, S = 32, 256
    y_prompt = np.empty((B, S, D), np.float32)
    y_sample = np.empty((2, NG, D), np.float32)
    nk = np.empty((B, 2, S, 2, 128), np.float32)
    nv = np.empty((B, 2, S, 2, 128), np.float32)
    ns = np.empty((B, 2, 2, 8, 128, 128), np.float32)
    for core in range(8):
        r = res[core]
        y, ko, vo, so = r["y_out"], r["k_out"], r["v_out"], r["s_out"]
        seqs = []
        if core < 6:
            for q in range(4):
                seqs.append((4 * core + q, q))
        else:
            y_sample[core - 6] = y[:NG]
        seqs.append((24 + core, 4))
        for (b, seg) in seqs:
            y_prompt[b] = y[seg * 256:(seg + 1) * 256]
            for j in range(2):
                nk[b, j] = ko[j, seg * 256:(seg + 1) * 256].reshape(S, 2, 128)
                nv[b, j] = vo[j, seg * 256:(seg + 1) * 256].reshape(S, 2, 128)
                ns[b, j] = so[j, seg]
    return (y_prompt, y_sample, nk, nv, ns)
```

```python
import numpy as np
from contextlib import ExitStack
import concourse.bass as bass
import concourse.mybir as mybir
from concourse.bass_utils import run_bass_kernel_spmd

F32 = mybir.dt.float32
BF16 = mybir.dt.bfloat16
AF = mybir.ActivationFunctionType
ALU = mybir.AluOpType

D = 1024
NT = 1280
NG = 1024
NE = 256
DEPTH = 4
DFF = 4096
HD = 128
EPS = 1e-6
TBS = [(0, 512), (512, 512), (1024, 256)]
NSLOT = 2
DN_RATIO = 2
NCONST = 1920
WCH = 16
DBG_STAGE = 99
DBG_SUB = 99
DBG_DUMP = False
DBG_NOPRO = False
SB_BASE = 16384 + 256
SB_END = 229376
KEYG = 256


class Reg:
    __slots__ = ("ap", "keys")

    def __init__(self, ap, keys):
        self.ap = ap
        self.keys = keys


class T:
    def __init__(self, P, name, F, dtype, space="sb", addr=None, handle=None):
        self.P, self.name, self.F, self.dtype, self.space = P, name, F, dtype, space
        self.esz = 4 if dtype == F32 else 2
        self.addr = addr
        self.h = handle

    def v(self, col, dims, p0=0, n=128):
        ap = bass.AP(self.h, p0 * self.F + col, [[self.F, n]] + [list(d) for d in dims])
        span = 1 + sum((c - 1) * s for s, c in dims)
        if self.space == "sb":
            lo = self.addr + col * self.esz
            hi = self.addr + (col + span) * self.esz
            keys = range(lo // KEYG, (hi - 1) // KEYG + 1)
        elif self.space == "ps":
            keys = (self.name,)
        else:
            keys = ()
        return Reg(ap, keys)

    def __call__(self, col=0, n=None, p0=0, np_=128):
        if n is None:
            n = self.F - col
        return self.v(col, [[1, n]], p0, np_)


class Prog:
    ENG = ("pe", "act", "dve", "pool", "sp")

    def __init__(self, nc):
        self.nc = nc
        self.ops = []
        self.sb_ptr = SB_BASE
        self.dram_in = {}
        self.dram_out = {}

    def sb(self, name, F, dtype, addr=None):
        esz = 4 if dtype == F32 else 2
        if addr is None:
            addr = self.sb_ptr
            self.sb_ptr += (F * esz + 63) // 64 * 64
            assert self.sb_ptr <= SB_END, ("SBUF overflow", name, self.sb_ptr)
        h = self.nc.alloc_sbuf_tensor_at(name, [128, F], dtype, offset=addr)
        return T(self, name, F, dtype, "sb", addr, h)

    def din(self, name, shape, dtype=F32):
        h = self.nc.dram_tensor(name, list(shape), dtype, kind="ExternalInput")
        self.dram_in[name] = (tuple(shape), dtype)
        return h

    def dout(self, name, shape, dtype=F32):
        h = self.nc.dram_tensor(name, list(shape), dtype, kind="ExternalOutput")
        self.dram_out[name] = (tuple(shape), dtype)
        return h

    def add(self, eng, fn, reads, writes, dma_key=None):
        rk = []
        for r in reads:
            if r is not None and isinstance(r, Reg):
                rk.extend(r.keys)
        wk = []
        for w in writes:
            wk.extend(w.keys)
        if eng != "pe":
            for r in reads:
                if r is not None and isinstance(r, Reg) and len(r.keys) and isinstance(r.keys[0], str):
                    wk.extend(r.keys)
        self.ops.append((eng, fn, rk, wk, dma_key))

    def finalize(self):
        nc = self.nc
        with ExitStack() as es:
            sems = {e: es.enter_context(nc.semaphore("s_" + e)) for e in ("pe", "act", "dve", "pool")}
            dsem = {}
            cnt = {e: 0 for e in sems}
            dcnt = {}
            known = {e: {} for e in self.ENG}
            last_w = {}
            readers = {}
            per_eng = {e: [] for e in self.ENG}
            for (eng, fn, rk, wk, dkey) in self.ops:
                need = {}

                def req(tok, kind):
                    name, val, teng = tok
                    if teng == eng and dkey is None:
                        if eng == "pe":
                            return
                    if known[eng].get(name, 0) >= val:
                        return
                    if need.get(name, 0) < val:
                        need[name] = val

                for k in rk:
                    t = last_w.get(k)
                    if t is not None:
                        req(t, "raw")
                for k in wk:
                    t = last_w.get(k)
                    if t is not None:
                        req(t, "waw")
                    for t in readers.get(k, ()):
                        req(t, "war")
                for name, val in need.items():
                    known[eng][name] = val
                if dkey is not None:
                    if dkey not in dsem:
                        dsem[dkey] = es.enter_context(nc.semaphore("d_" + dkey))
                        dcnt[dkey] = 0
                    dcnt[dkey] += 16
                    tok = ("d_" + dkey, dcnt[dkey], "dma")
                    inc = (dsem[dkey], 16)
                else:
                    cnt[eng] += 1
                    tok = (eng, cnt[eng], eng)
                    inc = (sems[eng], 1)
                for k in wk:
                    last_w[k] = tok
                    readers[k] = []
                for k in rk:
                    lst = readers.setdefault(k, [])
                    if not lst or lst[-1] != tok:
                        lst.append(tok)
                        if len(lst) > 24:
                            best = {}
                            for (a, b, c) in lst:
                                if best.get(a, (0, None))[0] < b:
                                    best[a] = (b, c)
                            lst[:] = [(a, b, c) for a, (b, c) in best.items()]
                per_eng[eng].append((fn, list(need.items()), inc))
            allsem = dict(sems)
            for k, s in dsem.items():
                allsem["d_" + k] = s
            final_waits = [(("d_" + k), v) for k, v in dcnt.items()]
            self.n_ops = {e: len(v) for e, v in per_eng.items()}
            with nc.Block() as block:
                def runner(ename):
                    def body(e):
                        for fn, waits, inc in per_eng[ename]:
                            for sname, val in waits:
                                e.wait_ge(allsem[sname], val)
                            ins = fn(e)
                            ins.then_inc(inc[0], inc[1])
                        if ename == "sp":
                            for sname, val in final_waits:
                                e.wait_ge(allsem[sname], val)
                            for en2 in ("pe", "act", "dve", "pool"):
                                if cnt[en2]:
                                    e.wait_ge(sems[en2], cnt[en2])
                    return body
                block.tensor(runner("pe"))
                block.scalar(runner("act"))
                block.vector(runner("dve"))
                block.gpsimd(runner("pool"))
                block.sync(runner("sp"))

    def mm(self, out, lhsT, rhs, start=True, stop=True):
        self.add("pe", lambda e: e.matmul(out.ap, lhsT.ap, rhs.ap, start=start, stop=stop),
                 [lhsT, rhs] + ([] if start else [out]), [out])

    def tr(self, out, in_, ident):
        self.add("pe", lambda e: e.transpose(out.ap, in_.ap, ident.ap), [in_, ident], [out])

    def act(self, out, in_, func, bias=0.0, scale=1.0, accum=None):
        b = bias.ap if isinstance(bias, Reg) else bias
        s = scale.ap if isinstance(scale, Reg) else scale
        if accum is None:
            fn = lambda e: e.activation(out.ap, in_.ap, func, bias=b, scale=s)
            w = [out]
        else:
            fn = lambda e: e.activation(out.ap, in_.ap, func, bias=b, scale=s, accum_out=accum.ap)
            w = [out, accum]
        self.add("act", fn, [in_, bias, scale], w)

    def tt(self, eng, out, in0, in1, op):
        self.add(eng, lambda e: e.tensor_tensor(out.ap, in0.ap, in1.ap, op), [in0, in1], [out])

    def ts(self, eng, out, in0, s1, op0, s2=None, op1=None):
        a = s1.ap if isinstance(s1, Reg) else s1
        b = s2.ap if isinstance(s2, Reg) else s2
        if op1 is None:
            fn = lambda e: e.tensor_scalar(out.ap, in0.ap, a, None, op0)
        else:
            fn = lambda e: e.tensor_scalar(out.ap, in0.ap, a, b, op0, op1)
        self.add(eng, fn, [in0, s1, s2], [out])

    def stt(self, eng, out, in0, sc, in1, op0, op1):
        a = sc.ap if isinstance(sc, Reg) else sc
        self.add(eng, lambda e: e.scalar_tensor_tensor(out.ap, in0.ap, a, in1.ap, op0, op1),
                 [in0, sc, in1], [out])

    def cp(self, eng, out, in_):
        if eng == "act":
            self.add("act", lambda e: e.copy(out.ap, in_.ap), [in_], [out])
        else:
            self.add(eng, lambda e: e.tensor_copy(out.ap, in_.ap), [in_], [out])

    def memset(self, eng, out, val):
        self.add(eng, lambda e: e.memset(out.ap, val), [], [out])

    def dma(self, q, out, in_, key):
        self.add(q, lambda e: e.dma_start(out=out.ap, in_=in_.ap), [in_], [out], dma_key=key)


def dreg(h, off, dims):
    return Reg(bass.AP(h, off, [list(d) for d in dims]), ())


class WStream:
    def __init__(self, P, ntiles_hint=None):
        self.P = P
        self.slots = [P.sb("wslot%d" % i, 4096, BF16) for i in range(NSLOT)]
        self.descs = []
        self.issued = 0
        self.h = None

    def bind(self, h, ntiles):
        self.h = h
        self.ntiles = ntiles

    def _issue(self):
        t = self.issued
        if self.h is not None and t < self.ntiles:
            s = self.slots[t % NSLOT]
            self.P.dma("pool", s(0, 4096), dreg(self.h[t // WCH], (t % WCH) * 128 * 4096, [[4096, 128], [1, 4096]]),
                       key="w%d" % (t % NSLOT))
        self.issued += 1

    def start(self):
        for _ in range(NSLOT):
            self._issue()

    def acquire(self, desc):
        t = len(self.descs)
        self.descs.append(desc)
        return self.slots[t % NSLOT]

    def release(self):
        self._issue()


def build(nlayers=DEPTH, mixers=True, wdescs_only=False, ntiles=0, dbg=None):
    nc = bass.Bass("TRN2", target_bir_lowering=False)
    P = Prog(nc)
    x_in = P.din("x_in", [NT, D])
    cond_in = P.din("cond_in", [128, 16])
    bmod_in = P.din("bmod_in", [DEPTH, 128, 48])
    gains_in = P.din("gains_in", [128, 72])
    consts_in = P.din("consts_in", [128, NCONST])
    flags_in = P.din("flags_in", [128, 16])
    if nlayers >= 2 and mixers:
        dnab_in = P.din("dnab_in", [2, 128, 256])
        convw_in = P.din("convw_in", [2, 128, 120])
        dngp_in = P.din("dngp_in", [2, 128, 33])
        s0_in = P.din("s0_in", [2, 16, 128, 128])
        s_out = P.dout("s_out", [2, 5, 2, 8, 128, 128])
    dbg_out = P.dout("dbg_out", [8, 128, NT]) if DBG_DUMP else None
    wts = [P.din("wts%d" % i, [min(WCH, max(ntiles, 1) - i * WCH), 128, 4096])
           for i in range((max(ntiles, 1) + WCH - 1) // WCH)]
    y_out = P.dout("y_out", [NT, D])
    rope_in = P.din("rope_in", [2, 128, NG])
    rott_in = P.din("rott_in", [128, 128])
    ck_in = P.din("ck_in", [2, 256, 256])
    cv_in = P.din("cv_in", [2, 256, 256])
    maskb_in = P.din("maskb_in", [128, 20])
    again_in = P.din("again_in", [2, 128, 256])
    k_out = P.dout("k_out", [2, NT, 256])
    v_out = P.dout("v_out", [2, NT, 256])

    W = WStream(P)
    W.bind(wts, ntiles)

    X = P.sb("X", 8 * NT, F32)
    H = P.sb("H", 8 * NT, BF16)
    CONST = P.sb("CONST", NCONST, F32)
    IDENT = P.sb("IDENT", 128, F32)
    FLG = P.sb("FLG", 16, F32)
    ONESB = P.sb("ONESB", 128, BF16)
    IDENTB = P.sb("IDENTB", 128, BF16)
    EPSB = P.sb("EPSB", 4, F32)
    ROTT = P.sb("ROTT", 128, BF16)
    CONDT = P.sb("CONDT", 16, F32)
    SC = P.sb("SC", 16, BF16)
    GAINS = P.sb("GAINS", 80, F32)
    BMOD = P.sb("BMOD", 48, F32)
    MODS = [P.sb("MOD%d" % i, 96, F32) for i in range(2)]
    AB = [P.sb("AB%d" % i, 32, F32) for i in range(2)]
    P.sb_ptr = (P.sb_ptr + 255) // 256 * 256
    scr0 = P.sb_ptr
    SCR_SZ = SB_END - scr0
    assert SCR_SZ >= 102400, SCR_SZ

    def scr(name, F, dtype, off):
        esz = 4 if dtype == F32 else 2
        assert off + F * esz <= SCR_SZ, (name, off, F * esz, SCR_SZ)
        return P.sb(name, F, dtype, addr=scr0 + off)

    HID = scr("HID", 32 * NT, BF16, 0)
    RSTD = scr("RSTD", NT, F32, 81920)
    SQ = [scr("SQ%d" % i, NT, BF16, 81920 + 5120 + i * 2560) for i in range(2)]
    TMPN = [scr("TMPN%d" % i, NT, F32, 81920 + 10240 + i * 5120) for i in range(2)]
    XTOK = [scr("XTOK%d" % i, 1024, F32, i * 4096) for i in range(2)]
    O = scr("O", 8 * NT, BF16, 81920)
    QT = scr("QT", 8 * NT, BF16, 0)
    KT = scr("KT", 2 * 1536, BF16, 20480)
    VT = scr("VT", 12 * 256, BF16, 26624)
    COS = scr("COS", NG, F32, 32768)
    SIN = scr("SIN", NG, F32, 36864)
    QRAW = [scr("QRAW%d" % i, NT, F32, 40960 + i * 5120) for i in range(2)]
    QN = scr("QN", NT, F32, 51200)
    QNB = scr("QNB", NT, BF16, 56320)
    T1 = [scr("T1_%d" % i, 512, F32, 58880 + i * 2048) for i in range(2)]
    T2 = [scr("T2_%d" % i, 512, F32, 62976 + i * 2048) for i in range(2)]
    PT = [scr("PT%d" % i, 512, BF16, 67072 + i * 1024) for i in range(4)]
    KTOK = [scr("KTOK%d" % i, 256, F32, 71168 + i * 1024) for i in range(2)]
    VSTG = [scr("VSTG%d" % i, 256, F32, 73216 + i * 1024) for i in range(2)]
    CKV = scr("CKV", 512, F32, 75264)
    MASKB = scr("MASKB", 32, F32, 77312)
    AGN = scr("AGN", 129, F32, 77440)
    KSS = scr("KSS", 16, F32, 78016)
    REC = scr("REC", 512, F32, 78080)
    AGN2 = scr("AGN2", 256, F32, 102400)
    KGB = AGN2
    KNB = scr("KNB", 256, BF16, 62976)
    KNF = scr("KNF", 256, F32, 62976 + 512)
    if nlayers >= 2 and mixers:
        _o = [0]

        def dscr(name, F, dtype):
            esz = 4 if dtype == F32 else 2
            t = scr(name, F, dtype, _o[0])
            _o[0] += (F * esz + 63) // 64 * 64
            assert _o[0] <= 81920, (name, _o[0])
            P.dscr_used = _o[0]
            return t
        D_NB = [dscr("D_NB%d" % i, 512, F32) for i in range(10)]
        D_FT = P.sb("D_FT", NT, F32, addr=D_NB[5].addr)
        D_G = P.sb("D_G", 160, F32, addr=D_NB[9].addr)
        D_TG = P.sb("D_TG", 160, F32, addr=D_NB[9].addr + 640)
        D_GLO = P.sb("D_GLO", 160, F32, addr=D_NB[9].addr + 1280)
        D_QNB = dscr("D_QNB", NT, BF16)
        D_KNBS = [dscr("D_KNB%d" % i, NT, BF16) for i in range(2)]
        D_VB = dscr("D_VB", NT, BF16)
        D_ZS = [dscr("D_ZS%d" % i, NT, BF16) for i in range(2)]
        D_PAD = dscr("D_PAD", 1408, BF16)
        D_DG = dscr("D_DG", 5 * 128, BF16)
        D_SQ = P.sb("D_SQ", NT, BF16, addr=D_VB.addr)
        D_SQF = P.sb("D_SQF", 512, BF16, addr=D_NB[4].addr)
        D_RSQ = P.sb("D_RSQ", 512, F32, addr=D_NB[8].addr)
        _top = 103424
        D_VTOKS = [dscr("D_VTOK0", NT, BF16), dscr("D_VTOK1", NT, BF16)]
        D_KDECS = [dscr("D_KDEC0", 2 * NT, BF16), scr("D_KDEC1", 2 * NT, BF16, _top)]
        D_XBS = [dscr("D_XB0", 5 * 512, BF16), scr("D_XB1", 5 * 512, BF16, _top + 5120)]
        D_QKMS = [dscr("D_QKM0", 5 * 512, BF16), scr("D_QKM1", 5 * 512, BF16, _top + 10240)]
        D_QGTS = [dscr("D_QGT0", 2 * NT, BF16), scr("D_QGT1", 2 * NT, BF16, _top + 15360)]
        D_OD = dscr("D_OD", 2 * NT, BF16)
        D_SF = [dscr("D_SF%d" % i, 128, F32) for i in range(2)]
        D_SB = [dscr("D_SB%d" % i, 128, BF16) for i in range(2)]
        D_RB = [dscr("D_RB%d" % i, 128, BF16) for i in range(2)]
        D_VNB = [dscr("D_VNB%d" % i, 128, BF16) for i in range(2)]
        D_STG = [dscr("D_STG%d" % i, 128, F32) for i in range(2)]
        D_GC = dscr("D_GC", 160, F32)
        D_NEGGC = dscr("D_NEGGC", 160, F32)
        D_NEGEGC = dscr("D_NEGEGC", 160, F32)
        D_KD = dscr("D_KD", 160, F32)
        D_BETA = dscr("D_BETA", 160, F32)
        D_NEGBETA = dscr("D_NEGBETA", 160, F32)
        D_EGL = dscr("D_EGL", 320, F32)
        D_GP = dscr("D_GP", 33, F32)
        D_EA = dscr("D_EA", 16, F32)
        D_WAB = dscr("D_WAB", 256, BF16)
        D_CW = dscr("D_CW", 120, F32)


    with ExitStack() as es:
        PS = []
        for i in range(8):
            h = es.enter_context(nc.psum_tensor("ps%d" % i, [128, 512], F32))
            PS.append(T(P, "ps%d" % i, 512, F32, "ps", None, h))

        P.dma("sp", IDENT(0, 128), dreg(consts_in, 0, [[NCONST, 128], [1, 128]]), "cI")
        if not DBG_NOPRO:
            for ci in range(0, NCONST, 512):
                cw = min(512, NCONST - ci)
                P.dma("sp", CONST(ci, cw), dreg(consts_in, ci, [[NCONST, 128], [1, cw]]), "cI%d" % ci)
        if not DBG_NOPRO:
            P.dma("sp", FLG(0, 16), dreg(flags_in, 0, [[16, 128], [1, 16]]), "cF")
            P.memset("dve", EPSB(3, 1), 1.0)
            P.cp("act", IDENTB(0, 128), CONST(0, 128))
        P.dma("sp", CONDT(0, 16), dreg(cond_in, 0, [[16, 128], [1, 16]]), "cC")
        P.dma("sp", GAINS(0, 72), dreg(gains_in, 0, [[72, 128], [1, 72]]), "cG")
        P.memset("dve", ONESB(0, 128), 1.0)
        P.dma("pool", ROTT(0, 128), dreg(rott_in, 0, [[128, 128], [1, 128]]), "c1")
        P.memset("dve", EPSB(0, 1), float(D * EPS))
        P.memset("dve", EPSB(1, 1), float(HD * EPS))
        P.memset("dve", EPSB(2, 1), float(EPS))
        P.act(SC(0, 16), CONDT(0, 16), AF.Silu)
        P.ts("dve", GAINS(72, 8), GAINS(64, 8), 32.0, ALU.mult)
        W.start()

        for b in range(10):
            xt = XTOK[b % 2]
            P.dma("sp", xt(0, 1024), dreg(x_in, b * 128 * D, [[D, 128], [1, D]]), "xin%d" % (b % 2))
            for half in range(2):
                ps = PS[(2 * b + half) % 4]
                for j in range(4):
                    c = half * 4 + j
                    P.tr(ps(j * 128, 128), xt(c * 128, 128), IDENT(0, 128))
                P.cp("dve" if half == 0 else "act",
                     X.v(half * 4 * NT + b * 128, [[NT, 4], [1, 128]]),
                     ps.v(0, [[128, 4], [1, 128]]))

        def modulation(l, par):
            MOD = MODS[par]
            P.dma("sp", BMOD(0, 48), dreg(bmod_in, l * 128 * 48, [[48, 128], [1, 48]]), "bmod")
            ps = PS[7]
            for t in range(12):
                slot = W.acquire(("mod", l, t))
                for j in range(4):
                    f = t * 4 + j
                    for kc in range(8):
                        P.mm(ps(f * 2, 2), slot(kc * 512 + j * 128, 128), SC(kc * 2, 2),
                             start=(kc == 0), stop=(kc == 7))
                W.release()
                yield
            P.tt("dve", MOD.v(0, [[2, 48], [1, 2]]), ps.v(0, [[2, 48], [1, 2]]),
                 BMOD.v(0, [[1, 48], [0, 2]]), ALU.add)
            A = AB[par]
            for which, (m_scale, goff) in enumerate(((1, l * 8), (4, 32 + l * 8))):
                P.ts("dve", A.v(which * 16, [[2, 8], [1, 2]]), MOD.v(m_scale * 16, [[2, 8], [1, 2]]),
                     1.0, ALU.add, 32.0, ALU.mult)
                P.tt("dve", A.v(which * 16, [[2, 8], [1, 2]]), A.v(which * 16, [[2, 8], [1, 2]]),
                     GAINS.v(goff, [[1, 8], [0, 2]]), ALU.mult)
            yield

        def run_all(gen):
            for _ in gen:
                pass

        def rms_rstd(src_chunk):
            for c in range(8):
                sq = SQ[c % 2]
                P.act(sq(0, NT), src_chunk(c), AF.Square)
                for ti, (t0, tn) in enumerate(TBS):
                    P.mm(PS[4 + ti](0, tn), ONESB(0, 128), sq(t0, tn), start=(c == 0), stop=(c == 7))
            for ti, (t0, tn) in enumerate(TBS):
                P.act(RSTD(t0, tn), PS[4 + ti](0, tn), AF.Sqrt, bias=EPSB(0, 1))
                P.add("dve", (lambda o, i: (lambda e: e.reciprocal(o.ap, i.ap)))(RSTD(t0, tn), RSTD(t0, tn)),
                      [RSTD(t0, tn)], [RSTD(t0, tn)])

        def norm_mod(par, which, m_shift):
            MOD = MODS[par]
            A = AB[par]
            rms_rstd(lambda c: X(c * NT, NT))
            for c in range(8):
                tmp = TMPN[c % 2]
                P.tt("dve" if c % 2 == 0 else "pool", tmp(0, NT), X(c * NT, NT), RSTD(0, NT), ALU.mult)
                for (t0, tn, cd) in ((0, NG, 0), (NG, NE, 1)):
                    P.act(H(c * NT + t0, tn), tmp(t0, tn), AF.Identity,
                          bias=MOD((m_shift * 8 + c) * 2 + cd, 1),
                          scale=A(which * 16 + c * 2 + cd, 1))

        def gated_residual(par, m_gate, c, pss):
            MOD = MODS[par]
            for ti, (t0, tn) in enumerate(TBS):
                cd = 0 if t0 < NG else 1
                P.stt("dve", X(c * NT + t0, tn), pss[ti](0, tn), MOD((m_gate * 8 + c) * 2 + cd, 1),
                      X(c * NT + t0, tn), ALU.mult, ALU.add)

        def mlp(l, par, side_gen):
            norm_mod(par, 1, 3)
            RT = TMPN
            for t in range(8):
                slot = W.acquire(("w1", l, t))
                for j in range(4):
                    oc = t * 4 + j
                    base = (oc % 2) * 3
                    for kc in range(8):
                        for ti, (t0, tn) in enumerate(TBS):
                            P.mm(PS[base + ti](0, tn), slot(kc * 512 + j * 128, 128), H(kc * NT + t0, tn),
                                 start=(kc == 0), stop=(kc == 7))
                    rt = RT[oc % 2]
                    for ti, (t0, tn) in enumerate(TBS):
                        P.act(rt(t0, tn), PS[base + ti](0, tn), AF.Relu)
                    P.tt("pool", HID(oc * NT, NT), rt(0, NT), rt(0, NT), ALU.mult)
                W.release()
                if side_gen is not None:
                    next(side_gen, None)
            for oc in range(8):
                slot = W.acquire(("w2", l, oc))
                base = (oc % 2) * 3
                for kc in range(32):
                    for ti, (t0, tn) in enumerate(TBS):
                        P.mm(PS[base + ti](0, tn), slot(kc * 128, 128), HID(kc * NT + t0, tn),
                             start=(kc == 0), stop=(kc == 31))
                gated_residual(par, 5, oc, [PS[base + ti] for ti in range(3)])
                W.release()
                if side_gen is not None:
                    next(side_gen, None)
            if side_gen is not None:
                run_all(side_gen)


        def attention(l, par):
            j = l // 2
            norm_mod(par, 0, 0)
            P.dma("sp", COS(0, NG), dreg(rope_in, 0, [[NG, 128], [1, NG]]), "aCOS")
            P.dma("sp", SIN(0, NG), dreg(rope_in, 128 * NG, [[NG, 128], [1, NG]]), "aSIN")
            P.dma("sp", MASKB(0, 20), dreg(maskb_in, 0, [[20, 128], [1, 20]]), "aMB")
            P.dma("sp", AGN2(0, 256), dreg(again_in, j * 128 * 256, [[256, 128], [1, 256]]), "aAG")
            P.dma("sp", CKV.v(0, [[256, 2], [1, 256]]),
                  dreg(ck_in, j * 256 * 256, [[256, 128], [128 * 256, 2], [1, 256]]), "aCK")
            P.dma("pool", VT.v(0, [[256, 2], [1, 256]]),
                  dreg(cv_in, j * 256 * 256, [[256, 128], [128 * 256, 2], [1, 256]]), "acv")
            if DBG_STAGE == 0:
                return
            for kc in range(2):
                ps = PS[6 + kc]
                for kv in range(2):
                    P.tr(ps(kv * 128, 128), CKV(kc * 256 + kv * 128, 128), IDENT(0, 128))
                P.cp("act", KT.v(kc * 128, [[1536, 2], [1, 128]]), ps.v(0, [[128, 2], [1, 128]]))
            if DBG_STAGE == 1:
                return
            qslots = {}

            def q_proj(h):
                t, jj = divmod(h, 4)
                if jj == 0:
                    qslots[t] = W.acquire(("aq", j, t))
                slot = qslots[t]
                base = (h % 2) * 3
                for kc in range(8):
                    for ti, (t0, tn) in enumerate(TBS):
                        P.mm(PS[base + ti](0, tn), slot(kc * 512 + jj * 128, 128), H(kc * NT + t0, tn),
                             start=(kc == 0), stop=(kc == 7))
                qraw = QRAW[h % 2]
                sq = SQ[h % 2]
                for ti, (t0, tn) in enumerate(TBS):
                    P.cp("dve", qraw(t0, tn), PS[base + ti](0, tn))
                    P.act(sq(t0, tn), qraw(t0, tn), AF.Square)
                if jj == 3:
                    W.release()

            def q_norm_rope(h):
                qraw = QRAW[h % 2]
                sq = SQ[h % 2]
                RECS = TMPN[0]
                pbs = [PS[6 + ti % 2] for ti in range(3)]
                for ti, (t0, tn) in enumerate(TBS):
                    P.mm(pbs[ti](0, tn), ONESB(0, 128), sq(t0, tn))
                    P.act(RECS(t0, tn), pbs[ti](0, tn), AF.Sqrt, bias=EPSB(2, 1), scale=1.0 / HD)
                for ti, (t0, tn) in enumerate(TBS):
                    P.add("dve", (lambda o: (lambda e: e.reciprocal(o.ap, o.ap)))(RECS(t0, tn)),
                          [RECS(t0, tn)], [RECS(t0, tn)])
                    P.stt("dve", QN(t0, tn), qraw(t0, tn), AGN2(128, 1), RECS(t0, tn), ALU.mult, ALU.mult)
                    if t0 >= NG:
                        P.cp("act", QT(h * NT + t0, tn), QN(t0, tn))
                    else:
                        P.cp("act", QNB(t0, tn), QN(t0, tn))
                for ti, (t0, tn) in enumerate(TBS):
                    if t0 >= NG:
                        continue
                    P.mm(pbs[ti](0, tn), ROTT(0, 128), QNB(t0, tn))
                    P.tt("pool", T2[ti](0, tn), QN(t0, tn), COS(t0, tn), ALU.mult)
                    P.tt("dve", T1[ti](0, tn), pbs[ti](0, tn), SIN(t0, tn), ALU.mult)
                    P.tt("pool", QT(h * NT + t0, tn), T1[ti](0, tn), T2[ti](0, tn), ALU.add)

            q_proj(0)
            for h in range(8):
                if h + 1 < 8:
                    q_proj(h + 1)
                q_norm_rope(h)
            if DBG_STAGE == 2:
                return
            P.memset("dve", KSS(0, 16), 1.0)
            slot = W.acquire(("akv", j))

            def kv_proj(b):
                for kc in range(8):
                    P.mm(PS[b % 4](0, 512), H(kc * NT + b * 128, 128), slot(kc * 512, 512),
                         start=(kc == 0), stop=(kc == 7))

            kv_proj(0)
            for b in range(10):
                ps = PS[b % 4]
                if b + 1 < 10:
                    kv_proj(b + 1)
                ktok = KTOK[b % 2]
                vstg = VSTG[b % 2]
                if DBG_SUB == 10:
                    continue
                P.act(T1[0](0, 256), ps(0, 256), AF.Square)
                P.add("dve", (lambda o, i: (lambda e: e.tensor_reduce(o.ap, i.ap, mybir.AxisListType.X, ALU.add)))(
                    KSS(0, 2), T1[0].v(0, [[128, 2], [1, 128]])), [T1[0](0, 256)], [KSS(0, 2)])
                P.act(KSS(8, 8), KSS(0, 8), AF.Sqrt, bias=EPSB(2, 1), scale=1.0 / HD)
                P.add("dve", (lambda o: (lambda e: e.reciprocal(o.ap, o.ap)))(KSS(8, 8)), [KSS(8, 8)], [KSS(8, 8)])
                for kv in range(2):
                    P.stt("dve", ktok(kv * 128, 128), ps(kv * 128, 128), KSS(8 + kv, 1), KGB(0, 128),
                          ALU.mult, ALU.mult)
                if DBG_SUB == 11:
                    continue
                P.dma("sp", dreg(k_out, (j * NT + b * 128) * 256, [[256, 128], [1, 256]]), ktok(0, 256),
                      "ko%d" % (b % 2))
                if DBG_SUB == 12:
                    continue
                P.cp("act", VT((2 + b) * 256, 256), ps(256, 256))
                P.cp("dve", vstg(0, 256), ps(256, 256))
                P.dma("sp", dreg(v_out, (j * NT + b * 128) * 256, [[256, 128], [1, 256]]), vstg(0, 256),
                      "vo%d" % (b % 2))
                if DBG_SUB == 13:
                    continue
                pt = PS[4 + b % 2]
                for kv in range(2):
                    P.tr(pt(kv * 128, 128), ktok(kv * 128, 128), IDENT(0, 128))
                kdst = KT.v(256 + b * 128, [[1536, 2], [1, 128]])
                if DBG_SUB == 14:
                    continue
                if b >= 8:
                    if DBG_SUB not in (152, 153):
                        P.cp("act", kdst, pt.v(0, [[128, 2], [1, 128]]))
                else:
                    if DBG_SUB not in (151, 153):
                        P.cp("act", KNB(0, 256), pt(0, 256))
                    if DBG_SUB not in (151, 152):
                        P.cp("dve", KNF(0, 256), pt(0, 256))
                    if DBG_SUB in (15, 151, 152, 153):
                        continue
                    pr = PS[6 + b % 2]
                    P.mm(pr(0, 256), ROTT(0, 128), KNB(0, 256))
                    if DBG_SUB == 16:
                        continue
                    P.tt("dve", T1[1].v(0, [[128, 2], [1, 128]]), pr.v(0, [[128, 2], [1, 128]]),
                         SIN.v(b * 128, [[0, 2], [1, 128]]), ALU.mult)
                    P.tt("dve", T2[1].v(0, [[128, 2], [1, 128]]), KNF.v(0, [[128, 2], [1, 128]]),
                         COS.v(b * 128, [[0, 2], [1, 128]]), ALU.mult)
                    P.tt("dve", kdst, T1[1].v(0, [[128, 2], [1, 128]]), T2[1].v(0, [[128, 2], [1, 128]]), ALU.add)
            W.release()
            if DBG_STAGE == 3:
                return
            units = []
            it = 0
            for h in range(8):
                kv = h // 4
                for (q0, qn, chunks) in ((0, 512, list(range(10))), (512, 512, list(range(10))),
                                         (NG, NE, [10, 11])):
                    po = PS[2 + it % 2]
                    pd = PS[4 + it % 2]
                    it += 1
                    for ci, kc in enumerate(chunks):
                        units.append((h, kv, q0, qn, kc, ci == 0, ci == len(chunks) - 1, po, pd))

            def score(idx):
                h, kv, q0, qn, kc, first, last, po, pd = units[idx]
                pS = PS[idx % 2]
                ptile = PT[idx % 4]
                P.mm(pS(0, qn), KT(kv * 1536 + kc * 128, 128), QT(h * NT + q0, qn))
                if q0 >= NG:
                    P.act(ptile(0, qn), pS(0, qn), AF.Exp, scale=float(HD ** -0.5))
                else:
                    kb = 0 if kc < 2 else 1 + (kc - 2) // 2
                    for hf in range(2):
                        qblk = q0 // 256 + hf
                        P.act(ptile(hf * 256, 256), pS(hf * 256, 256), AF.Exp,
                              bias=MASKB(kb * 4 + qblk, 1), scale=float(HD ** -0.5))

            def pv(idx):
                h, kv, q0, qn, kc, first, last, po, pd = units[idx]
                ptile = PT[idx % 4]
                P.mm(po(0, qn), VT(kc * 256 + kv * 128, 128), ptile(0, qn), start=first, stop=last)
                P.mm(pd(0, qn), ONESB(0, 128), ptile(0, qn), start=first, stop=last)
                if last:
                    P.add("dve", (lambda o, i: (lambda e: e.reciprocal(o.ap, i.ap)))(REC(0, qn), pd(0, qn)),
                          [pd(0, qn)], [REC(0, qn)])
                    P.tt("dve", O(h * NT + q0, qn), po(0, qn), REC(0, qn), ALU.mult)

            for idx in range(len(units) + 1):
                if idx < len(units):
                    score(idx)
                if idx >= 1:
                    pv(idx - 1)
            if DBG_STAGE == 4:
                return
            for t in range(2):
                slot = W.acquire(("ao", j, t))
                for jj in range(4):
                    oc = t * 4 + jj
                    base = (oc % 2) * 3
                    for kc in range(8):
                        for ti, (t0, tn) in enumerate(TBS):
                            P.mm(PS[base + ti](0, tn), slot(kc * 512 + jj * 128, 128), O(kc * NT + t0, tn),
                                 start=(kc == 0), stop=(kc == 7))
                    gated_residual(par, 2, oc, [PS[base + ti] for ti in range(3)])
                W.release()


        def deltanet(l, par):
            j = l // 2
            norm_mod(par, 0, 0)
            C_MADD, C_MS, C_UF, C_UB, C_SAME, C_IND0, C_ONESF = 128, 384, 640, 768, 896, 1024, 1280
            C_BD16, C_OFF1, C_OFF2, C_OFF3 = 1408, 1536, 1664, 1792
            P.dma("pool", D_WAB(0, 256), dreg(dnab_in, j * 128 * 256, [[256, 128], [1, 256]]), "dWAB")
            P.dma("sp", D_CW(0, 120), dreg(convw_in, j * 128 * 120, [[120, 128], [1, 120]]), "dCW")
            P.dma("sp", D_GP(0, 33), dreg(dngp_in, j * 128 * 33, [[33, 128], [1, 33]]), "dGP")
            P.memset("pool", D_PAD(0, 5 * 260), 0.0)
            pab, pgc, pgl, peg = PS[0], PS[1], PS[2], PS[3]
            for b in range(10):
                for kc in range(8):
                    P.mm(pab(b * 32, 32), H(kc * NT + b * 128, 128), D_WAB(kc * 32, 32),
                         start=(kc == 0), stop=(kc == 7))
            v3 = lambda t, c0=0, w=16, st=16: t.v(c0, [[st, 10], [1, w]])
            P.tt("dve", v3(D_TG), pab.v(0, [[32, 10], [1, 16]]), D_GP.v(16, [[0, 10], [1, 16]]), ALU.add)
            SPY, SPZ, SPP = D_GLO, D_KD, D_NEGGC
            P.ts("dve", SPY(0, 160), D_TG(0, 160), -1.0, ALU.mult)
            P.tt("dve", SPY(0, 160), SPY(0, 160), D_TG(0, 160), ALU.max)
            P.act(SPY(0, 160), SPY(0, 160), AF.Exp, scale=-1.0)
            P.ts("dve", SPZ(0, 160), SPY(0, 160), 2.0, ALU.add)
            P.add("dve", (lambda o: (lambda e: e.reciprocal(o.ap, o.ap)))(SPZ(0, 160)), [SPZ(0, 160)], [SPZ(0, 160)])
            P.tt("dve", SPZ(0, 160), SPY(0, 160), SPZ(0, 160), ALU.mult)
            P.tt("dve", SPY(0, 160), SPZ(0, 160), SPZ(0, 160), ALU.mult)
            P.ts("dve", SPP(0, 160), SPY(0, 160), 1.0 / 9.0, ALU.mult, 1.0 / 7.0, ALU.add)
            for cst in (1.0 / 5.0, 1.0 / 3.0, 1.0):
                P.tt("dve", SPP(0, 160), SPP(0, 160), SPY(0, 160), ALU.mult)
                P.ts("dve", SPP(0, 160), SPP(0, 160), cst, ALU.add)
            P.tt("dve", SPP(0, 160), SPP(0, 160), SPZ(0, 160), ALU.mult)
            P.ts("dve", D_TG(0, 160), D_TG(0, 160), 0.0, ALU.max)
            P.stt("dve", D_TG(0, 160), SPP(0, 160), 2.0, D_TG(0, 160), ALU.mult, ALU.add)
            P.act(D_EA(0, 16), D_GP(0, 16), AF.Exp)
            P.stt("dve", v3(D_G), v3(D_TG), -1.0, D_EA.v(0, [[0, 10], [1, 16]]), ALU.mult, ALU.mult)
            P.act(v3(D_BETA), pab.v(16, [[32, 10], [1, 16]]), AF.Sigmoid)
            P.ts("pool", D_NEGBETA(0, 160), D_BETA(0, 160), -1.0, ALU.mult)
            for b in range(10):
                P.mm(pgc(b * 16, 8), CONST(C_UF, 128), D_G(b * 16, 8))
                P.mm(pgc(b * 16 + 8, 8), CONST(C_UB, 128), D_G(b * 16 + 8, 8))
                P.mm(pgl(b * 16, 16), CONST(C_SAME, 128), D_G(b * 16, 16))
                for c in range(2):
                    P.mm(peg(b * 32 + c * 16, 16), CONST(C_IND0 + c * 128, 128), D_G(b * 16, 16))
            P.cp("dve", D_GC(0, 160), pgc(0, 160))
            P.ts("dve", D_NEGGC(0, 160), pgc(0, 160), -1.0, ALU.mult)
            P.act(D_NEGEGC(0, 160), pgc(0, 160), AF.Exp)
            P.ts("pool", D_NEGEGC(0, 160), D_NEGEGC(0, 160), -1.0, ALU.mult)
            P.tt("dve", D_KD(0, 160), pgl(0, 160), D_GC(0, 160), ALU.subtract)
            P.act(D_KD(0, 160), D_KD(0, 160), AF.Exp)
            P.act(D_EGL(0, 320), peg(0, 320), AF.Exp)

            def stageA(h):
                slot = W.acquire(("dnh", j, h))
                for part in range(4):
                    for kc in range(8):
                        for ti, (t0, tn) in enumerate(TBS):
                            P.mm(PS[ti](0, tn), slot(kc * 512 + part * 128, 128), H(kc * NT + t0, tn),
                                 start=(kc == 0), stop=(kc == 7))
                    yield
                    if part == 3:
                        for ti, (t0, tn) in enumerate(TBS):
                            P.act(D_ZS[h % 2](t0, tn), PS[ti](0, tn), AF.Silu)
                        yield
                        continue
                    ch = part * 8 + h
                    for tap in range(5):
                        P.ts("pool", D_DG(tap * 128, 128), IDENT(0, 128), D_CW(ch * 5 + tap, 1), ALU.mult)
                    for ti, (t0, tn) in enumerate(TBS):
                        ns = tn // 256
                        P.cp("act" if ti != 1 else "dve", D_PAD.v((t0 // 256) * 260 + 2, [[260, ns], [1, 256]]),
                             PS[ti].v(0, [[256, ns], [1, 256]]))
                    P.ts("dve", D_PAD.v(260, [[260, 3], [1, 2]]), D_PAD.v(256, [[260, 3], [1, 2]]), FLG(0, 1), ALU.mult)
                    P.ts("dve", D_PAD.v(258, [[260, 3], [1, 2]]), D_PAD.v(262, [[260, 3], [1, 2]]), FLG(0, 1), ALU.mult)
                    for ti, (t0, tn) in enumerate(TBS):
                        ns = tn // 256
                        for tap in range(5):
                            P.mm(PS[3 + ti](0, tn), D_DG(tap * 128, 128),
                                 D_PAD.v((t0 // 256) * 260 + tap, [[260, ns], [1, 256]]),
                                 start=(tap == 0), stop=(tap == 4))
                    yield
                    dst = D_VB if part == 2 else D_FT
                    for ti, (t0, tn) in enumerate(TBS):
                        P.act(dst(t0, tn), PS[3 + ti](0, tn), AF.Silu)
                    if part == 2:
                        yield
                        continue
                    P.act(D_SQ(0, NT), D_FT(0, NT), AF.Square)
                    out = D_QNB if part == 0 else D_KNBS[h % 2]
                    for ti, (t0, tn) in enumerate(TBS):
                        P.mm(PS[3 + ti](0, tn), ONESB(0, 128), D_SQ(t0, tn))
                        P.act(D_RSQ(0, tn), PS[3 + ti](0, tn), AF.Sqrt, bias=EPSB(2, 1))
                        P.add("dve", (lambda o: (lambda e: e.reciprocal(o.ap, o.ap)))(D_RSQ(0, tn)),
                              [D_RSQ(0, tn)], [D_RSQ(0, tn)])
                        P.stt("dve", out(t0, tn), D_FT(t0, tn), float(HD ** -0.5) if part == 0 else 1.0,
                              D_RSQ(0, tn), ALU.mult, ALU.mult)
                    yield
                W.release()

            def stageB(h):
                D_KNB = D_KNBS[h % 2]
                D_VTOK, D_KDEC, D_XB, D_QKM, D_QGT = (D_VTOKS[h % 2], D_KDECS[h % 2], D_XBS[h % 2],
                                                      D_QKMS[h % 2], D_QGTS[h % 2])
                for b in range(10):
                    ps = PS[b % 2]
                    P.mm(ps(0, 128), D_KNB(b * 128, 128), IDENTB(0, 128))
                    P.mm(ps(128, 128), D_VB(b * 128, 128), IDENTB(0, 128))
                    for dr in range(2):
                        P.ts("dve", D_KDEC(dr * NT + b * 128, 128), ps(0, 128), D_KD(b * 16 + dr * 8 + h, 1), ALU.mult)
                    P.cp("act", D_VTOK(b * 128, 128), ps(128, 128))
                yield
                m4 = lambda t, c0=0: t.v(c0, [[256, 2], [128, 2], [1, 128]])
                for bp in range(5):
                    pk, pc, pt_, pl = PS[2], PS[3], PS[4], (PS[5], PS[0], PS[1])
                    for bl in range(2):
                        blk = 2 * bp + bl
                        P.mm(pk(bl * 128, 128), D_KNB(blk * 128, 128), D_KNB(blk * 128, 128))
                        P.mm(pk(256 + bl * 128, 128), D_KNB(blk * 128, 128), D_QNB(blk * 128, 128))
                    DIAG, EB, DT = D_NB[0], D_NB[1], D_NB[2]
                    Pa, Za = D_NB[3], D_NB[4]
                    gsel = lambda t: t.v(bp * 32 + h, [[16, 2], [8, 2], [0, 128]])
                    P.tt("pool", m4(DIAG), IDENT.v(0, [[0, 2], [0, 2], [1, 128]]), gsel(D_GC), ALU.mult)
                    for m in range(4):
                        P.mm(pc(m * 128, 128), CONST(C_ONESF, 128), DIAG(m * 128, 128))
                    P.act(EB(0, 512), pc(0, 512), AF.Exp)
                    P.tt("pool", D_QGT.v(2 * bp * 128, [[128, 2], [NT, 2], [1, 128]]),
                         D_QNB.v(2 * bp * 128, [[128, 2], [0, 2], [1, 128]]), m4(EB), ALU.mult)
                    P.tt("dve", m4(DT), m4(pc), CONST.v(C_MADD, [[0, 2], [128, 2], [1, 128]]), ALU.add)
                    for m in range(4):
                        bl, dr = m // 2, m % 2
                        P.act(DT(m * 128, 128), DT(m * 128, 128), AF.Exp,
                              bias=D_NEGGC((2 * bp + bl) * 16 + dr * 8 + h, 1))
                    P.tt("dve", m4(D_QKM, bp * 512), pk.v(256, [[128, 2], [0, 2], [1, 128]]), m4(DT), ALU.mult)
                    P.tt("pool", m4(DT), m4(DT), CONST.v(C_MS, [[0, 2], [128, 2], [1, 128]]), ALU.mult)
                    for m in range(4):
                        bl, dr = m // 2, m % 2
                        P.stt("dve", Pa(m * 128, 128), pk(bl * 128, 128),
                              D_NEGBETA((2 * bp + bl) * 16 + dr * 8 + h, 1), DT(m * 128, 128), ALU.mult, ALU.mult)
                    for m in range(4):
                        P.tr(pt_(m * 128, 128), Pa(m * 128, 128), IDENT(0, 128))
                    P.cp("act", Za(0, 512), pt_(0, 512))
                    yield
                    NBs = D_NB
                    msk = lambda c0: CONST.v(c0, [[0, 4], [1, 128]])
                    f4 = lambda t: t.v(0, [[128, 4], [1, 128]])
                    Pc, Zc = NBs[0], NBs[1]
                    P.tt("pool", f4(Pc), f4(Pa), msk(C_BD16), ALU.mult)
                    P.tt("pool", f4(Zc), f4(Za), msk(C_BD16), ALU.mult)
                    Xc, XTc, Xn, XTn = NBs[5], NBs[6], NBs[8], NBs[9]
                    P.tt("pool", f4(Xc), f4(Pc), msk(0), ALU.add)
                    P.tt("dve", f4(XTc), f4(Zc), msk(0), ALU.add)
                    pz = [(NBs[2], NBs[7]), (NBs[0], NBs[1])]
                    for k in range(1, 4):
                        Pn, Zn = pz[(k - 1) % 2]
                        for m in range(4):
                            P.mm(pl[0](m * 128, 128), Zc(m * 128, 128), Pc(m * 128, 128))
                        for m in range(4):
                            P.mm(pl[1](m * 128, 128), Pc(m * 128, 128), Zc(m * 128, 128))
                        P.cp("act", Pn(0, 512), pl[0](0, 512))
                        P.cp("dve", Zn(0, 512), pl[1](0, 512))
                        yield
                        for m in range(4):
                            P.mm(pl[2](m * 128, 128), Zn(m * 128, 128), Xc(m * 128, 128))
                        for m in range(4):
                            P.mm(pt_(m * 128, 128), Pn(m * 128, 128), XTc(m * 128, 128))
                        P.tt("dve", Xn(0, 512), pl[2](0, 512), Xc(0, 512), ALU.add)
                        P.tt("dve", XTn(0, 512), pt_(0, 512), XTc(0, 512), ALU.add)
                        Pc, Zc = Pn, Zn
                        Xc, XTc, Xn, XTn = Xn, XTn, Xc, XTc
                        yield
                    for lv, cmask in ((1, C_OFF1), (2, C_OFF2), (3, C_OFF3)):
                        Nl, NlT, T1, T1p = NBs[0], NBs[1], NBs[2], NBs[7]
                        P.tt("pool", f4(Nl), f4(Pa), msk(cmask), ALU.mult)
                        P.tt("pool", f4(NlT), f4(Za), msk(cmask), ALU.mult)
                        for m in range(4):
                            P.mm(pl[0](m * 128, 128), NlT(m * 128, 128), Xc(m * 128, 128))
                        P.cp("act", T1(0, 512), pl[0](0, 512))
                        if lv < 3:
                            for m in range(4):
                                P.mm(pl[1](m * 128, 128), Nl(m * 128, 128), XTc(m * 128, 128))
                            P.cp("dve", T1p(0, 512), pl[1](0, 512))
                        yield
                        for m in range(4):
                            P.mm(pl[2](m * 128, 128), XTc(m * 128, 128), T1(m * 128, 128))
                        if lv < 3:
                            for m in range(4):
                                P.mm(pt_(m * 128, 128), Xc(m * 128, 128), T1p(m * 128, 128))
                            P.tt("dve", Xn(0, 512), pl[2](0, 512), Xc(0, 512), ALU.add)
                            P.tt("dve", XTn(0, 512), pt_(0, 512), XTc(0, 512), ALU.add)
                            Xc, XTc, Xn, XTn = Xn, XTn, Xc, XTc
                        else:
                            P.tt("dve", D_XB(bp * 512, 512), pl[2](0, 512), Xc(0, 512), ALU.add)
                        yield

            def chain(h, dr):
                D_KNB = D_KNBS[h % 2]
                D_VTOK, D_KDEC, D_XB, D_QKM, D_QGT = (D_VTOKS[h % 2], D_KDECS[h % 2], D_XBS[h % 2],
                                                      D_QKMS[h % 2], D_QGTS[h % 2])
                SF, SB_, RB, VNB = D_SF[dr], D_SB[dr], D_RB[dr], D_VNB[dr]
                ps = PS[6 + dr]
                gcol = dr * 8 + h
                if dr == 0:
                    P.dma("sp", SF(0, 128), dreg(s0_in, ((j * 16) + gcol) * 16384, [[128, 128], [1, 128]]), "dS0_%d" % dr)
                else:
                    P.memset("dve", SF(0, 128), 0.0)
                P.cp("act", SB_(0, 128), SF(0, 128))
                pr, pv, po, pss = ps(0, 128), ps(128, 128), ps(256, 128), ps(384, 128)
                order = list(range(10)) if dr == 0 else list(range(9, -1, -1))
                nst = 0
                for b in order:
                    bp, m = b // 2, (b % 2) * 2 + dr
                    seg = b // 2
                    P.mm(pr, D_KNB(b * 128, 128), SB_(0, 128))
                    P.stt("dve", RB(0, 128), pr, D_NEGEGC(b * 16 + gcol, 1), D_VTOK(b * 128, 128),
                          ALU.mult, ALU.add)
                    yield
                    P.mm(pv, D_XB(bp * 512 + m * 128, 128), RB(0, 128))
                    P.ts("dve", VNB(0, 128), pv, D_BETA(b * 16 + gcol, 1), ALU.mult)
                    yield
                    P.mm(po, SB_(0, 128), D_QGT(dr * NT + b * 128, 128), start=True, stop=False)
                    P.mm(po, VNB(0, 128), D_QKM(bp * 512 + m * 128, 128), start=False, stop=True)
                    P.cp("act", D_OD(dr * NT + b * 128, 128), po)
                    yield
                    P.mm(pss, D_KDEC(dr * NT + b * 128, 128), VNB(0, 128))
                    P.stt("dve", SF(0, 128), SF(0, 128), D_EGL(b * 32 + gcol, 1), pss, ALU.mult, ALU.add)
                    last = (b % 2 == 1) if dr == 0 else (b % 2 == 0)
                    if last:
                        stg = D_STG[dr]
                        nst += 1
                        P.cp("act", stg(0, 128), SF(0, 128))
                        P.dma("sp", dreg(s_out, ((((j * 5 + seg) * 2 + dr) * 8 + h) * 16384), [[128, 128], [1, 128]]),
                              stg(0, 128), "dSO%d" % dr)
                        if dr == 0:
                            if seg < 3:
                                P.ts("dve", SF(0, 128), SF(0, 128), FLG(0, 1), ALU.mult)
                            elif seg == 3:
                                P.memset("dve", SF(0, 128), 0.0)
                        else:
                            if seg == 4:
                                P.dma("sp", SF(0, 128), dreg(s0_in, ((j * 16) + gcol) * 16384, [[128, 128], [1, 128]]),
                                      "dS0_%d" % dr)
                            elif seg >= 1:
                                P.ts("dve", SF(0, 128), SF(0, 128), FLG(0, 1), ALU.mult)
                    P.cp("act", SB_(0, 128), SF(0, 128))
                    yield

            def finalize(h):
                OS, TM = D_NB[0], D_NB[3]
                for ti, (t0, tn) in enumerate(TBS):
                    P.tt("pool", OS(0, tn), D_OD(t0, tn), D_OD(NT + t0, tn), ALU.add)
                    P.act(D_SQF(0, tn), OS(0, tn), AF.Square)
                    P.mm(PS[ti](0, tn), ONESB(0, 128), D_SQF(0, tn))
                    P.act(D_RSQ(0, tn), PS[ti](0, tn), AF.Sqrt, bias=EPSB(2, 1), scale=1.0 / HD)
                    P.add("dve", (lambda o: (lambda e: e.reciprocal(o.ap, o.ap)))(D_RSQ(0, tn)),
                          [D_RSQ(0, tn)], [D_RSQ(0, tn)])
                    P.stt("dve", TM(0, tn), OS(0, tn), D_GP(32, 1), D_RSQ(0, tn), ALU.mult, ALU.mult)
                    P.tt("pool", O(h * NT + t0, tn), TM(0, tn), D_ZS[h % 2](t0, tn), ALU.mult)

            def seq(*gens):
                for g in gens:
                    if g is not None:
                        for _ in g:
                            yield

            def interleave(chains, side, ratio):
                chains = list(chains)
                while chains or side is not None:
                    for g in list(chains):
                        try:
                            next(g)
                        except StopIteration:
                            chains.remove(g)
                    if side is not None:
                        for _ in range(ratio if chains else 10 ** 9):
                            try:
                                next(side)
                            except StopIteration:
                                side = None
                                break

            if DBG_STAGE == 10:
                return
            run_all(stageA(0))
            run_all(stageB(0))
            for h in range(8):
                side = seq(stageA(h + 1), stageB(h + 1)) if h < 7 else None
                interleave([chain(h, 0), chain(h, 1)], side, DN_RATIO)
                finalize(h)
            for t in range(2):
                slot = W.acquire(("do", j, t))
                for jj in range(4):
                    oc = t * 4 + jj
                    base = (oc % 2) * 3
                    for kc in range(8):
                        for ti, (t0, tn) in enumerate(TBS):
                            P.mm(PS[base + ti](0, tn), slot(kc * 512 + jj * 128, 128), O(kc * NT + t0, tn),
                                 start=(kc == 0), stop=(kc == 7))
                    gated_residual(par, 2, oc, [PS[base + ti] for ti in range(3)])
                W.release()

        run_all(modulation(0, 0))
        for l in range(nlayers):
            par = l % 2
            if mixers:
                if l % 2 == 0:
                    attention(l, par)
                else:
                    deltanet(l, par)
            side = modulation(l + 1, 1 - par) if l + 1 < nlayers else None
            mlp(l, par, side)

        rms_rstd(lambda c: X(c * NT, NT))
        for c in range(8):
            tmp = TMPN[c % 2]
            P.tt("dve", tmp(0, NT), X(c * NT, NT), RSTD(0, NT), ALU.mult)
            P.act(X(c * NT, NT), tmp(0, NT), AF.Copy, scale=GAINS(72 + c, 1))
        for b in range(10):
            yt = XTOK[b % 2]
            for half in range(2):
                ps = PS[(2 * b + half) % 4]
                for j in range(4):
                    c = half * 4 + j
                    P.tr(ps(j * 128, 128), X(c * NT + b * 128, 128), IDENT(0, 128))
                P.cp("dve" if half == 0 else "act", yt(half * 512, 512), ps(0, 512))
            P.dma("sp", dreg(y_out, b * 128 * D, [[D, 128], [1, D]]), yt(0, 1024), "yout%d" % (b % 2))

        if not wdescs_only:
            P.finalize()
    return nc, P, W.descs


def _tile_kc8(Wm, n0):
    blk = Wm[:, n0:n0 + 512].reshape(8, 128, 512)
    return np.ascontiguousarray(blk.transpose(1, 0, 2)).reshape(128, 4096)


def _tile_w2(Wm, oc):
    blk = Wm[:, oc * 128:(oc + 1) * 128].reshape(32, 128, 128)
    return np.ascontiguousarray(blk.transpose(1, 0, 2)).reshape(128, 4096)


def make_tiles(descs, inp):
    tiles = np.empty((len(descs), 128, 4096), np.float32)
    for i, d in enumerate(descs):
        kind, l = d[0], d[1]
        if kind == "mod":
            tiles[i] = _tile_kc8(inp["w_mod"][l], d[2] * 512)
        elif kind == "aq":
            tiles[i] = _tile_kc8(inp["attn_w_in"][l], d[2] * 512)
        elif kind == "akv":
            tiles[i] = _tile_kc8(inp["attn_w_in"][l], 1024)
        elif kind == "ao":
            tiles[i] = _tile_kc8(inp["attn_w_out"][l], d[2] * 512)
        elif kind == "dnh":
            h = d[2]
            Wm = inp["dn_w_in"][l]
            cols = np.concatenate([Wm[:, p * 1024 + h * 128: p * 1024 + (h + 1) * 128] for p in range(4)], axis=1)
            tiles[i] = _tile_kc8(cols, 0)
        elif kind == "do":
            tiles[i] = _tile_kc8(inp["dn_w_out"][l], d[2] * 512)
        elif kind == "w1":
            tiles[i] = _tile_kc8(inp["mlp_w1"][l], d[2] * 512)
        elif kind == "w2":
            tiles[i] = _tile_w2(inp["mlp_w2"][l], d[2])
        else:
            raise ValueError(d)
    return tiles


_CACHE = {}


def _get_program(nlayers=DEPTH, mixers=True):
    key = (nlayers, mixers)
    if key not in _CACHE:
        _, _, descs = build(nlayers, mixers, wdescs_only=True)
        nc, P, descs2 = build(nlayers, mixers, ntiles=len(descs))
        assert descs == descs2
        _CACHE[key] = (nc, P, descs)
    return _CACHE[key]


def _rope_tables():
    t = np.arange(NG)
    row = (t // 64).astype(np.float32)
    col = (t % 64).astype(np.float32)
    inv = (10000.0 ** (-np.arange(32, dtype=np.float32) / 32)).astype(np.float32)
    d = np.arange(128)
    pos = np.where((d // 64)[:, None] == 0, row[None, :], col[None, :]).astype(np.float32)
    ang = pos * inv[d % 32][:, None]
    return np.stack([np.cos(ang), np.sin(ang)]).astype(np.float32)


def _rott():
    R = np.zeros((128, 128), np.float32)
    for d in range(128):
        g, r = d // 64, d % 64
        if r < 32:
            R[d, g * 64 + r + 32] = -1.0
        else:
            R[d, g * 64 + r - 32] = 1.0
    return np.ascontiguousarray(R.T)


def core_inputs(inp, core, descs_tiles):
    xp, xs = inp["x_prompt"], inp["x_sample"]
    sample = core >= 6
    if not sample:
        xg = xp[4 * core:4 * core + 4].reshape(NG, D)
        condg = inp["c_ctx"]
        rope = np.stack([np.ones((128, NG), np.float32), np.zeros((128, NG), np.float32)])
        ck = np.zeros((2, 256, 256), np.float32)
        cv = np.zeros((2, 256, 256), np.float32)
        s0 = np.zeros((2, 16, 128, 128), np.float32)
        flag = 0.0
        mb = np.full((5, 4), -30000.0, np.float32)
        for q in range(4):
            mb[1 + q, q] = 0.0
    else:
        b = core - 6
        xg = xs[b]
        condg = inp["c"][b]
        rope = _rope_tables()
        ck = inp["cache_k"][b].reshape(2, 256, 256)
        cv = inp["cache_v"][b].reshape(2, 256, 256)
        mb = np.zeros((5, 4), np.float32)
        s0 = inp["state_delta"][b].reshape(2, 16, 128, 128)
        flag = 1.0
    xe = xp[24 + core]
    x_in = np.concatenate([xg, xe], axis=0)
    cond = np.stack([condg, inp["c_ctx"]], axis=0)
    cond_in = np.ascontiguousarray(cond.reshape(2, 8, 128).transpose(2, 1, 0)).reshape(128, 16)
    m = {"x_in": np.ascontiguousarray(x_in, dtype=np.float32), "cond_in": cond_in.astype(np.float32),
         "rope_in": rope, "ck_in": np.ascontiguousarray(ck, dtype=np.float32),
         "cv_in": np.ascontiguousarray(cv, dtype=np.float32),
         "maskb_in": np.ascontiguousarray(np.broadcast_to(mb.reshape(1, 20), (128, 20))),
         "s0_in": np.ascontiguousarray(s0, dtype=np.float32),
         "flags_in": np.full((128, 16), flag, np.float32)}
    for i in range((len(descs_tiles) + WCH - 1) // WCH):
        m["wts%d" % i] = descs_tiles[i * WCH:(i + 1) * WCH]
    return m


def _consts():
    c = np.zeros((128, NCONST), np.float32)
    j = np.arange(128)[:, None]
    i = np.arange(128)[None, :]
    c[:, 0:128] = np.eye(128)
    c[:, 128:256] = np.where(i >= j, 0.0, -60000.0)
    c[:, 256:384] = np.where(i <= j, 0.0, -60000.0)
    c[:, 384:512] = (i > j)
    c[:, 512:640] = (i < j)
    c[:, 640:768] = (j <= i)
    c[:, 768:896] = (j >= i)
    c[:, 896:1024] = 1.0
    c[:, 1024:1152] = 1.0
    c[:, 1152:1280] = 1.0
    c[:, 1280:1408] = 1.0
    c[:, 1408:1536] = (j // 16) == (i // 16)
    c[:, 1536:1664] = ((j // 32) == (i // 32)) & ((j // 16) != (i // 16))
    c[:, 1664:1792] = ((j // 64) == (i // 64)) & ((j // 32) != (i // 32))
    c[:, 1792:1920] = ((j // 64) != (i // 64))
    return c


def shared_inputs(inp):
    bmod = np.ascontiguousarray(inp["b_mod"].reshape(DEPTH, 48, 128).transpose(0, 2, 1))
    g = np.concatenate([
        inp["norm1"].reshape(DEPTH * 8, 128).T, inp["norm2"].reshape(DEPTH * 8, 128).T,
        inp["final_norm"].reshape(8, 128).T], axis=1)
    again = np.zeros((2, 128, 256), np.float32)
    for j in range(2):
        again[j, :, 128:] = np.broadcast_to(inp["attn_q_gain"][j][:, None], (128, 128))
        again[j, :, :128] = np.broadcast_to(inp["attn_k_gain"][j][None, :], (128, 128))
    dnab = np.ascontiguousarray(
        inp["dn_w_in"][:, :, 4096:4128].reshape(2, 8, 128, 32).transpose(0, 2, 1, 3)).reshape(2, 128, 256)
    convw = np.ascontiguousarray(
        inp["dn_conv"].reshape(2, 5, 24, 128).transpose(0, 3, 2, 1)).reshape(2, 128, 120)
    dngp = np.zeros((2, 128, 33), np.float32)
    for j in range(2):
        dngp[j, :, 0:16] = np.broadcast_to(inp["dn_a_log"][j].reshape(1, 16), (128, 16))
        dngp[j, :, 16:32] = np.broadcast_to(inp["dn_dt_bias"][j].reshape(1, 16), (128, 16))
        dngp[j, :, 32] = inp["dn_norm"][j]
    return {"bmod_in": bmod.astype(np.float32), "gains_in": np.ascontiguousarray(g, dtype=np.float32),
            "consts_in": _consts(), "rott_in": _rott(), "again_in": again,
            "dnab_in": dnab.astype(np.float32), "convw_in": convw.astype(np.float32), "dngp_in": dngp}


def kernel(**inp):
    inp = {k: np.asarray(v) for k, v in inp.items()}
    nc, P, descs = _get_program()
    tiles = make_tiles(descs, inp)
    sh = shared_inputs(inp)
    in_maps = []
    for core in range(8):
        m = core_inputs(inp, core, tiles)
        m.update(sh)
        in_maps.append(m)
    res = run_bass_kernel_spmd(nc, in_maps, core_ids=list(range(8))).results
    B, S = 32, 256
    y_prompt = np.empty((B, S, D), np.float32)
    y_sample = np.empty((2, NG, D), np.float32)
    nk = np.empty((B, 2, S, 2, 128), np.float32)
    nv = np.empty((B, 2, S, 2, 128), np.float32)
    ns = np.empty((B, 2, 2, 8, 128, 128), np.float32)
    for core in range(8):
        r = res[core]
        y, ko, vo, so = r["y_out"], r["k_out"], r["v_out"], r["s_out"]
        seqs = []
        if core < 6:
            for q in range(4):
                seqs.append((4 * core + q, q))
        else:
            y_sample[core - 6] = y[:NG]
        seqs.append((24 + core, 4))
        for (b, seg) in seqs:
            y_prompt[b] = y[seg * 256:(seg + 1) * 256]
            for j in range(2):
                nk[b, j] = ko[j, seg * 256:(seg + 1) * 256].reshape(S, 2, 128)
                nv[b, j] = vo[j, seg * 256:(seg + 1) * 256].reshape(S, 2, 128)
                ns[b, j] = so[j, seg]
    return (y_prompt, y_sample, nk, nv, ns)
```

```python
import numpy as np
from contextlib import ExitStack
import concourse.bass as bass
import concourse.mybir as mybir
from concourse.bass_utils import run_bass_kernel_spmd

F32 = mybir.dt.float32
BF16 = mybir.dt.bfloat16
AF = mybir.ActivationFunctionType
ALU = mybir.AluOpType

D = 1024
NT = 1280
NG = 1024
NE = 256
DEPTH = 4
DFF = 4096
HD = 128
EPS = 1e-6
TBS = [(0, 512), (512, 512), (1024, 256)]
NSLOT = 2
DN_RATIO = 2
NCONST = 1920
WCH = 16
DBG_STAGE = 99
DBG_SUB = 99
DBG_DUMP = False
DBG_NOPRO = False
SB_BASE = 16384 + 256
SB_END = 229376
KEYG = 256


class Reg:
    __slots__ = ("ap", "keys")

    def __init__(self, ap, keys):
        self.ap = ap
        self.keys = keys


class T:
    def __init__(self, P, name, F, dtype, space="sb", addr=None, handle=None):
        self.P, self.name, self.F, self.dtype, self.space = P, name, F, dtype, space
        self.esz = 4 if dtype == F32 else 2
        self.addr = addr
        self.h = handle

    def v(self, col, dims, p0=0, n=128):
        ap = bass.AP(self.h, p0 * self.F + col, [[self.F, n]] + [list(d) for d in dims])
        span = 1 + sum((c - 1) * s for s, c in dims)
        if self.space == "sb":
            lo = self.addr + col * self.esz
            hi = self.addr + (col + span) * self.esz
            keys = range(lo // KEYG, (hi - 1) // KEYG + 1)
        elif self.space == "ps":
            keys = (self.name,)
        else:
            keys = ()
        return Reg(ap, keys)

    def __call__(self, col=0, n=None, p0=0, np_=128):
        if n is None:
            n = self.F - col
        return self.v(col, [[1, n]], p0, np_)


class Prog:
    ENG = ("pe", "act", "dve", "pool", "sp")

    def __init__(self, nc):
        self.nc = nc
        self.ops = []
        self.sb_ptr = SB_BASE
        self.dram_in = {}
        self.dram_out = {}

    def sb(self, name, F, dtype, addr=None):
        esz = 4 if dtype == F32 else 2
        if addr is None:
            addr = self.sb_ptr
            self.sb_ptr += (F * esz + 63) // 64 * 64
            assert self.sb_ptr <= SB_END, ("SBUF overflow", name, self.sb_ptr)
        h = self.nc.alloc_sbuf_tensor_at(name, [128, F], dtype, offset=addr)
        return T(self, name, F, dtype, "sb", addr, h)

    def din(self, name, shape, dtype=F32):
        h = self.nc.dram_tensor(name, list(shape), dtype, kind="ExternalInput")
        self.dram_in[name] = (tuple(shape), dtype)
        return h

    def dout(self, name, shape, dtype=F32):
        h = self.nc.dram_tensor(name, list(shape), dtype, kind="ExternalOutput")
        self.dram_out[name] = (tuple(shape), dtype)
        return h

    def add(self, eng, fn, reads, writes, dma_key=None):
        rk = []
        for r in reads:
            if r is not None and isinstance(r, Reg):
                rk.extend(r.keys)
        wk = []
        for w in writes:
            wk.extend(w.keys)
        if eng != "pe":
            for r in reads:
                if r is not None and isinstance(r, Reg) and len(r.keys) and isinstance(r.keys[0], str):
                    wk.extend(r.keys)
        self.ops.append((eng, fn, rk, wk, dma_key))

    def finalize(self):
        nc = self.nc
        with ExitStack() as es:
            sems = {e: es.enter_context(nc.semaphore("s_" + e)) for e in ("pe", "act", "dve", "pool")}
            dsem = {}
            cnt = {e: 0 for e in sems}
            dcnt = {}
            known = {e: {} for e in self.ENG}
            last_w = {}
            readers = {}
            per_eng = {e: [] for e in self.ENG}
            for (eng, fn, rk, wk, dkey) in self.ops:
                need = {}

                def req(tok, kind):
                    name, val, teng = tok
                    if teng == eng and dkey is None:
                        if eng == "pe":
                            return
                    if known[eng].get(name, 0) >= val:
                        return
                    if need.get(name, 0) < val:
                        need[name] = val

                for k in rk:
                    t = last_w.get(k)
                    if t is not None:
                        req(t, "raw")
                for k in wk:
                    t = last_w.get(k)
                    if t is not None:
                        req(t, "waw")
                    for t in readers.get(k, ()):
                        req(t, "war")
                for name, val in need.items():
                    known[eng][name] = val
                if dkey is not None:
                    if dkey not in dsem:
                        dsem[dkey] = es.enter_context(nc.semaphore("d_" + dkey))
                        dcnt[dkey] = 0
                    dcnt[dkey] += 16
                    tok = ("d_" + dkey, dcnt[dkey], "dma")
                    inc = (dsem[dkey], 16)
                else:
                    cnt[eng] += 1
                    tok = (eng, cnt[eng], eng)
                    inc = (sems[eng], 1)
                for k in wk:
                    last_w[k] = tok
                    readers[k] = []
                for k in rk:
                    lst = readers.setdefault(k, [])
                    if not lst or lst[-1] != tok:
                        lst.append(tok)
                        if len(lst) > 24:
                            best = {}
                            for (a, b, c) in lst:
                                if best.get(a, (0, None))[0] < b:
                                    best[a] = (b, c)
                            lst[:] = [(a, b, c) for a, (b, c) in best.items()]
                per_eng[eng].append((fn, list(need.items()), inc))
            allsem = dict(sems)
            for k, s in dsem.items():
                allsem["d_" + k] = s
            final_waits = [(("d_" + k), v) for k, v in dcnt.items()]
            self.n_ops = {e: len(v) for e, v in per_eng.items()}
            with nc.Block() as block:
                def runner(ename):
                    def body(e):
                        for fn, waits, inc in per_eng[ename]:
                            for sname, val in waits:
                                e.wait_ge(allsem[sname], val)
                            ins = fn(e)
                            ins.then_inc(inc[0], inc[1])
                        if ename == "sp":
                            for sname, val in final_waits:
                                e.wait_ge(allsem[sname], val)
                            for en2 in ("pe", "act", "dve", "pool"):
                                if cnt[en2]:
                                    e.wait_ge(sems[en2], cnt[en2])
                    return body
                block.tensor(runner("pe"))
                block.scalar(runner("act"))
                block.vector(runner("dve"))
                block.gpsimd(runner("pool"))
                block.sync(runner("sp"))

    def mm(self, out, lhsT, rhs, start=True, stop=True):
        self.add("pe", lambda e: e.matmul(out.ap, lhsT.ap, rhs.ap, start=start, stop=stop),
                 [lhsT, rhs] + ([] if start else [out]), [out])

    def tr(self, out, in_, ident):
        self.add("pe", lambda e: e.transpose(out.ap, in_.ap, ident.ap), [in_, ident], [out])

    def act(self, out, in_, func, bias=0.0, scale=1.0, accum=None):
        b = bias.ap if isinstance(bias, Reg) else bias
        s = scale.ap if isinstance(scale, Reg) else scale
        if accum is None:
            fn = lambda e: e.activation(out.ap, in_.ap, func, bias=b, scale=s)
            w = [out]
        else:
            fn = lambda e: e.activation(out.ap, in_.ap, func, bias=b, scale=s, accum_out=accum.ap)
            w = [out, accum]
        self.add("act", fn, [in_, bias, scale], w)

    def tt(self, eng, out, in0, in1, op):
        self.add(eng, lambda e: e.tensor_tensor(out.ap, in0.ap, in1.ap, op), [in0, in1], [out])

    def ts(self, eng, out, in0, s1, op0, s2=None, op1=None):
        a = s1.ap if isinstance(s1, Reg) else s1
        b = s2.ap if isinstance(s2, Reg) else s2
        if op1 is None:
            fn = lambda e: e.tensor_scalar(out.ap, in0.ap, a, None, op0)
        else:
            fn = lambda e: e.tensor_scalar(out.ap, in0.ap, a, b, op0, op1)
        self.add(eng, fn, [in0, s1, s2], [out])

    def stt(self, eng, out, in0, sc, in1, op0, op1):
        a = sc.ap if isinstance(sc, Reg) else sc
        self.add(eng, lambda e: e.scalar_tensor_tensor(out.ap, in0.ap, a, in1.ap, op0, op1),
                 [in0, sc, in1], [out])

    def cp(self, eng, out, in_):
        if eng == "act":
            self.add("act", lambda e: e.copy(out.ap, in_.ap), [in_], [out])
        else:
            self.add(eng, lambda e: e.tensor_copy(out.ap, in_.ap), [in_], [out])

    def memset(self, eng, out, val):
        self.add(eng, lambda e: e.memset(out.ap, val), [], [out])

    def dma(self, q, out, in_, key):
        self.add(q, lambda e: e.dma_start(out=out.ap, in_=in_.ap), [in_], [out], dma_key=key)


def dreg(h, off, dims):
    return Reg(bass.AP(h, off, [list(d) for d in dims]), ())


class WStream:
    def __init__(self, P, ntiles_hint=None):
        self.P = P
        self.slots = [P.sb("wslot%d" % i, 4096, BF16) for i in range(NSLOT)]
        self.descs = []
        self.issued = 0
        self.h = None

    def bind(self, h, ntiles):
        self.h = h
        self.ntiles = ntiles

    def _issue(self):
        t = self.issued
        if self.h is not None and t < self.ntiles:
            s = self.slots[t % NSLOT]
            self.P.dma("pool", s(0, 4096), dreg(self.h[t // WCH], (t % WCH) * 128 * 4096, [[4096, 128], [1, 4096]]),
                       key="w%d" % (t % NSLOT))
        self.issued += 1

    def start(self):
        for _ in range(NSLOT):
            self._issue()

    def acquire(self, desc):
        t = len(self.descs)
        self.descs.append(desc)
        return self.slots[t % NSLOT]

    def release(self):
        self._issue()


def build(nlayers=DEPTH, mixers=True, wdescs_only=False, ntiles=0, dbg=None):
    nc = bass.Bass("TRN2", target_bir_lowering=False)
    P = Prog(nc)
    x_in = P.din("x_in", [NT, D])
    cond_in = P.din("cond_in", [128, 16])
    bmod_in = P.din("bmod_in", [DEPTH, 128, 48])
    gains_in = P.din("gains_in", [128, 72])
    consts_in = P.din("consts_in", [128, NCONST])
    flags_in = P.din("flags_in", [128, 16])
    if nlayers >= 2 and mixers:
        dnab_in = P.din("dnab_in", [2, 128, 256])
        convw_in = P.din("convw_in", [2, 128, 120])
        dngp_in = P.din("dngp_in", [2, 128, 33])
        s0_in = P.din("s0_in", [2, 16, 128, 128])
        s_out = P.dout("s_out", [2, 5, 2, 8, 128, 128])
    dbg_out = P.dout("dbg_out", [8, 128, NT]) if DBG_DUMP else None
    wts = [P.din("wts%d" % i, [min(WCH, max(ntiles, 1) - i * WCH), 128, 4096])
           for i in range((max(ntiles, 1) + WCH - 1) // WCH)]
    y_out = P.dout("y_out", [NT, D])
    rope_in = P.din("rope_in", [2, 128, NG])
    rott_in = P.din("rott_in", [128, 128])
    ck_in = P.din("ck_in", [2, 256, 256])
    cv_in = P.din("cv_in", [2, 256, 256])
    maskb_in = P.din("maskb_in", [128, 20])
    again_in = P.din("again_in", [2, 128, 256])
    k_out = P.dout("k_out", [2, NT, 256])
    v_out = P.dout("v_out", [2, NT, 256])

    W = WStream(P)
    W.bind(wts, ntiles)

    X = P.sb("X", 8 * NT, F32)
    H = P.sb("H", 8 * NT, BF16)
    CONST = P.sb("CONST", NCONST, F32)
    IDENT = P.sb("IDENT", 128, F32)
    FLG = P.sb("FLG", 16, F32)
    ONESB = P.sb("ONESB", 128, BF16)
    IDENTB = P.sb("IDENTB", 128, BF16)
    EPSB = P.sb("EPSB", 4, F32)
    ROTT = P.sb("ROTT", 128, BF16)
    CONDT = P.sb("CONDT", 16, F32)
    SC = P.sb("SC", 16, BF16)
    GAINS = P.sb("GAINS", 80, F32)
    BMOD = P.sb("BMOD", 48, F32)
    MODS = [P.sb("MOD%d" % i, 96, F32) for i in range(2)]
    AB = [P.sb("AB%d" % i, 32, F32) for i in range(2)]
    P.sb_ptr = (P.sb_ptr + 255) // 256 * 256
    scr0 = P.sb_ptr
    SCR_SZ = SB_END - scr0
    assert SCR_SZ >= 102400, SCR_SZ

    def scr(name, F, dtype, off):
        esz = 4 if dtype == F32 else 2
        assert off + F * esz <= SCR_SZ, (name, off, F * esz, SCR_SZ)
        return P.sb(name, F, dtype, addr=scr0 + off)

    HID = scr("HID", 32 * NT, BF16, 0)
    RSTD = scr("RSTD", NT, F32, 81920)
    SQ = [scr("SQ%d" % i, NT, BF16, 81920 + 5120 + i * 2560) for i in range(2)]
    TMPN = [scr("TMPN%d" % i, NT, F32, 81920 + 10240 + i * 5120) for i in range(2)]
    XTOK = [scr("XTOK%d" % i, 1024, F32, i * 4096) for i in range(2)]
    O = scr("O", 8 * NT, BF16, 81920)
    QT = scr("QT", 8 * NT, BF16, 0)
    KT = scr("KT", 2 * 1536, BF16, 20480)
    VT = scr("VT", 12 * 256, BF16, 26624)
    COS = scr("COS", NG, F32, 32768)
    SIN = scr("SIN", NG, F32, 36864)
    QRAW = [scr("QRAW%d" % i, NT, F32, 40960 + i * 5120) for i in range(2)]
    QN = scr("QN", NT, F32, 51200)
    QNB = scr("QNB", NT, BF16, 56320)
    T1 = [scr("T1_%d" % i, 512, F32, 58880 + i * 2048) for i in range(2)]
    T2 = [scr("T2_%d" % i, 512, F32, 62976 + i * 2048) for i in range(2)]
    PT = [scr("PT%d" % i, 512, BF16, 67072 + i * 1024) for i in range(4)]
    KTOK = [scr("KTOK%d" % i, 256, F32, 71168 + i * 1024) for i in range(2)]
    VSTG = [scr("VSTG%d" % i, 256, F32, 73216 + i * 1024) for i in range(2)]
    CKV = scr("CKV", 512, F32, 75264)
    MASKB = scr("MASKB", 32, F32, 77312)
    AGN = scr("AGN", 129, F32, 77440)
    KSS = scr("KSS", 16, F32, 78016)
    REC = scr("REC", 512, F32, 78080)
    AGN2 = scr("AGN2", 256, F32, 102400)
    KGB = AGN2
    KNB = scr("KNB", 256, BF16, 62976)
    KNF = scr("KNF", 256, F32, 62976 + 512)
    if nlayers >= 2 and mixers:
        _o = [0]

        def dscr(name, F, dtype):
            esz = 4 if dtype == F32 else 2
            t = scr(name, F, dtype, _o[0])
            _o[0] += (F * esz + 63) // 64 * 64
            assert _o[0] <= 81920, (name, _o[0])
            P.dscr_used = _o[0]
            return t
        D_NB = [dscr("D_NB%d" % i, 512, F32) for i in range(10)]
        D_FT = P.sb("D_FT", NT, F32, addr=D_NB[5].addr)
        D_G = P.sb("D_G", 160, F32, addr=D_NB[9].addr)
        D_TG = P.sb("D_TG", 160, F32, addr=D_NB[9].addr + 640)
        D_GLO = P.sb("D_GLO", 160, F32, addr=D_NB[9].addr + 1280)
        D_QNB = dscr("D_QNB", NT, BF16)
        D_KNBS = [dscr("D_KNB%d" % i, NT, BF16) for i in range(2)]
        D_VB = dscr("D_VB", NT, BF16)
        D_ZS = [dscr("D_ZS%d" % i, NT, BF16) for i in range(2)]
        D_PAD = dscr("D_PAD", 1408, BF16)
        D_DG = dscr("D_DG", 5 * 128, BF16)
        D_SQ = P.sb("D_SQ", NT, BF16, addr=D_VB.addr)
        D_SQF = P.sb("D_SQF", 512, BF16, addr=D_NB[4].addr)
        D_RSQ = P.sb("D_RSQ", 512, F32, addr=D_NB[8].addr)
        _top = 103424
        D_VTOKS = [dscr("D_VTOK0", NT, BF16), dscr("D_VTOK1", NT, BF16)]
        D_KDECS = [dscr("D_KDEC0", 2 * NT, BF16), scr("D_KDEC1", 2 * NT, BF16, _top)]
        D_XBS = [dscr("D_XB0", 5 * 512, BF16), scr("D_XB1", 5 * 512, BF16, _top + 5120)]
        D_QKMS = [dscr("D_QKM0", 5 * 512, BF16), scr("D_QKM1", 5 * 512, BF16, _top + 10240)]
        D_QGTS = [dscr("D_QGT0", 2 * NT, BF16), scr("D_QGT1", 2 * NT, BF16, _top + 15360)]
        D_OD = dscr("D_OD", 2 * NT, BF16)
        D_SF = [dscr("D_SF%d" % i, 128, F32) for i in range(2)]
        D_SB = [dscr("D_SB%d" % i, 128, BF16) for i in range(2)]
        D_RB = [dscr("D_RB%d" % i, 128, BF16) for i in range(2)]
        D_VNB = [dscr("D_VNB%d" % i, 128, BF16) for i in range(2)]
        D_STG = [dscr("D_STG%d" % i, 128, F32) for i in range(2)]
        D_GC = dscr("D_GC", 160, F32)
        D_NEGGC = dscr("D_NEGGC", 160, F32)
        D_NEGEGC = dscr("D_NEGEGC", 160, F32)
        D_KD = dscr("D_KD", 160, F32)
        D_BETA = dscr("D_BETA", 160, F32)
        D_NEGBETA = dscr("D_NEGBETA", 160, F32)
        D_EGL = dscr("D_EGL", 320, F32)
        D_GP = dscr("D_GP", 33, F32)
        D_EA = dscr("D_EA", 16, F32)
        D_WAB = dscr("D_WAB", 256, BF16)
        D_CW = dscr("D_CW", 120, F32)


    with ExitStack() as es:
        PS = []
        for i in range(8):
            h = es.enter_context(nc.psum_tensor("ps%d" % i, [128, 512], F32))
            PS.append(T(P, "ps%d" % i, 512, F32, "ps", None, h))

        P.dma("sp", IDENT(0, 128), dreg(consts_in, 0, [[NCONST, 128], [1, 128]]), "cI")
        if not DBG_NOPRO:
            for ci in range(0, NCONST, 512):
                cw = min(512, NCONST - ci)
                P.dma("sp", CONST(ci, cw), dreg(consts_in, ci, [[NCONST, 128], [1, cw]]), "cI%d" % ci)
        if not DBG_NOPRO:
            P.dma("sp", FLG(0, 16), dreg(flags_in, 0, [[16, 128], [1, 16]]), "cF")
            P.memset("dve", EPSB(3, 1), 1.0)
            P.cp("act", IDENTB(0, 128), CONST(0, 128))
        P.dma("sp", CONDT(0, 16), dreg(cond_in, 0, [[16, 128], [1, 16]]), "cC")
        P.dma("sp", GAINS(0, 72), dreg(gains_in, 0, [[72, 128], [1, 72]]), "cG")
        P.memset("dve", ONESB(0, 128), 1.0)
        P.dma("pool", ROTT(0, 128), dreg(rott_in, 0, [[128, 128], [1, 128]]), "c1")
        P.memset("dve", EPSB(0, 1), float(D * EPS))
        P.memset("dve", EPSB(1, 1), float(HD * EPS))
        P.memset("dve", EPSB(2, 1), float(EPS))
        P.act(SC(0, 16), CONDT(0, 16), AF.Silu)
        P.ts("dve", GAINS(72, 8), GAINS(64, 8), 32.0, ALU.mult)
        W.start()

        for b in range(10):
            xt = XTOK[b % 2]
            P.dma("sp", xt(0, 1024), dreg(x_in, b * 128 * D, [[D, 128], [1, D]]), "xin%d" % (b % 2))
            for half in range(2):
                ps = PS[(2 * b + half) % 4]
                for j in range(4):
                    c = half * 4 + j
                    P.tr(ps(j * 128, 128), xt(c * 128, 128), IDENT(0, 128))
                P.cp("dve" if half == 0 else "act",
                     X.v(half * 4 * NT + b * 128, [[NT, 4], [1, 128]]),
                     ps.v(0, [[128, 4], [1, 128]]))

        def modulation(l, par):
            MOD = MODS[par]
            P.dma("sp", BMOD(0, 48), dreg(bmod_in, l * 128 * 48, [[48, 128], [1, 48]]), "bmod")
            ps = PS[7]
            for t in range(12):
                slot = W.acquire(("mod", l, t))
                for j in range(4):
                    f = t * 4 + j
                    for kc in range(8):
                        P.mm(ps(f * 2, 2), slot(kc * 512 + j * 128, 128), SC(kc * 2, 2),
                             start=(kc == 0), stop=(kc == 7))
                W.release()
                yield
            P.tt("dve", MOD.v(0, [[2, 48], [1, 2]]), ps.v(0, [[2, 48], [1, 2]]),
                 BMOD.v(0, [[1, 48], [0, 2]]), ALU.add)
            A = AB[par]
            for which, (m_scale, goff) in enumerate(((1, l * 8), (4, 32 + l * 8))):
                P.ts("dve", A.v(which * 16, [[2, 8], [1, 2]]), MOD.v(m_scale * 16, [[2, 8], [1, 2]]),
                     1.0, ALU.add, 32.0, ALU.mult)
                P.tt("dve", A.v(which * 16, [[2, 8], [1, 2]]), A.v(which * 16, [[2, 8], [1, 2]]),
                     GAINS.v(goff, [[1, 8], [0, 2]]), ALU.mult)
            yield

        def run_all(gen):
            for _ in gen:
                pass

        def rms_rstd(src_chunk):
            for c in range(8):
                sq = SQ[c % 2]
                P.act(sq(0, NT), src_chunk(c), AF.Square)
                for ti, (t0, tn) in enumerate(TBS):
                    P.mm(PS[4 + ti](0, tn), ONESB(0, 128), sq(t0, tn), start=(c == 0), stop=(c == 7))
            for ti, (t0, tn) in enumerate(TBS):
                P.act(RSTD(t0, tn), PS[4 + ti](0, tn), AF.Ln, bias=EPSB(0, 1))
                P.act(RSTD(t0, tn), RSTD(t0, tn), AF.Exp, scale=-0.5)

        def norm_mod(par, which, m_shift):
            MOD = MODS[par]
            A = AB[par]
            rms_rstd(lambda c: X(c * NT, NT))
            for c in range(8):
                tmp = TMPN[c % 2]
                P.tt("dve" if c % 2 == 0 else "pool", tmp(0, NT), X(c * NT, NT), RSTD(0, NT), ALU.mult)
                for (t0, tn, cd) in ((0, NG, 0), (NG, NE, 1)):
                    P.act(H(c * NT + t0, tn), tmp(t0, tn), AF.Identity,
                          bias=MOD((m_shift * 8 + c) * 2 + cd, 1),
                          scale=A(which * 16 + c * 2 + cd, 1))

        def gated_residual(par, m_gate, c, pss):
            MOD = MODS[par]
            for ti, (t0, tn) in enumerate(TBS):
                cd = 0 if t0 < NG else 1
                P.stt("dve", X(c * NT + t0, tn), pss[ti](0, tn), MOD((m_gate * 8 + c) * 2 + cd, 1),
                      X(c * NT + t0, tn), ALU.mult, ALU.add)

        def mlp(l, par, side_gen):
            norm_mod(par, 1, 3)
            RT = TMPN
            for t in range(8):
                slot = W.acquire(("w1", l, t))
                for j in range(4):
                    oc = t * 4 + j
                    base = (oc % 2) * 3
                    for kc in range(8):
                        for ti, (t0, tn) in enumerate(TBS):
                            P.mm(PS[base + ti](0, tn), slot(kc * 512 + j * 128, 128), H(kc * NT + t0, tn),
                                 start=(kc == 0), stop=(kc == 7))
                    rt = RT[oc % 2]
                    for ti, (t0, tn) in enumerate(TBS):
                        P.act(rt(t0, tn), PS[base + ti](0, tn), AF.Relu)
                    P.tt("pool", HID(oc * NT, NT), rt(0, NT), rt(0, NT), ALU.mult)
                W.release()
                if side_gen is not None:
                    next(side_gen, None)
            for oc in range(8):
                slot = W.acquire(("w2", l, oc))
                base = (oc % 2) * 3
                for kc in range(32):
                    for ti, (t0, tn) in enumerate(TBS):
                        P.mm(PS[base + ti](0, tn), slot(kc * 128, 128), HID(kc * NT + t0, tn),
                             start=(kc == 0), stop=(kc == 31))
                gated_residual(par, 5, oc, [PS[base + ti] for ti in range(3)])
                W.release()
                if side_gen is not None:
                    next(side_gen, None)
            if side_gen is not None:
                run_all(side_gen)


        def attention(l, par):
            j = l // 2
            norm_mod(par, 0, 0)
            P.dma("sp", COS(0, NG), dreg(rope_in, 0, [[NG, 128], [1, NG]]), "aCOS")
            P.dma("sp", SIN(0, NG), dreg(rope_in, 128 * NG, [[NG, 128], [1, NG]]), "aSIN")
            P.dma("sp", MASKB(0, 20), dreg(maskb_in, 0, [[20, 128], [1, 20]]), "aMB")
            P.dma("sp", AGN2(0, 256), dreg(again_in, j * 128 * 256, [[256, 128], [1, 256]]), "aAG")
            P.dma("sp", CKV.v(0, [[256, 2], [1, 256]]),
                  dreg(ck_in, j * 256 * 256, [[256, 128], [128 * 256, 2], [1, 256]]), "aCK")
            P.dma("pool", VT.v(0, [[256, 2], [1, 256]]),
                  dreg(cv_in, j * 256 * 256, [[256, 128], [128 * 256, 2], [1, 256]]), "acv")
            if DBG_STAGE == 0:
                return
            for kc in range(2):
                ps = PS[6 + kc]
                for kv in range(2):
                    P.tr(ps(kv * 128, 128), CKV(kc * 256 + kv * 128, 128), IDENT(0, 128))
                P.cp("act", KT.v(kc * 128, [[1536, 2], [1, 128]]), ps.v(0, [[128, 2], [1, 128]]))
            if DBG_STAGE == 1:
                return
            qslots = {}

            def q_proj(h):
                t, jj = divmod(h, 4)
                if jj == 0:
                    qslots[t] = W.acquire(("aq", j, t))
                slot = qslots[t]
                base = (h % 2) * 3
                for kc in range(8):
                    for ti, (t0, tn) in enumerate(TBS):
                        P.mm(PS[base + ti](0, tn), slot(kc * 512 + jj * 128, 128), H(kc * NT + t0, tn),
                             start=(kc == 0), stop=(kc == 7))
                qraw = QRAW[h % 2]
                sq = SQ[h % 2]
                for ti, (t0, tn) in enumerate(TBS):
                    P.cp("dve", qraw(t0, tn), PS[base + ti](0, tn))
                    P.act(sq(t0, tn), qraw(t0, tn), AF.Square)
                if jj == 3:
                    W.release()

            def q_norm_rope(h):
                qraw = QRAW[h % 2]
                sq = SQ[h % 2]
                RECS = TMPN[0]
                pbs = [PS[6 + ti % 2] for ti in range(3)]
                for ti, (t0, tn) in enumerate(TBS):
                    P.mm(pbs[ti](0, tn), ONESB(0, 128), sq(t0, tn))
                    P.act(RECS(t0, tn), pbs[ti](0, tn), AF.Ln, bias=EPSB(2, 1), scale=1.0 / HD)
                for ti, (t0, tn) in enumerate(TBS):
                    P.act(RECS(t0, tn), RECS(t0, tn), AF.Exp, scale=-0.5)
                    P.stt("dve", QN(t0, tn), qraw(t0, tn), AGN2(128, 1), RECS(t0, tn), ALU.mult, ALU.mult)
                    if t0 >= NG:
                        P.cp("act", QT(h * NT + t0, tn), QN(t0, tn))
                    else:
                        P.cp("act", QNB(t0, tn), QN(t0, tn))
                for ti, (t0, tn) in enumerate(TBS):
                    if t0 >= NG:
                        continue
                    P.mm(pbs[ti](0, tn), ROTT(0, 128), QNB(t0, tn))
                    P.tt("pool", T2[ti](0, tn), QN(t0, tn), COS(t0, tn), ALU.mult)
                    P.tt("dve", T1[ti](0, tn), pbs[ti](0, tn), SIN(t0, tn), ALU.mult)
                    P.tt("pool", QT(h * NT + t0, tn), T1[ti](0, tn), T2[ti](0, tn), ALU.add)

            q_proj(0)
            for h in range(8):
                if h + 1 < 8:
                    q_proj(h + 1)
                q_norm_rope(h)
            if DBG_STAGE == 2:
                return
            P.memset("dve", KSS(0, 16), 1.0)
            slot = W.acquire(("akv", j))

            def kv_proj(b):
                for kc in range(8):
                    P.mm(PS[b % 4](0, 512), H(kc * NT + b * 128, 128), slot(kc * 512, 512),
                         start=(kc == 0), stop=(kc == 7))

            kv_proj(0)
            for b in range(10):
                ps = PS[b % 4]
                if b + 1 < 10:
                    kv_proj(b + 1)
                ktok = KTOK[b % 2]
                vstg = VSTG[b % 2]
                if DBG_SUB == 10:
                    continue
                P.act(T1[0](0, 256), ps(0, 256), AF.Square)
                P.add("dve", (lambda o, i: (lambda e: e.tensor_reduce(o.ap, i.ap, mybir.AxisListType.X, ALU.add)))(
                    KSS(0, 2), T1[0].v(0, [[128, 2], [1, 128]])), [T1[0](0, 256)], [KSS(0, 2)])
                P.act(KSS(8, 8), KSS(0, 8), AF.Sqrt, bias=EPSB(2, 1), scale=1.0 / HD)
                P.add("dve", (lambda o: (lambda e: e.reciprocal(o.ap, o.ap)))(KSS(8, 8)), [KSS(8, 8)], [KSS(8, 8)])
                for kv in range(2):
                    P.stt("dve", ktok(kv * 128, 128), ps(kv * 128, 128), KSS(8 + kv, 1), KGB(0, 128),
                          ALU.mult, ALU.mult)
                if DBG_SUB == 11:
                    continue
                P.dma("sp", dreg(k_out, (j * NT + b * 128) * 256, [[256, 128], [1, 256]]), ktok(0, 256),
                      "ko%d" % (b % 2))
                if DBG_SUB == 12:
                    continue
                P.cp("act", VT((2 + b) * 256, 256), ps(256, 256))
                P.cp("dve", vstg(0, 256), ps(256, 256))
                P.dma("sp", dreg(v_out, (j * NT + b * 128) * 256, [[256, 128], [1, 256]]), vstg(0, 256),
                      "vo%d" % (b % 2))
                if DBG_SUB == 13:
                    continue
                pt = PS[4 + b % 2]
                for kv in range(2):
                    P.tr(pt(kv * 128, 128), ktok(kv * 128, 128), IDENT(0, 128))
                kdst = KT.v(256 + b * 128, [[1536, 2], [1, 128]])
                if DBG_SUB == 14:
                    continue
                if b >= 8:
                    if DBG_SUB not in (152, 153):
                        P.cp("act", kdst, pt.v(0, [[128, 2], [1, 128]]))
                else:
                    if DBG_SUB not in (151, 153):
                        P.cp("act", KNB(0, 256), pt(0, 256))
                    if DBG_SUB not in (151, 152):
                        P.cp("dve", KNF(0, 256), pt(0, 256))
                    if DBG_SUB in (15, 151, 152, 153):
                        continue
                    pr = PS[6 + b % 2]
                    P.mm(pr(0, 256), ROTT(0, 128), KNB(0, 256))
                    if DBG_SUB == 16:
                        continue
                    P.tt("dve", T1[1].v(0, [[128, 2], [1, 128]]), pr.v(0, [[128, 2], [1, 128]]),
                         SIN.v(b * 128, [[0, 2], [1, 128]]), ALU.mult)
                    P.tt("dve", T2[1].v(0, [[128, 2], [1, 128]]), KNF.v(0, [[128, 2], [1, 128]]),
                         COS.v(b * 128, [[0, 2], [1, 128]]), ALU.mult)
                    P.tt("dve", kdst, T1[1].v(0, [[128, 2], [1, 128]]), T2[1].v(0, [[128, 2], [1, 128]]), ALU.add)
            W.release()
            if DBG_STAGE == 3:
                return
            units = []
            it = 0
            for h in range(8):
                kv = h // 4
                for (q0, qn, chunks) in ((0, 512, list(range(10))), (512, 512, list(range(10))),
                                         (NG, NE, [10, 11])):
                    po = PS[2 + it % 2]
                    pd = PS[4 + it % 2]
                    it += 1
                    for ci, kc in enumerate(chunks):
                        units.append((h, kv, q0, qn, kc, ci == 0, ci == len(chunks) - 1, po, pd))

            def score(idx):
                h, kv, q0, qn, kc, first, last, po, pd = units[idx]
                pS = PS[idx % 2]
                ptile = PT[idx % 4]
                P.mm(pS(0, qn), KT(kv * 1536 + kc * 128, 128), QT(h * NT + q0, qn))
                if q0 >= NG:
                    P.act(ptile(0, qn), pS(0, qn), AF.Exp, scale=float(HD ** -0.5))
                else:
                    kb = 0 if kc < 2 else 1 + (kc - 2) // 2
                    for hf in range(2):
                        qblk = q0 // 256 + hf
                        P.act(ptile(hf * 256, 256), pS(hf * 256, 256), AF.Exp,
                              bias=MASKB(kb * 4 + qblk, 1), scale=float(HD ** -0.5))

            def pv(idx):
                h, kv, q0, qn, kc, first, last, po, pd = units[idx]
                ptile = PT[idx % 4]
                P.mm(po(0, qn), VT(kc * 256 + kv * 128, 128), ptile(0, qn), start=first, stop=last)
                P.mm(pd(0, qn), ONESB(0, 128), ptile(0, qn), start=first, stop=last)
                if last:
                    P.add("dve", (lambda o, i: (lambda e: e.reciprocal(o.ap, i.ap)))(REC(0, qn), pd(0, qn)),
                          [pd(0, qn)], [REC(0, qn)])
                    P.tt("dve", O(h * NT + q0, qn), po(0, qn), REC(0, qn), ALU.mult)

            for idx in range(len(units) + 1):
                if idx < len(units):
                    score(idx)
                if idx >= 1:
                    pv(idx - 1)
            if DBG_STAGE == 4:
                return
            for t in range(2):
                slot = W.acquire(("ao", j, t))
                for jj in range(4):
                    oc = t * 4 + jj
                    base = (oc % 2) * 3
                    for kc in range(8):
                        for ti, (t0, tn) in enumerate(TBS):
                            P.mm(PS[base + ti](0, tn), slot(kc * 512 + jj * 128, 128), O(kc * NT + t0, tn),
                                 start=(kc == 0), stop=(kc == 7))
                    gated_residual(par, 2, oc, [PS[base + ti] for ti in range(3)])
                W.release()


        def deltanet(l, par):
            j = l // 2
            norm_mod(par, 0, 0)
            C_MADD, C_MS, C_UF, C_UB, C_SAME, C_IND0, C_ONESF = 128, 384, 640, 768, 896, 1024, 1280
            C_BD16, C_OFF1, C_OFF2, C_OFF3 = 1408, 1536, 1664, 1792
            P.dma("pool", D_WAB(0, 256), dreg(dnab_in, j * 128 * 256, [[256, 128], [1, 256]]), "dWAB")
            P.dma("sp", D_CW(0, 120), dreg(convw_in, j * 128 * 120, [[120, 128], [1, 120]]), "dCW")
            P.dma("sp", D_GP(0, 33), dreg(dngp_in, j * 128 * 33, [[33, 128], [1, 33]]), "dGP")
            P.memset("pool", D_PAD(0, 5 * 260), 0.0)
            pab, pgc, pgl, peg = PS[0], PS[1], PS[2], PS[3]
            for b in range(10):
                for kc in range(8):
                    P.mm(pab(b * 32, 32), H(kc * NT + b * 128, 128), D_WAB(kc * 32, 32),
                         start=(kc == 0), stop=(kc == 7))
            v3 = lambda t, c0=0, w=16, st=16: t.v(c0, [[st, 10], [1, w]])
            P.tt("dve", v3(D_TG), pab.v(0, [[32, 10], [1, 16]]), D_GP.v(16, [[0, 10], [1, 16]]), ALU.add)
            SPY, SPZ, SPP = D_GLO, D_KD, D_NEGGC
            P.ts("dve", SPY(0, 160), D_TG(0, 160), -1.0, ALU.mult)
            P.tt("dve", SPY(0, 160), SPY(0, 160), D_TG(0, 160), ALU.max)
            P.act(SPY(0, 160), SPY(0, 160), AF.Exp, scale=-1.0)
            P.ts("dve", SPZ(0, 160), SPY(0, 160), 2.0, ALU.add)
            P.add("dve", (lambda o: (lambda e: e.reciprocal(o.ap, o.ap)))(SPZ(0, 160)), [SPZ(0, 160)], [SPZ(0, 160)])
            P.tt("dve", SPZ(0, 160), SPY(0, 160), SPZ(0, 160), ALU.mult)
            P.tt("dve", SPY(0, 160), SPZ(0, 160), SPZ(0, 160), ALU.mult)
            P.ts("dve", SPP(0, 160), SPY(0, 160), 1.0 / 9.0, ALU.mult, 1.0 / 7.0, ALU.add)
            for cst in (1.0 / 5.0, 1.0 / 3.0, 1.0):
                P.tt("dve", SPP(0, 160), SPP(0, 160), SPY(0, 160), ALU.mult)
                P.ts("dve", SPP(0, 160), SPP(0, 160), cst, ALU.add)
            P.tt("dve", SPP(0, 160), SPP(0, 160), SPZ(0, 160), ALU.mult)
            P.ts("dve", D_TG(0, 160), D_TG(0, 160), 0.0, ALU.max)
            P.stt("dve", D_TG(0, 160), SPP(0, 160), 2.0, D_TG(0, 160), ALU.mult, ALU.add)
            P.act(D_EA(0, 16), D_GP(0, 16), AF.Exp)
            P.stt("dve", v3(D_G), v3(D_TG), -1.0, D_EA.v(0, [[0, 10], [1, 16]]), ALU.mult, ALU.mult)
            P.act(v3(D_BETA), pab.v(16, [[32, 10], [1, 16]]), AF.Sigmoid)
            P.ts("pool", D_NEGBETA(0, 160), D_BETA(0, 160), -1.0, ALU.mult)
            for b in range(10):
                P.mm(pgc(b * 16, 8), CONST(C_UF, 128), D_G(b * 16, 8))
                P.mm(pgc(b * 16 + 8, 8), CONST(C_UB, 128), D_G(b * 16 + 8, 8))
                P.mm(pgl(b * 16, 16), CONST(C_SAME, 128), D_G(b * 16, 16))
                for c in range(2):
                    P.mm(peg(b * 32 + c * 16, 16), CONST(C_IND0 + c * 128, 128), D_G(b * 16, 16))
            P.cp("dve", D_GC(0, 160), pgc(0, 160))
            P.ts("dve", D_NEGGC(0, 160), pgc(0, 160), -1.0, ALU.mult)
            P.act(D_NEGEGC(0, 160), pgc(0, 160), AF.Exp)
            P.ts("pool", D_NEGEGC(0, 160), D_NEGEGC(0, 160), -1.0, ALU.mult)
            P.tt("dve", D_KD(0, 160), pgl(0, 160), D_GC(0, 160), ALU.subtract)
            P.act(D_KD(0, 160), D_KD(0, 160), AF.Exp)
            P.act(D_EGL(0, 320), peg(0, 320), AF.Exp)

            def stageA(h):
                slot = W.acquire(("dnh", j, h))
                for part in range(4):
                    for kc in range(8):
                        for ti, (t0, tn) in enumerate(TBS):
                            P.mm(PS[ti](0, tn), slot(kc * 512 + part * 128, 128), H(kc * NT + t0, tn),
                                 start=(kc == 0), stop=(kc == 7))
                    yield
                    if part == 3:
                        for ti, (t0, tn) in enumerate(TBS):
                            P.act(D_ZS[h % 2](t0, tn), PS[ti](0, tn), AF.Silu)
                        yield
                        continue
                    ch = part * 8 + h
                    for tap in range(5):
                        P.ts("pool", D_DG(tap * 128, 128), IDENT(0, 128), D_CW(ch * 5 + tap, 1), ALU.mult)
                    for ti, (t0, tn) in enumerate(TBS):
                        ns = tn // 256
                        P.cp("act" if ti != 1 else "dve", D_PAD.v((t0 // 256) * 260 + 2, [[260, ns], [1, 256]]),
                             PS[ti].v(0, [[256, ns], [1, 256]]))
                    P.ts("dve", D_PAD.v(260, [[260, 3], [1, 2]]), D_PAD.v(256, [[260, 3], [1, 2]]), FLG(0, 1), ALU.mult)
                    P.ts("dve", D_PAD.v(258, [[260, 3], [1, 2]]), D_PAD.v(262, [[260, 3], [1, 2]]), FLG(0, 1), ALU.mult)
                    for ti, (t0, tn) in enumerate(TBS):
                        ns = tn // 256
                        for tap in range(5):
                            P.mm(PS[3 + ti](0, tn), D_DG(tap * 128, 128),
                                 D_PAD.v((t0 // 256) * 260 + tap, [[260, ns], [1, 256]]),
                                 start=(tap == 0), stop=(tap == 4))
                    yield
                    dst = D_VB if part == 2 else D_FT
                    for ti, (t0, tn) in enumerate(TBS):
                        P.act(dst(t0, tn), PS[3 + ti](0, tn), AF.Silu)
                    if part == 2:
                        yield
                        continue
                    P.act(D_SQ(0, NT), D_FT(0, NT), AF.Square)
                    out = D_QNB if part == 0 else D_KNBS[h % 2]
                    for ti, (t0, tn) in enumerate(TBS):
                        P.mm(PS[3 + ti](0, tn), ONESB(0, 128), D_SQ(t0, tn))
                        P.act(D_RSQ(0, tn), PS[3 + ti](0, tn), AF.Ln, bias=EPSB(2, 1))
                        P.act(D_RSQ(0, tn), D_RSQ(0, tn), AF.Exp, scale=-0.5)
                        P.stt("dve", out(t0, tn), D_FT(t0, tn), float(HD ** -0.5) if part == 0 else 1.0,
                              D_RSQ(0, tn), ALU.mult, ALU.mult)
                    yield
                W.release()

            def stageB(h):
                D_KNB = D_KNBS[h % 2]
                D_VTOK, D_KDEC, D_XB, D_QKM, D_QGT = (D_VTOKS[h % 2], D_KDECS[h % 2], D_XBS[h % 2],
                                                      D_QKMS[h % 2], D_QGTS[h % 2])
                for b in range(10):
                    ps = PS[b % 2]
                    P.mm(ps(0, 128), D_KNB(b * 128, 128), IDENTB(0, 128))
                    P.mm(ps(128, 128), D_VB(b * 128, 128), IDENTB(0, 128))
                    for dr in range(2):
                        P.ts("dve", D_KDEC(dr * NT + b * 128, 128), ps(0, 128), D_KD(b * 16 + dr * 8 + h, 1), ALU.mult)
                    P.cp("act", D_VTOK(b * 128, 128), ps(128, 128))
                yield
                m4 = lambda t, c0=0: t.v(c0, [[256, 2], [128, 2], [1, 128]])
                for bp in range(5):
                    pk, pc, pt_, pl = PS[2], PS[3], PS[4], (PS[5], PS[0], PS[1])
                    for bl in range(2):
                        blk = 2 * bp + bl
                        P.mm(pk(bl * 128, 128), D_KNB(blk * 128, 128), D_KNB(blk * 128, 128))
                        P.mm(pk(256 + bl * 128, 128), D_KNB(blk * 128, 128), D_QNB(blk * 128, 128))
                    DIAG, EB, DT = D_NB[0], D_NB[1], D_NB[2]
                    Pa, Za = D_NB[3], D_NB[4]
                    gsel = lambda t: t.v(bp * 32 + h, [[16, 2], [8, 2], [0, 128]])
                    P.tt("pool", m4(DIAG), IDENT.v(0, [[0, 2], [0, 2], [1, 128]]), gsel(D_GC), ALU.mult)
                    for m in range(4):
                        P.mm(pc(m * 128, 128), CONST(C_ONESF, 128), DIAG(m * 128, 128))
                    P.act(EB(0, 512), pc(0, 512), AF.Exp)
                    P.tt("pool", D_QGT.v(2 * bp * 128, [[128, 2], [NT, 2], [1, 128]]),
                         D_QNB.v(2 * bp * 128, [[128, 2], [0, 2], [1, 128]]), m4(EB), ALU.mult)
                    P.tt("dve", m4(DT), m4(pc), CONST.v(C_MADD, [[0, 2], [128, 2], [1, 128]]), ALU.add)
                    for m in range(4):
                        bl, dr = m // 2, m % 2
                        P.act(DT(m * 128, 128), DT(m * 128, 128), AF.Exp,
                              bias=D_NEGGC((2 * bp + bl) * 16 + dr * 8 + h, 1))
                    P.tt("dve", m4(D_QKM, bp * 512), pk.v(256, [[128, 2], [0, 2], [1, 128]]), m4(DT), ALU.mult)
                    P.tt("pool", m4(DT), m4(DT), CONST.v(C_MS, [[0, 2], [128, 2], [1, 128]]), ALU.mult)
                    for m in range(4):
                        bl, dr = m // 2, m % 2
                        P.stt("dve", Pa(m * 128, 128), pk(bl * 128, 128),
                              D_NEGBETA((2 * bp + bl) * 16 + dr * 8 + h, 1), DT(m * 128, 128), ALU.mult, ALU.mult)
                    for m in range(4):
                        P.tr(pt_(m * 128, 128), Pa(m * 128, 128), IDENT(0, 128))
                    P.cp("act", Za(0, 512), pt_(0, 512))
                    yield
                    NBs = D_NB
                    msk = lambda c0: CONST.v(c0, [[0, 4], [1, 128]])
                    f4 = lambda t: t.v(0, [[128, 4], [1, 128]])
                    Pc, Zc = NBs[0], NBs[1]
                    P.tt("pool", f4(Pc), f4(Pa), msk(C_BD16), ALU.mult)
                    P.tt("pool", f4(Zc), f4(Za), msk(C_BD16), ALU.mult)
                    Xc, XTc, Xn, XTn = NBs[5], NBs[6], NBs[8], NBs[9]
                    P.tt("pool", f4(Xc), f4(Pc), msk(0), ALU.add)
                    P.tt("dve", f4(XTc), f4(Zc), msk(0), ALU.add)
                    pz = [(NBs[2], NBs[7]), (NBs[0], NBs[1])]
                    for k in range(1, 4):
                        Pn, Zn = pz[(k - 1) % 2]
                        for m in range(4):
                            P.mm(pl[0](m * 128, 128), Zc(m * 128, 128), Pc(m * 128, 128))
                        for m in range(4):
                            P.mm(pl[1](m * 128, 128), Pc(m * 128, 128), Zc(m * 128, 128))
                        P.cp("act", Pn(0, 512), pl[0](0, 512))
                        P.cp("dve", Zn(0, 512), pl[1](0, 512))
                        yield
                        for m in range(4):
                            P.mm(pl[2](m * 128, 128), Zn(m * 128, 128), Xc(m * 128, 128))
                        for m in range(4):
                            P.mm(pt_(m * 128, 128), Pn(m * 128, 128), XTc(m * 128, 128))
                        P.tt("dve", Xn(0, 512), pl[2](0, 512), Xc(0, 512), ALU.add)
                        P.tt("dve", XTn(0, 512), pt_(0, 512), XTc(0, 512), ALU.add)
                        Pc, Zc = Pn, Zn
                        Xc, XTc, Xn, XTn = Xn, XTn, Xc, XTc
                        yield
                    for lv, cmask in ((1, C_OFF1), (2, C_OFF2), (3, C_OFF3)):
                        Nl, NlT, T1, T1p = NBs[0], NBs[1], NBs[2], NBs[7]
                        P.tt("pool", f4(Nl), f4(Pa), msk(cmask), ALU.mult)
                        P.tt("pool", f4(NlT), f4(Za), msk(cmask), ALU.mult)
                        for m in range(4):
                            P.mm(pl[0](m * 128, 128), NlT(m * 128, 128), Xc(m * 128, 128))
                        P.cp("act", T1(0, 512), pl[0](0, 512))
                        if lv < 3:
                            for m in range(4):
                                P.mm(pl[1](m * 128, 128), Nl(m * 128, 128), XTc(m * 128, 128))
                            P.cp("dve", T1p(0, 512), pl[1](0, 512))
                        yield
                        for m in range(4):
                            P.mm(pl[2](m * 128, 128), XTc(m * 128, 128), T1(m * 128, 128))
                        if lv < 3:
                            for m in range(4):
                                P.mm(pt_(m * 128, 128), Xc(m * 128, 128), T1p(m * 128, 128))
                            P.tt("dve", Xn(0, 512), pl[2](0, 512), Xc(0, 512), ALU.add)
                            P.tt("dve", XTn(0, 512), pt_(0, 512), XTc(0, 512), ALU.add)
                            Xc, XTc, Xn, XTn = Xn, XTn, Xc, XTc
                        else:
                            P.tt("dve", D_XB(bp * 512, 512), pl[2](0, 512), Xc(0, 512), ALU.add)
                        yield

            def chain(h, dr):
                D_KNB = D_KNBS[h % 2]
                D_VTOK, D_KDEC, D_XB, D_QKM, D_QGT = (D_VTOKS[h % 2], D_KDECS[h % 2], D_XBS[h % 2],
                                                      D_QKMS[h % 2], D_QGTS[h % 2])
                SF, SB_, RB, VNB = D_SF[dr], D_SB[dr], D_RB[dr], D_VNB[dr]
                ps = PS[6 + dr]
                gcol = dr * 8 + h
                if dr == 0:
                    P.dma("sp", SF(0, 128), dreg(s0_in, ((j * 16) + gcol) * 16384, [[128, 128], [1, 128]]), "dS0_%d" % dr)
                else:
                    P.memset("dve", SF(0, 128), 0.0)
                P.cp("act", SB_(0, 128), SF(0, 128))
                pr, pv, po, pss = ps(0, 128), ps(128, 128), ps(256, 128), ps(384, 128)
                order = list(range(10)) if dr == 0 else list(range(9, -1, -1))
                nst = 0
                for b in order:
                    bp, m = b // 2, (b % 2) * 2 + dr
                    seg = b // 2
                    P.mm(pr, D_KNB(b * 128, 128), SB_(0, 128))
                    P.stt("dve", RB(0, 128), pr, D_NEGEGC(b * 16 + gcol, 1), D_VTOK(b * 128, 128),
                          ALU.mult, ALU.add)
                    yield
                    P.mm(pv, D_XB(bp * 512 + m * 128, 128), RB(0, 128))
                    P.ts("dve", VNB(0, 128), pv, D_BETA(b * 16 + gcol, 1), ALU.mult)
                    yield
                    P.mm(po, SB_(0, 128), D_QGT(dr * NT + b * 128, 128), start=True, stop=False)
                    P.mm(po, VNB(0, 128), D_QKM(bp * 512 + m * 128, 128), start=False, stop=True)
                    P.cp("act", D_OD(dr * NT + b * 128, 128), po)
                    yield
                    P.mm(pss, D_KDEC(dr * NT + b * 128, 128), VNB(0, 128))
                    P.stt("dve", SF(0, 128), SF(0, 128), D_EGL(b * 32 + gcol, 1), pss, ALU.mult, ALU.add)
                    last = (b % 2 == 1) if dr == 0 else (b % 2 == 0)
                    if last:
                        stg = D_STG[dr]
                        nst += 1
                        P.cp("act", stg(0, 128), SF(0, 128))
                        P.dma("sp", dreg(s_out, ((((j * 5 + seg) * 2 + dr) * 8 + h) * 16384), [[128, 128], [1, 128]]),
                              stg(0, 128), "dSO%d" % dr)
                        if dr == 0:
                            if seg < 3:
                                P.ts("dve", SF(0, 128), SF(0, 128), FLG(0, 1), ALU.mult)
                            elif seg == 3:
                                P.memset("dve", SF(0, 128), 0.0)
                        else:
                            if seg == 4:
                                P.dma("sp", SF(0, 128), dreg(s0_in, ((j * 16) + gcol) * 16384, [[128, 128], [1, 128]]),
                                      "dS0_%d" % dr)
                            elif seg >= 1:
                                P.ts("dve", SF(0, 128), SF(0, 128), FLG(0, 1), ALU.mult)
                    P.cp("act", SB_(0, 128), SF(0, 128))
                    yield

            def finalize(h):
                OS, TM = D_NB[0], D_NB[3]
                for ti, (t0, tn) in enumerate(TBS):
                    P.tt("pool", OS(0, tn), D_OD(t0, tn), D_OD(NT + t0, tn), ALU.add)
                    P.act(D_SQF(0, tn), OS(0, tn), AF.Square)
                    P.mm(PS[ti](0, tn), ONESB(0, 128), D_SQF(0, tn))
                    P.act(D_RSQ(0, tn), PS[ti](0, tn), AF.Ln, bias=EPSB(2, 1), scale=1.0 / HD)
                    P.act(D_RSQ(0, tn), D_RSQ(0, tn), AF.Exp, scale=-0.5)
                    P.stt("dve", TM(0, tn), OS(0, tn), D_GP(32, 1), D_RSQ(0, tn), ALU.mult, ALU.mult)
                    P.tt("pool", O(h * NT + t0, tn), TM(0, tn), D_ZS[h % 2](t0, tn), ALU.mult)

            def seq(*gens):
                for g in gens:
                    if g is not None:
                        for _ in g:
                            yield

            def interleave(chains, side, ratio):
                chains = list(chains)
                while chains or side is not None:
                    for g in list(chains):
                        try:
                            next(g)
                        except StopIteration:
                            chains.remove(g)
                    if side is not None:
                        for _ in range(ratio if chains else 10 ** 9):
                            try:
                                next(side)
                            except StopIteration:
                                side = None
                                break

            if DBG_STAGE == 10:
                return
            run_all(stageA(0))
            run_all(stageB(0))
            for h in range(8):
                side = seq(stageA(h + 1), stageB(h + 1)) if h < 7 else None
                interleave([chain(h, 0), chain(h, 1)], side, DN_RATIO)
                finalize(h)
            for t in range(2):
                slot = W.acquire(("do", j, t))
                for jj in range(4):
                    oc = t * 4 + jj
                    base = (oc % 2) * 3
                    for kc in range(8):
                        for ti, (t0, tn) in enumerate(TBS):
                            P.mm(PS[base + ti](0, tn), slot(kc * 512 + jj * 128, 128), O(kc * NT + t0, tn),
                                 start=(kc == 0), stop=(kc == 7))
                    gated_residual(par, 2, oc, [PS[base + ti] for ti in range(3)])
                W.release()

        run_all(modulation(0, 0))
        for l in range(nlayers):
            par = l % 2
            if mixers:
                if l % 2 == 0:
                    attention(l, par)
                else:
                    deltanet(l, par)
            side = modulation(l + 1, 1 - par) if l + 1 < nlayers else None
            mlp(l, par, side)

        rms_rstd(lambda c: X(c * NT, NT))
        for c in range(8):
            tmp = TMPN[c % 2]
            P.tt("dve", tmp(0, NT), X(c * NT, NT), RSTD(0, NT), ALU.mult)
            P.act(X(c * NT, NT), tmp(0, NT), AF.Copy, scale=GAINS(72 + c, 1))
        for b in range(10):
            yt = XTOK[b % 2]
            for half in range(2):
                ps = PS[(2 * b + half) % 4]
                for j in range(4):
                    c = half * 4 + j
                    P.tr(ps(j * 128, 128), X(c * NT + b * 128, 128), IDENT(0, 128))
                P.cp("dve" if half == 0 else "act", yt(half * 512, 512), ps(0, 512))
            P.dma("sp", dreg(y_out, b * 128 * D, [[D, 128], [1, D]]), yt(0, 1024), "yout%d" % (b % 2))

        if not wdescs_only:
            P.finalize()
    return nc, P, W.descs


def _tile_kc8(Wm, n0):
    blk = Wm[:, n0:n0 + 512].reshape(8, 128, 512)
    return np.ascontiguousarray(blk.transpose(1, 0, 2)).reshape(128, 4096)


def _tile_w2(Wm, oc):
    blk = Wm[:, oc * 128:(oc + 1) * 128].reshape(32, 128, 128)
    return np.ascontiguousarray(blk.transpose(1, 0, 2)).reshape(128, 4096)


def make_tiles(descs, inp):
    tiles = np.empty((len(descs), 128, 4096), np.float32)
    for i, d in enumerate(descs):
        kind, l = d[0], d[1]
        if kind == "mod":
            tiles[i] = _tile_kc8(inp["w_mod"][l], d[2] * 512)
        elif kind == "aq":
            tiles[i] = _tile_kc8(inp["attn_w_in"][l], d[2] * 512)
        elif kind == "akv":
            tiles[i] = _tile_kc8(inp["attn_w_in"][l], 1024)
        elif kind == "ao":
            tiles[i] = _tile_kc8(inp["attn_w_out"][l], d[2] * 512)
        elif kind == "dnh":
            h = d[2]
            Wm = inp["dn_w_in"][l]
            cols = np.concatenate([Wm[:, p * 1024 + h * 128: p * 1024 + (h + 1) * 128] for p in range(4)], axis=1)
            tiles[i] = _tile_kc8(cols, 0)
        elif kind == "do":
            tiles[i] = _tile_kc8(inp["dn_w_out"][l], d[2] * 512)
        elif kind == "w1":
            tiles[i] = _tile_kc8(inp["mlp_w1"][l], d[2] * 512)
        elif kind == "w2":
            tiles[i] = _tile_w2(inp["mlp_w2"][l], d[2])
        else:
            raise ValueError(d)
    return tiles


_CACHE = {}


def _get_program(nlayers=DEPTH, mixers=True):
    key = (nlayers, mixers)
    if key not in _CACHE:
        _, _, descs = build(nlayers, mixers, wdescs_only=True)
        nc, P, descs2 = build(nlayers, mixers, ntiles=len(descs))
        assert descs == descs2
        _CACHE[key] = (nc, P, descs)
    return _CACHE[key]


def _rope_tables():
    t = np.arange(NG)
    row = (t // 64).astype(np.float32)
    col = (t % 64).astype(np.float32)
    inv = (10000.0 ** (-np.arange(32, dtype=np.float32) / 32)).astype(np.float32)
    d = np.arange(128)
    pos = np.where((d // 64)[:, None] == 0, row[None, :], col[None, :]).astype(np.float32)
    ang = pos * inv[d % 32][:, None]
    return np.stack([np.cos(ang), np.sin(ang)]).astype(np.float32)


def _rott():
    R = np.zeros((128, 128), np.float32)
    for d in range(128):
        g, r = d // 64, d % 64
        if r < 32:
            R[d, g * 64 + r + 32] = -1.0
        else:
            R[d, g * 64 + r - 32] = 1.0
    return np.ascontiguousarray(R.T)


def core_inputs(inp, core, descs_tiles):
    xp, xs = inp["x_prompt"], inp["x_sample"]
    sample = core >= 6
    if not sample:
        xg = xp[4 * core:4 * core + 4].reshape(NG, D)
        condg = inp["c_ctx"]
        rope = np.stack([np.ones((128, NG), np.float32), np.zeros((128, NG), np.float32)])
        ck = np.zeros((2, 256, 256), np.float32)
        cv = np.zeros((2, 256, 256), np.float32)
        s0 = np.zeros((2, 16, 128, 128), np.float32)
        flag = 0.0
        mb = np.full((5, 4), -30000.0, np.float32)
        for q in range(4):
            mb[1 + q, q] = 0.0
    else:
        b = core - 6
        xg = xs[b]
        condg = inp["c"][b]
        rope = _rope_tables()
        ck = inp["cache_k"][b].reshape(2, 256, 256)
        cv = inp["cache_v"][b].reshape(2, 256, 256)
        mb = np.zeros((5, 4), np.float32)
        s0 = inp["state_delta"][b].reshape(2, 16, 128, 128)
        flag = 1.0
    xe = xp[24 + core]
    x_in = np.concatenate([xg, xe], axis=0)
    cond = np.stack([condg, inp["c_ctx"]], axis=0)
    cond_in = np.ascontiguousarray(cond.reshape(2, 8, 128).transpose(2, 1, 0)).reshape(128, 16)
    m = {"x_in": np.ascontiguousarray(x_in, dtype=np.float32), "cond_in": cond_in.astype(np.float32),
         "rope_in": rope, "ck_in": np.ascontiguousarray(ck, dtype=np.float32),
         "cv_in": np.ascontiguousarray(cv, dtype=np.float32),
         "maskb_in": np.ascontiguousarray(np.broadcast_to(mb.reshape(1, 20), (128, 20))),
         "s0_in": np.ascontiguousarray(s0, dtype=np.float32),
         "flags_in": np.full((128, 16), flag, np.float32)}
    for i in range((len(descs_tiles) + WCH - 1) // WCH):
        m["wts%d" % i] = descs_tiles[i * WCH:(i + 1) * WCH]
    return m


def _consts():
    c = np.zeros((128, NCONST), np.float32)
    j = np.arange(128)[:, None]
    i = np.arange(128)[None, :]
    c[:, 0:128] = np.eye(128)
    c[:, 128:256] = np.where(i >= j, 0.0, -60000.0)
    c[:, 256:384] = np.where(i <= j, 0.0, -60000.0)
    c[:, 384:512] = (i > j)
    c[:, 512:640] = (i < j)
    c[:, 640:768] = (j <= i)
    c[:, 768:896] = (j >= i)
    c[:, 896:1024] = 1.0
    c[:, 1024:1152] = 1.0
    c[:, 1152:1280] = 1.0
    c[:, 1280:1408] = 1.0
    c[:, 1408:1536] = (j // 16) == (i // 16)
    c[:, 1536:1664] = ((j // 32) == (i // 32)) & ((j // 16) != (i // 16))
    c[:, 1664:1792] = ((j // 64) == (i // 64)) & ((j // 32) != (i // 32))
    c[:, 1792:1920] = ((j // 64) != (i // 64))
    return c


def shared_inputs(inp):
    bmod = np.ascontiguousarray(inp["b_mod"].reshape(DEPTH, 48, 128).transpose(0, 2, 1))
    g = np.concatenate([
        inp["norm1"].reshape(DEPTH * 8, 128).T, inp["norm2"].reshape(DEPTH * 8, 128).T,
        inp["final_norm"].reshape(8, 128).T], axis=1)
    again = np.zeros((2, 128, 256), np.float32)
    for j in range(2):
        again[j, :, 128:] = np.broadcast_to(inp["attn_q_gain"][j][:, None], (128, 128))
        again[j, :, :128] = np.broadcast_to(inp["attn_k_gain"][j][None, :], (128, 128))
    dnab = np.ascontiguousarray(
        inp["dn_w_in"][:, :, 4096:4128].reshape(2, 8, 128, 32).transpose(0, 2, 1, 3)).reshape(2, 128, 256)
    convw = np.ascontiguousarray(
        inp["dn_conv"].reshape(2, 5, 24, 128).transpose(0, 3, 2, 1)).reshape(2, 128, 120)
    dngp = np.zeros((2, 128, 33), np.float32)
    for j in range(2):
        dngp[j, :, 0:16] = np.broadcast_to(inp["dn_a_log"][j].reshape(1, 16), (128, 16))
        dngp[j, :, 16:32] = np.broadcast_to(inp["dn_dt_bias"][j].reshape(1, 16), (128, 16))
        dngp[j, :, 32] = inp["dn_norm"][j]
    return {"bmod_in": bmod.astype(np.float32), "gains_in": np.ascontiguousarray(g, dtype=np.float32),
            "consts_in": _consts(), "rott_in": _rott(), "again_in": again,
            "dnab_in": dnab.astype(np.float32), "convw_in": convw.astype(np.float32), "dngp_in": dngp}


def kernel(**inp):
    inp = {k: np.asarray(v) for k, v in inp.items()}
    nc, P, descs = _get_program()
    tiles = make_tiles(descs, inp)
    sh = shared_inputs(inp)
    in_maps = []
    for core in range(8):
        m = core_inputs(inp, core, tiles)
        m.update(sh)
        in_maps.append(m)
    res = run_bass_kernel_spmd(nc, in_maps, core_ids=list(range(8))).results
    B, S = 32, 256
    y_prompt = np.empty((B, S, D), np.float32)
    y_sample = np.empty((2, NG, D), np.float32)
    nk = np.empty((B, 2, S, 2, 128), np.float32)
    nv = np.empty((B, 2, S, 2, 128), np.float32)
    ns = np.empty((B, 2, 2, 8, 128, 128), np.float32)
    for core in range(8):
        r = res[core]
        y, ko, vo, so = r["y_out"], r["k_out"], r["v_out"], r["s_out"]
        seqs = []
        if core < 6:
            for q in range(4):
                seqs.append((4 * core + q, q))
        else:
            y_sample[core - 6] = y[:NG]
        seqs.append((24 + core, 4))
        for (b, seg) in seqs:
            y_prompt[b] = y[seg * 256:(seg + 1) * 256]
            for j in range(2):
                nk[b, j] = ko[j, seg * 256:(seg + 1) * 256].reshape(S, 2, 128)
                nv[b, j] = vo[j, seg * 256:(seg + 1) * 256].reshape(S, 2, 128)
                ns[b, j] = so[j, seg]
    return (y_prompt, y_sample, nk, nv, ns)
```
